# Optimizing a Trainium2 kernel written in Bass

```python
import jax
import jax.numpy as jnp
from jax import lax
import numpy as np

D_MODEL = 1024
BATCH = 32
SEQ = 256
DEPTH = 4
DEC_BATCH = 2
DEC_SEQ = 2048
PAST_LEN = 512

GRID_W = 64
N_MIXERS = 4
Q_BLOCK = 128
N_MOD = 9
MACARON_WEIGHT = 0.5
NORM_EPS = 1e-6
LN_EPS = 1e-5
ROPE_THETA = 10000.0
D_FF = ((8 * D_MODEL // 3 + 255) // 256) * 256
GQ_HEADS = 16
GQ_KV_HEADS = 4
GQ_HEAD_DIM = D_MODEL // GQ_HEADS
GQ_GROUP = GQ_HEADS // GQ_KV_HEADS
GQ_Q_WIDTH = GQ_HEADS * GQ_HEAD_DIM
GQ_KV_WIDTH = GQ_KV_HEADS * GQ_HEAD_DIM
GQ_SCALE = GQ_HEAD_DIM ** -0.5
CV_WIDTH = 31
CV_PAD = CV_WIDTH // 2
DF_HEAD_DIM = 64
DF_HEADS = D_MODEL // (2 * DF_HEAD_DIM)
DF_SCALE = DF_HEAD_DIM ** -0.5
DF_LAMBDA_INIT = 0.470713018
DF_SUBLN_EPS = 1e-5
RW_HEAD_DIM = 64
RW_HEADS = D_MODEL // RW_HEAD_DIM
RW_DECAY_LORA = 64
RW_A_LORA = 64
RW_GATE_LORA = 128
RW_GN_EPS = 64e-5
N_DIRECTIONS = 2

kernel_name = 'hybrid_diffusion_prefix_trunk_step'


def _rmsnorm(x, g, eps=NORM_EPS):
    xf = x.astype(jnp.float32)
    y = xf * lax.rsqrt(jnp.mean(xf * xf, axis=-1, keepdims=True) + eps)
    return (y * g.astype(jnp.float32)).astype(x.dtype)


def _layernorm(x, g, b, eps=LN_EPS):
    xf = x.astype(jnp.float32)
    mu = jnp.mean(xf, axis=-1, keepdims=True)
    var = jnp.mean(jnp.square(xf - mu), axis=-1, keepdims=True)
    return ((xf - mu) * lax.rsqrt(var + eps) * g + b).astype(x.dtype)


def _modulate(h, shift, scale):
    return h * (1.0 + scale) + shift


def _swiglu(h, w_in, w_down):
    gate, up = jnp.split(h @ w_in, 2, axis=-1)
    return (jax.nn.silu(gate) * up) @ w_down


def _axial_rope(n_tokens, head_dim):
    rows = n_tokens // GRID_W
    t = jnp.arange(rows * GRID_W)
    row = (t // GRID_W).astype(jnp.float32)
    col = (t % GRID_W).astype(jnp.float32)
    axis_dim = head_dim // 2
    freqs = ROPE_THETA ** (-jnp.arange(0, axis_dim, 2, dtype=jnp.float32) / axis_dim)
    ang = jnp.concatenate([row[:, None] * freqs, col[:, None] * freqs], axis=-1)
    return jnp.cos(ang), jnp.sin(ang)


def _apply_rope(x, cos, sin):
    shp = x.shape
    xf = x.astype(jnp.float32).reshape(shp[:-1] + (shp[-1] // 2, 2))
    bshape = (1, shp[1]) + (1,) * (x.ndim - 3) + (shp[-1] // 2,)
    c, s = cos.reshape(bshape), sin.reshape(bshape)
    x0, x1 = xf[..., 0], xf[..., 1]
    return jnp.stack([x0 * c - x1 * s, x0 * s + x1 * c], axis=-1).reshape(shp).astype(x.dtype)


def _sweep_query_blocks(block_fn, q):
    b, sq = q.shape[0], q.shape[1]
    nb = sq // Q_BLOCK
    qb = jnp.moveaxis(q.reshape((b, nb, Q_BLOCK) + q.shape[2:]), 1, 0)
    ob = lax.map(block_fn, qb)
    return jnp.moveaxis(ob, 0, 1).reshape((b, sq) + ob.shape[3:])


def _gqa_project(h, gq):
    gq_w_qkv, gq_q_norm, gq_k_norm, _ = gq
    b, s, _ = h.shape
    q, k, v = jnp.split(h @ gq_w_qkv, [GQ_Q_WIDTH, GQ_Q_WIDTH + GQ_KV_WIDTH], axis=-1)
    q = _rmsnorm(q.reshape(b, s, GQ_KV_HEADS, GQ_GROUP, GQ_HEAD_DIM), gq_q_norm)
    k = _rmsnorm(k.reshape(b, s, GQ_KV_HEADS, GQ_HEAD_DIM), gq_k_norm)
    v = v.reshape(b, s, GQ_KV_HEADS, GQ_HEAD_DIM)
    return q, k, v


def _gqa_attend(q, k, v):
    kf, vf = k.astype(jnp.float32), v.astype(jnp.float32)

    def block(qb):
        s = jnp.einsum('bqkgd,bskd->bkgqs', qb.astype(jnp.float32), kf) * GQ_SCALE
        p = jax.nn.softmax(s, axis=-1)
        return jnp.einsum('bkgqs,bskd->bqkgd', p, vf)

    return _sweep_query_blocks(block, q).astype(q.dtype)


def _gqa_context(h, gq):
    q, k, v = _gqa_project(h, gq)
    o = _gqa_attend(q, k, v)
    return o.reshape(h.shape) @ gq[3], (k, v)


def _gqa_latent(h, cache_k, cache_v, gq):
    q, k, v = _gqa_project(h, gq)
    cos, sin = _axial_rope(h.shape[1], GQ_HEAD_DIM)
    q, k = _apply_rope(q, cos, sin), _apply_rope(k, cos, sin)
    k_all = jnp.concatenate([cache_k.astype(k.dtype), k], axis=1)
    v_all = jnp.concatenate([cache_v.astype(v.dtype), v], axis=1)
    o = _gqa_attend(q, k_all, v_all)
    return o.reshape(h.shape) @ gq[3], ()


def _conv_module(h, cv):
    cv_w_in, cv_b_in, cv_w_dw, cv_b_dw, cv_ln_g, cv_ln_b, cv_w_out, cv_b_out = cv
    u_a, u_g = jnp.split(h @ cv_w_in + cv_b_in, 2, axis=-1)
    u = u_a * jax.nn.sigmoid(u_g)
    u = lax.conv_general_dilated(
        u, cv_w_dw[:, None, :].astype(u.dtype), window_strides=(1,),
        padding=[(CV_PAD, CV_PAD)], dimension_numbers=('NWC', 'WIO', 'NWC'),
        feature_group_count=D_MODEL) + cv_b_dw
    u = jax.nn.silu(_layernorm(u, cv_ln_g, cv_ln_b))
    return u @ cv_w_out + cv_b_out, ()


def _diff_lambda(lq1, lk1, lq2, lk2):
    f32 = jnp.float32
    return (jnp.exp(jnp.sum(lq1.astype(f32) * lk1.astype(f32)))
            - jnp.exp(jnp.sum(lq2.astype(f32) * lk2.astype(f32))) + DF_LAMBDA_INIT)


def _diff_project(h, df_w_qkv):
    b, s, _ = h.shape
    q, k, v = jnp.split(h @ df_w_qkv, 3, axis=-1)
    q = q.reshape(b, s, DF_HEADS, 2, DF_HEAD_DIM)
    k = k.reshape(b, s, DF_HEADS, 2, DF_HEAD_DIM)
    v = v.reshape(b, s, DF_HEADS, 2 * DF_HEAD_DIM)
    return q, k, v


def _diff_attend(q, k, v, lam, df_subln_g):
    kf, vf = k.astype(jnp.float32), v.astype(jnp.float32)

    def block(qb):
        s = jnp.einsum('bqhjd,bshjd->bhjqs', qb.astype(jnp.float32), kf) * DF_SCALE
        p = jax.nn.softmax(s, axis=-1)
        return jnp.einsum('bhqs,bshe->bqhe', p[:, :, 0] - lam * p[:, :, 1], vf)

    o = _sweep_query_blocks(block, q)
    o = _rmsnorm(o, df_subln_g, DF_SUBLN_EPS) * (1.0 - DF_LAMBDA_INIT)
    return o.astype(q.dtype)


def _diff_context(h, df_w_qkv, lam, df_subln_g, df_w_o):
    q, k, v = _diff_project(h, df_w_qkv)
    o = _diff_attend(q, k, v, lam, df_subln_g)
    return o.reshape(h.shape) @ df_w_o, (k, v)


def _diff_latent(h, cache_k, cache_v, df_w_qkv, lam, df_subln_g, df_w_o):
    q, k, v = _diff_project(h, df_w_qkv)
    cos, sin = _axial_rope(h.shape[1], DF_HEAD_DIM)
    q, k = _apply_rope(q, cos, sin), _apply_rope(k, cos, sin)
    k_all = jnp.concatenate([cache_k.astype(k.dtype), k], axis=1)
    v_all = jnp.concatenate([cache_v.astype(v.dtype), v], axis=1)
    o = _diff_attend(q, k_all, v_all, lam, df_subln_g)
    return o.reshape(h.shape) @ df_w_o, ()


def _wkv_scan(s0, r, decay, k, v, a, b, reverse):
    def step(s, inp):
        r_t, w_t, k_t, v_t, a_t, b_t = inp
        sa = jnp.einsum('bhvk,bhk->bhv', s, a_t)
        s = s * w_t[:, :, None, :] + sa[..., None] * b_t[:, :, None, :] + v_t[..., None] * k_t[:, :, None, :]
        return s, jnp.einsum('bhvk,bhk->bhv', s, r_t)

    xs = tuple(jnp.swapaxes(t, 0, 1) for t in (r, decay, k, v, a, b))
    s_final, ys = lax.scan(step, s0, xs, reverse=reverse)
    return s_final, jnp.swapaxes(ys, 0, 1)


def _rwkv_mixer(h, s_init, rw):
    (rw_mix, rw_w_r, rw_w_k, rw_w_v, rw_w_o, rw_k_k, rw_k_a, rw_r_k, rw_g1, rw_g2,
     rw_ln_g, rw_ln_b, rw_w0, rw_w1, rw_w2, rw_a0, rw_a1, rw_a2) = rw
    f32 = jnp.float32
    b, t, _ = h.shape

    def heads(z):
        return z.astype(f32).reshape(b, t, RW_HEADS, RW_HEAD_DIM)

    zero = jnp.zeros_like(h[:, :1])
    xx = 0.5 * (jnp.concatenate([zero, h[:, :-1]], axis=1) + jnp.concatenate([h[:, 1:], zero], axis=1)) - h
    xr, xw, xk, xv, xa, xg = (h + xx * rw_mix[i] for i in range(6))
    r = heads(xr @ rw_w_r)
    k = heads(xk @ rw_w_k)
    v = heads(xv @ rw_w_v)
    g = jax.nn.sigmoid(xg @ rw_g1) @ rw_g2
    kk = k * rw_k_k.astype(f32).reshape(RW_HEADS, RW_HEAD_DIM)
    kk = kk * lax.rsqrt(jnp.maximum(jnp.sum(kk * kk, axis=-1, keepdims=True), 1e-24))
    k_a = rw_k_a.astype(f32).reshape(RW_HEADS, RW_HEAD_DIM)
    r_k = rw_r_k.astype(f32)
    ys, bonuses, finals = [], [], []
    for d in range(N_DIRECTIONS):
        w_log = -jax.nn.softplus(-(rw_w0[d] + jnp.tanh(xw @ rw_w1[d]) @ rw_w2[d])) - 0.5
        decay = jnp.exp(-jnp.exp(heads(w_log)))
        a = heads(jax.nn.sigmoid(rw_a0[d] + (xa @ rw_a1[d]) @ rw_a2[d]))
        k_d = k * (1.0 + (a - 1.0) * k_a)
        s_fin, y_d = _wkv_scan(s_init[:, d], r, decay, k_d, v, -kk, kk * a, reverse=(d == 1))
        ys.append(y_d)
        bonuses.append(jnp.sum(r * k_d * r_k, axis=-1, keepdims=True) * v)
        finals.append(s_fin)
    y = ys[0] + ys[1]
    mu = jnp.mean(y, axis=-1, keepdims=True)
    var = jnp.mean(jnp.square(y - mu), axis=-1, keepdims=True)
    yn = ((y - mu) * lax.rsqrt(var + RW_GN_EPS)).reshape(b, t, D_MODEL)
    out = yn * rw_ln_g + rw_ln_b + (bonuses[0] + bonuses[1]).reshape(b, t, D_MODEL)
    return (out.astype(h.dtype) * g) @ rw_w_o, jnp.stack(finals, axis=1)


def _rwkv_context(h, rw):
    s0 = jnp.zeros((h.shape[0], N_DIRECTIONS, RW_HEADS, RW_HEAD_DIM, RW_HEAD_DIM), jnp.float32)
    out, s_final = _rwkv_mixer(h, s0, rw)
    return out, (s_final.astype(h.dtype),)


def _rwkv_latent(h, state, rw):
    out, _ = _rwkv_mixer(h, state.astype(jnp.float32), rw)
    return out, ()


def _trunk_layer(x, cond, mixer, norm_g, mod_w, mod_b, ffn_w_in, ffn_w_down):
    mods = jnp.split((jax.nn.silu(cond) @ mod_w + mod_b)[:, None, :], N_MOD, axis=-1)
    sh1, sc1, gt1, sh2, sc2, gt2, sh3, sc3, gt3 = mods
    x = x + MACARON_WEIGHT * gt1 * _swiglu(_modulate(_rmsnorm(x, norm_g[0]), sh1, sc1), ffn_w_in[0], ffn_w_down[0])
    out, ctx_tensors = mixer(_modulate(_rmsnorm(x, norm_g[1]), sh2, sc2))
    x = x + gt2 * out
    x = x + MACARON_WEIGHT * gt3 * _swiglu(_modulate(_rmsnorm(x, norm_g[2]), sh3, sc3), ffn_w_in[1], ffn_w_down[1])
    return x, ctx_tensors


def setup_inputs(seed: int = 0) -> dict:
    key = jax.random.key(seed)
    keys = iter(jax.random.split(key, 64))
    D = D_MODEL

    def nrm(shape, scale=1.0):
        return scale * jax.random.normal(next(keys), shape, jnp.float32)

    def gain(shape):
        return 1.0 + nrm(shape, 0.05)

    def unif(shape, lo, hi):
        return jax.random.uniform(next(keys), shape, jnp.float32, lo, hi)

    return {
        'x_prompt': nrm((BATCH, SEQ, D)),
        'x_sample': nrm((DEC_BATCH, DEC_SEQ, D)),
        'c': nrm((DEC_BATCH, D)),
        'c_ctx': nrm((D,)),
        'cache_k0': nrm((DEC_BATCH, PAST_LEN, GQ_KV_HEADS, GQ_HEAD_DIM)),
        'cache_v0': nrm((DEC_BATCH, PAST_LEN, GQ_KV_HEADS, GQ_HEAD_DIM)),
        'cache_k2': nrm((DEC_BATCH, PAST_LEN, DF_HEADS, 2, DF_HEAD_DIM)),
        'cache_v2': nrm((DEC_BATCH, PAST_LEN, DF_HEADS, 2 * DF_HEAD_DIM)),
        'state_wkv3': nrm((DEC_BATCH, N_DIRECTIONS, RW_HEADS, RW_HEAD_DIM, RW_HEAD_DIM)),
        'norm_g': gain((DEPTH, 3, D)),
        'mod_w': nrm((DEPTH, D, N_MOD * D), 0.5 * D ** -0.5),
        'mod_b': nrm((DEPTH, N_MOD * D), 0.02),
        'ffn_w_in': nrm((DEPTH, 2, D, 2 * D_FF), D ** -0.5),
        'ffn_w_down': nrm((DEPTH, 2, D_FF, D), D_FF ** -0.5),
        'final_norm_g': gain((D,)),
        'gq_w_qkv': nrm((D, GQ_Q_WIDTH + 2 * GQ_KV_WIDTH), D ** -0.5),
        'gq_q_norm': gain((GQ_HEAD_DIM,)),
        'gq_k_norm': gain((GQ_HEAD_DIM,)),
        'gq_w_o': nrm((D, D), D ** -0.5),
        'cv_w_in': nrm((D, 2 * D), D ** -0.5),
        'cv_b_in': nrm((2 * D,), 0.02),
        'cv_w_dw': nrm((CV_WIDTH, D), CV_WIDTH ** -0.5),
        'cv_b_dw': nrm((D,), 0.02),
        'cv_ln_g': gain((D,)),
        'cv_ln_b': nrm((D,), 0.02),
        'cv_w_out': nrm((D, D), D ** -0.5),
        'cv_b_out': nrm((D,), 0.02),
        'df_w_qkv': nrm((D, 3 * D), D ** -0.5),
        'df_lambda_q1': nrm((DF_HEAD_DIM,), 0.1),
        'df_lambda_k1': nrm((DF_HEAD_DIM,), 0.1),
        'df_lambda_q2': nrm((DF_HEAD_DIM,), 0.1),
        'df_lambda_k2': nrm((DF_HEAD_DIM,), 0.1),
        'df_subln_g': gain((2 * DF_HEAD_DIM,)),
        'df_w_o': nrm((D, D), D ** -0.5),
        'rw_mix': unif((6, D), 0.0, 1.0),
        'rw_w_r': nrm((D, D), D ** -0.5),
        'rw_w_k': nrm((D, D), D ** -0.5),
        'rw_w_v': nrm((D, D), D ** -0.5),
        'rw_w_o': nrm((D, D), D ** -0.5),
        'rw_k_k': 0.85 + nrm((D,), 0.05),
        'rw_k_a': gain((D,)),
        'rw_r_k': nrm((RW_HEADS, RW_HEAD_DIM), 0.1),
        'rw_g1': nrm((D, RW_GATE_LORA), D ** -0.5),
        'rw_g2': nrm((RW_GATE_LORA, D), RW_GATE_LORA ** -0.5),
        'rw_ln_g': gain((D,)),
        'rw_ln_b': nrm((D,), 0.02),
        'rw_w0': unif((N_DIRECTIONS, D), -6.0, 1.0),
        'rw_w1': nrm((N_DIRECTIONS, D, RW_DECAY_LORA), D ** -0.5),
        'rw_w2': nrm((N_DIRECTIONS, RW_DECAY_LORA, D), 0.1 * RW_DECAY_LORA ** -0.5),
        'rw_a0': nrm((N_DIRECTIONS, D), 0.1),
        'rw_a1': nrm((N_DIRECTIONS, D, RW_A_LORA), D ** -0.5),
        'rw_a2': nrm((N_DIRECTIONS, RW_A_LORA, D), 0.1 * RW_A_LORA ** -0.5),
    }


def reference(x_prompt, x_sample, c, c_ctx, cache_k0, cache_v0, cache_k2, cache_v2, state_wkv3,
              norm_g, mod_w, mod_b, ffn_w_in, ffn_w_down, final_norm_g,
              gq_w_qkv, gq_q_norm, gq_k_norm, gq_w_o,
              cv_w_in, cv_b_in, cv_w_dw, cv_b_dw, cv_ln_g, cv_ln_b, cv_w_out, cv_b_out,
              df_w_qkv, df_lambda_q1, df_lambda_k1, df_lambda_q2, df_lambda_k2, df_subln_g, df_w_o,
              rw_mix, rw_w_r, rw_w_k, rw_w_v, rw_w_o, rw_k_k, rw_k_a, rw_r_k, rw_g1, rw_g2,
              rw_ln_g, rw_ln_b, rw_w0, rw_w1, rw_w2, rw_a0, rw_a1, rw_a2):
    gq = (gq_w_qkv, gq_q_norm, gq_k_norm, gq_w_o)
    cv = (cv_w_in, cv_b_in, cv_w_dw, cv_b_dw, cv_ln_g, cv_ln_b, cv_w_out, cv_b_out)
    lam = _diff_lambda(df_lambda_q1, df_lambda_k1, df_lambda_q2, df_lambda_k2)
    rw = (rw_mix, rw_w_r, rw_w_k, rw_w_v, rw_w_o, rw_k_k, rw_k_a, rw_r_k, rw_g1, rw_g2,
          rw_ln_g, rw_ln_b, rw_w0, rw_w1, rw_w2, rw_a0, rw_a1, rw_a2)

    ctx_mixers = (
        lambda h: _gqa_context(h, gq),
        lambda h: _conv_module(h, cv),
        lambda h: _diff_context(h, df_w_qkv, lam, df_subln_g, df_w_o),
        lambda h: _rwkv_context(h, rw),
    )
    lat_mixers = (
        lambda h: _gqa_latent(h, cache_k0, cache_v0, gq),
        lambda h: _conv_module(h, cv),
        lambda h: _diff_latent(h, cache_k2, cache_v2, df_w_qkv, lam, df_subln_g, df_w_o),
        lambda h: _rwkv_latent(h, state_wkv3, rw),
    )

    y = x_prompt
    ctx_tensors = []
    for layer in range(DEPTH):
        y, aux = _trunk_layer(y, c_ctx[None, :], ctx_mixers[layer % N_MIXERS], norm_g[layer],
                              mod_w[layer], mod_b[layer], ffn_w_in[layer], ffn_w_down[layer])
        ctx_tensors.append(aux)
    y_prompt = _rmsnorm(y, final_norm_g)
    (new_k0, new_v0), _, (new_k2, new_v2), (new_wkv3,) = ctx_tensors

    z = x_sample
    for layer in range(DEPTH):
        z, _ = _trunk_layer(z, c, lat_mixers[layer % N_MIXERS], norm_g[layer],
                            mod_w[layer], mod_b[layer], ffn_w_in[layer], ffn_w_down[layer])
    y_sample = _rmsnorm(z, final_norm_g)

    return (y_prompt, y_sample, new_k0, new_v0, new_k2, new_v2, new_wkv3)
```

```python
import contextlib
import math
import os
import numpy as np
import concourse.bass as bass
import concourse.mybir as mybir
from concourse.bass_utils import run_bass_kernel_spmd

F32 = mybir.dt.float32
BF16 = mybir.dt.bfloat16
AF = mybir.ActivationFunctionType
ALU = mybir.AluOpType
AX = mybir.AxisListType

D = 1024
D_FF = 2816
NORM_EPS = 1e-6
LN_EPS = 1e-5
GQ_SCALE = 64 ** -0.5
DF_SCALE = 64 ** -0.5
DF_LAMBDA_INIT = 0.470713018
DF_SUBLN_EPS = 1e-5
RW_GN_EPS = 64e-5
ROPE_THETA = 10000.0
GRID_W = 64

K_LAYERS = int(os.environ.get("K_LAYERS", "4"))
K_PHASES = [int(c) for c in os.environ.get("K_PHASES", "01")]
K_CORES = int(os.environ.get("K_CORES", "8"))
K_FFN16 = os.environ.get("K_FFN16", "1") == "1"
_PROG = {}


class Buf:
    __slots__ = ("w", "r", "w2")

    def __init__(self):
        self.w = None
        self.r = {}
        self.w2 = None


class Eng:
    def __init__(self, name):
        self.name = name
        self.prog = []
        self.count = 0
        self.waited = {}


class KB:
    NDMA = int(os.environ.get("K_NDMA", "8"))

    def __init__(self):
        self.eng = {n: Eng(n) for n in ("pe", "act", "dve", "pool", "sp")}
        self.dma_val = {}
        self.dma_rr = 0

    def _deps(self, e, reads, writes):
        deps = {}
        for b in list(reads) + list(writes):
            if b.w2 is not None and deps.get(b.w2[0], 0) < b.w2[1]:
                deps[b.w2[0]] = b.w2[1]
        for b in reads:
            if b.w is not None and deps.get(b.w[0], 0) < b.w[1]:
                deps[b.w[0]] = b.w[1]
        for b in writes:
            if b.w is not None and deps.get(b.w[0], 0) < b.w[1]:
                deps[b.w[0]] = b.w[1]
            for k, v in b.r.items():
                if deps.get(k, 0) < v:
                    deps[k] = v
        waits = []
        for k, v in deps.items():
            if k == "pe" and e.name == "pe":
                continue
            if e.waited.get(k, 0) < v:
                e.waited[k] = v
                waits.append((k, v))
        return waits

    def op(self, en, fn, reads=(), writes=()):
        e = self.eng[en]
        waits = self._deps(e, reads, writes)
        e.count += 1
        idx = e.count
        e.prog.append((waits, fn, (en, 1)))
        for b in reads:
            if b.r.get(en, 0) < idx:
                b.r[en] = idx
        for b in writes:
            b.w = (en, idx)
            b.w2 = None
            b.r = {}

    def dma(self, qn, out_ap, in_ap, reads=(), writes=()):
        e = self.eng[qn]
        key = "d%d" % (self.dma_rr % self.NDMA)
        self.dma_rr += 1
        prev = self.dma_val.get(key, 0)
        waits = self._deps(e, reads, writes)
        if prev and e.waited.get(key, 0) < prev:
            e.waited[key] = prev
            waits.append((key, prev))
        val = prev + 16
        self.dma_val[key] = val
        e.prog.append((waits, (lambda eng, o=out_ap, i=in_ap: eng.dma_start(out=o, in_=i)), (key, 16)))
        for b in reads:
            if b.r.get(key, 0) < val:
                b.r[key] = val
        for b in writes:
            b.w = (key, val)
            b.w2 = None
            b.r = {}

    def barrier(self):
        cur = {n: e.count for n, e in self.eng.items()}
        for n, e in self.eng.items():
            waits = []
            for k, v in list(cur.items()) + list(self.dma_val.items()):
                if k == n or v == 0:
                    continue
                if e.waited.get(k, 0) < v:
                    e.waited[k] = v
                    waits.append((k, v))
            if waits:
                e.prog.append((waits, None, None))

    def emit(self, nc):
        handles = {"pe": "tensor", "act": "scalar", "dve": "vector", "pool": "gpsimd", "sp": "sync"}
        self.barrier()
        with contextlib.ExitStack() as st:
            sems = {}
            for n in list(self.eng) + ["d%d" % i for i in range(self.NDMA)]:
                sems[n] = st.enter_context(nc.semaphore("s_" + n))
            block = st.enter_context(nc.Block())
            for n, e in self.eng.items():
                def body(engh, e=e):
                    for waits, fn, inc in e.prog:
                        for wk, wv in waits:
                            engh.wait_ge(sems[wk], wv)
                        if fn is not None:
                            fn(engh).then_inc(sems[inc[0]], inc[1])
                getattr(block, handles[n])(body)


def fm(v):
    v = np.asarray(v, np.float32)
    lead = v.shape[:-1]
    C = v.shape[-1] // 128
    return np.ascontiguousarray(np.moveaxis(v.reshape(lead + (C, 128)), -1, 0))


VEC_SPEC = (
    [("ng%d_%d" % (l, i), 8) for l in range(4) for i in range(3)]
    + [("mb%d" % l, 72) for l in range(4)]
    + [("fg", 8), ("gqn", 1), ("gkn", 1), ("cbi", 16), ("cwd", 248), ("cbd", 8), ("clg", 8), ("clb", 8),
       ("cbo", 8), ("dsg", 1), ("lam", 256), ("mix", 48), ("kk", 8), ("ka", 8), ("rk", 8), ("rlg", 8),
       ("rlb", 8), ("w0", 16), ("a0", 16), ("cond", 16)]
)
VEC_OFF = {}
_o = 0
for _n, _w in VEC_SPEC:
    VEC_OFF[_n] = (_o, _w)
    _o += _w
NV = _o


def pack_vecs(inp, c_vec):
    parts = {}
    for l in range(4):
        for i in range(3):
            parts["ng%d_%d" % (l, i)] = fm(inp["norm_g"][l, i])
        parts["mb%d" % l] = fm(inp["mod_b"][l])
    parts["fg"] = fm(inp["final_norm_g"])
    parts["gqn"] = np.tile(np.asarray(inp["gq_q_norm"], np.float32), 2)[:, None]
    parts["gkn"] = np.tile(np.asarray(inp["gq_k_norm"], np.float32), 2)[:, None]
    parts["cbi"] = fm(inp["cv_b_in"])
    parts["cwd"] = np.ascontiguousarray(fm(inp["cv_w_dw"]).transpose(0, 2, 1)).reshape(128, 248)
    parts["cbd"] = fm(inp["cv_b_dw"])
    parts["clg"] = fm(inp["cv_ln_g"])
    parts["clb"] = fm(inp["cv_ln_b"])
    parts["cbo"] = fm(inp["cv_b_out"])
    parts["dsg"] = np.asarray(inp["df_subln_g"], np.float32)[:, None]
    lam = np.concatenate([np.asarray(inp[k], np.float32) for k in
                          ("df_lambda_q1", "df_lambda_k1", "df_lambda_q2", "df_lambda_k2")])
    parts["lam"] = np.broadcast_to(lam[None, :], (128, 256))
    parts["mix"] = fm(inp["rw_mix"]).reshape(128, 48)
    parts["kk"] = fm(inp["rw_k_k"])
    parts["ka"] = fm(inp["rw_k_a"])
    parts["rk"] = fm(np.asarray(inp["rw_r_k"]).reshape(1024))
    parts["rlg"] = fm(inp["rw_ln_g"])
    parts["rlb"] = fm(inp["rw_ln_b"])
    parts["w0"] = fm(inp["rw_w0"]).reshape(128, 16)
    parts["a0"] = fm(inp["rw_a0"]).reshape(128, 16)
    parts["cond"] = np.stack([fm(inp["c_ctx"]), fm(c_vec)], axis=-1).reshape(128, 16)
    out = np.zeros((128, NV), np.float32)
    for n, w in VEC_SPEC:
        o, _ = VEC_OFF[n]
        out[:, o:o + w] = np.asarray(parts[n], np.float32).reshape(128, w)
    return out


def const_mats():
    cm = np.zeros((128, 8, 128), np.float32)
    cm[:, 0, :] = 1.0
    cm[0:64, 1, 0:64] = 1.0
    cm[64:128, 1, 64:128] = 1.0
    cm[0:64, 2, 0:64] = np.eye(64)
    cm[64:128, 2, 0:64] = np.eye(64)
    sw = np.zeros((64, 64), np.float32)
    for i in range(32):
        sw[2 * i, 2 * i + 1] = 1.0
        sw[2 * i + 1, 2 * i] = 1.0
    cm[0:64, 3, 0:64] = sw
    row = np.arange(64)[:, None]
    col = np.arange(64)[None, :]
    lo = (col < row).astype(np.float32)
    up = (col > row).astype(np.float32)
    loi = (col <= row).astype(np.float32)
    upi = (col >= row).astype(np.float32)
    for h in range(2):
        sl = slice(h * 64, (h + 1) * 64)
        cm[sl, 4, 0:64] = lo
        cm[sl, 5, 0:64] = up
        cm[sl, 6, 0:64] = up
        cm[sl, 6, 64:128] = upi
        cm[sl, 7, 0:64] = lo
        cm[sl, 7, 64:128] = loi
    return cm


def rope_tables(n_tokens=2048, head_dim=64):
    t = np.arange(n_tokens)
    row = (t // GRID_W).astype(np.float32)
    col = (t % GRID_W).astype(np.float32)
    axis_dim = head_dim // 2
    freqs = (np.float32(ROPE_THETA) ** (-np.arange(0, axis_dim, 2, dtype=np.float32) / np.float32(axis_dim))).astype(np.float32)
    ang = np.concatenate([row[:, None] * freqs, col[:, None] * freqs], axis=-1).astype(np.float32)
    cos, sin = np.cos(ang).astype(np.float32), np.sin(ang).astype(np.float32)
    tab = np.zeros((64, 2, n_tokens), np.float32)
    for i in range(32):
        tab[2 * i, 0] = cos[:, i]
        tab[2 * i + 1, 0] = cos[:, i]
        tab[2 * i, 1] = -sin[:, i]
        tab[2 * i + 1, 1] = sin[:, i]
    return tab


def build_program():
    nc = bass.Bass("TRN2", target_bir_lowering=False)
    K = KB()
    OP = K.op

    def din(name, shape):
        return nc.dram_tensor(name, list(shape), F32, kind="ExternalInput").ap()

    def dout(name, shape):
        return nc.dram_tensor(name, list(shape), F32, kind="ExternalOutput").ap()

    def dscr(name, shape):
        return nc.dram_tensor(name, list(shape), F32, kind="Internal").ap()

    I = {}
    for name, shape in [
        ("xp", (128, 8, 1024)), ("xs", (128, 8, 2048)), ("vecs", (128, NV)), ("cmat", (128, 8, 128)),
        ("rope", (64, 2, 2048)), ("k0c", (64, 4, 512)), ("v0c", (128, 4, 256)), ("k2c", (64, 8, 2, 512)),
        ("v2c", (128, 4, 8, 128)), ("h0", (128, 2, 8, 64)),
        ("mod_w", (4, 1024, 9216)), ("ffn_w_in", (4, 2, 1024, 2 * D_FF)), ("ffn_w_down", (4, 2, D_FF, 1024)),
        ("gq_w_qkv", (1024, 1536)), ("gq_w_o", (1024, 1024)), ("cv_w_in", (1024, 2048)), ("cv_w_out", (1024, 1024)),
        ("df_w_qkv", (1024, 3072)), ("df_w_o", (1024, 1024)),
        ("rw_w_r", (1024, 1024)), ("rw_w_k", (1024, 1024)), ("rw_w_v", (1024, 1024)), ("rw_w_o", (1024, 1024)),
        ("rw_g1", (1024, 128)), ("rw_g2", (128, 1024)), ("rw_w1", (2, 1024, 64)), ("rw_w2", (2, 64, 1024)),
        ("rw_a1", (2, 1024, 64)), ("rw_a2", (2, 64, 1024)),
    ]:
        I[name] = din(name, shape)
    O = {}
    for name, shape in [("yp", (128, 8, 1024)), ("ys", (128, 8, 2048)), ("nk0", (64, 4, 1024)), ("nv0", (1024, 4, 64)),
                        ("nk2", (64, 8, 2, 1024)), ("nv2", (1024, 8, 128)), ("hst", (128, 4, 2, 8, 64))]:
        O[name] = dout(name, shape)
    SCRN = ["R", "A", "V", "W0", "W1", "KD0", "KD1", "B0", "B1", "Y0", "Y1", "BON", "G"]
    DS = {n: dscr("scr_" + n, (128, 8, 2048)) for n in SCRN}
    DM = dscr("scr_M", (128, 8, 2048))
    DOT = dscr("scr_OT", (64, 16, 2048))

    with contextlib.ExitStack() as st:
        def sb(name, shape):
            return st.enter_context(nc.sbuf_tensor(name, list(shape), F32))

        xT = sb("xT", (128, 8, 2048))
        WS = [sb("ws0", (128, 4096)), sb("ws1", (128, 4096))]
        AR = sb("arena", (128, 24576))
        vecs = sb("vecs_t", (128, NV))
        cm = sb("cmat_t", (128, 8, 128))
        MODS = sb("mods", (128, 2 * 4 * 72))
        ABT = sb("abt", (128, 3 * 3 * 8))
        RS = [sb("rs0", (128, 512)), sb("rs1", (128, 512))]
        SM = sb("small", (128, 64))
        ps = [st.enter_context(nc.psum_tensor("ps%d" % i, [128, 512], F32)) for i in range(8)]
        PB = [Buf() for _ in range(8)]
        WB = [Buf(), Buf()]
        VB, CMB, MB, ABB, RPB, SMB = Buf(), Buf(), Buf(), Buf(), Buf(), Buf()
        RSB = [Buf(), Buf()]
        PTB = [Buf(), Buf()]
        XB = [Buf() for _ in range(8)]
        cnt = {"w": 0, "lb": 0, "h": 0, "rs": 0, "s": 0, "ch": 0}
        rope_loaded = {"t0": None}

        ones = cm[:, 0, :]
        BO = cm[:, 1, :]
        maskI = cm[:, 2, 0:64]
        pswap = cm[0:64, 3, 0:64]
        MODSv = MODS[:, :].rearrange("p (w l j) -> p w l j", w=2, l=4)
        ABv = ABT[:, :].rearrange("p (s k c) -> p s k c", s=3, k=3)

        def V(name):
            o, w = VEC_OFF[name]
            return vecs[:, o:o + w]

        def view(off, shape, p0=0):
            n = 1
            for s_ in shape[1:]:
                n *= s_
            a = AR[p0:p0 + shape[0], off:off + n]
            if len(shape) == 3:
                a = a.rearrange("p (a b) -> p a b", b=shape[2])
            elif len(shape) == 4:
                a = a.rearrange("p (a b c) -> p a b c", b=shape[2], c=shape[3])
            elif len(shape) == 5:
                a = a.rearrange("p (a b c d) -> p a b c d", b=shape[2], c=shape[3], d=shape[4])
            return a

        HV = [view(0, (128, 8, 512)), view(4096, (128, 8, 512))]
        PT = [view(22528, (128, 512)), view(23040, (128, 512))]
        RP = view(23552, (64, 2, 512))
        HB = [Buf(), Buf()]

        def xbufs(t0, n):
            return [XB[u] for u in range(t0 // 256, (t0 + n + 255) // 256)]

        K.dma("sp", vecs[:, :], I["vecs"], writes=[VB])
        K.dma("sp", cm[:, :, :], I["cmat"], writes=[CMB])

        def linear(W, KC, KP, c0, ncols, M, rhs_fn, N, epi):
            CT = max(M, min(ncols, (4096 // KC) // M * M))
            Wv = W.rearrange("(kc p) n -> p kc n", p=KP)
            off = 0
            j = 0
            while off < ncols:
                ct = min(CT, ncols - off)
                s = cnt["w"] % 2
                cnt["w"] += 1
                slot = WS[s][0:KP, 0:KC * ct].rearrange("p (kc n) -> p kc n", n=ct)
                K.dma("sp", slot, Wv[:, :, c0 + off:c0 + off + ct], writes=[WB[s]])
                for mc in range(ct // M):
                    b = cnt["lb"] % 2
                    cnt["lb"] += 1
                    pso = ps[b][0:M, 0:N]
                    rh = [rhs_fn(kc) for kc in range(KC)]
                    rbufs = []
                    for x in rh:
                        rbufs.extend(x[1])

                    def mm(e, slot=slot, mc=mc, pso=pso, rh=rh):
                        ins = None
                        for kc in range(KC):
                            ins = e.matmul(pso, slot[:, kc, mc * M:(mc + 1) * M], rh[kc][0],
                                           start=(kc == 0), stop=(kc == KC - 1))
                        return ins
                    OP("pe", mm, reads=[WB[s], CMB] + rbufs, writes=[PB[b]])
                    epi(j, pso, PB[b])
                    j += 1
                off += ct

        def linear_tok(W, c0, ncols, h_ap, hbufs, nsub, epi):
            Wv = W.rearrange("(kc p) n -> p kc n", p=128)
            s = cnt["w"] % 2
            cnt["w"] += 1
            slot = WS[s][:, 0:8 * ncols].rearrange("p (kc n) -> p kc n", n=ncols)
            K.dma("sp", slot, Wv[:, :, c0:c0 + ncols], writes=[WB[s]])
            for sub in range(nsub):
                b = cnt["lb"] % 2
                cnt["lb"] += 1
                pso = ps[b][:, 0:ncols]

                def mm(e, sub=sub, pso=pso):
                    ins = None
                    for kc in range(8):
                        ins = e.matmul(pso, h_ap[:, kc, sub * 128:(sub + 1) * 128], slot[:, kc, :],
                                       start=(kc == 0), stop=(kc == 7))
                    return ins
                OP("pe", mm, reads=[WB[s]] + hbufs, writes=[PB[b]])
                epi(sub, pso, PB[b])

        def rstd_from(pin, pb, nparts, n, scale, eps):
            r = cnt["rs"] % 2
            cnt["rs"] += 1
            t = RS[r][0:nparts, 0:n]
            OP("act", lambda e: e.activation(out=t, in_=pin, func=AF.Sqrt, bias=float(eps), scale=float(scale)),
               reads=[pb], writes=[RSB[r]])
            OP("dve", lambda e: e.reciprocal(out=t, in_=t), reads=[RSB[r]], writes=[RSB[r]])
            return t, RSB[r]

        def make_h(t0, n, A, Bv, off=0, slot=None, out16=None, out16b=None):
            if slot is None:
                s = cnt["h"] % 2
                cnt["h"] += 1
            else:
                s = slot
            hv, hb = HV[s], HB[s]
            xb = xbufs(t0, n)
            xin = xT[:, :, t0:t0 + n]
            hsl = hv[:, :, off:off + n]
            OP("act", lambda e: e.activation(out=hsl, in_=xin, func=AF.Square), reads=xb, writes=[hb])

            def mm(e):
                ins = None
                for c in range(8):
                    ins = e.matmul(ps[2][:, 0:n], ones, hv[:, c, off:off + n], start=(c == 0), stop=(c == 7))
                return ins
            OP("pe", mm, reads=[hb, CMB], writes=[PB[2]])
            rs, rsb = rstd_from(ps[2][:, 0:n], PB[2], 128, n, 1.0 / 1024, NORM_EPS)
            OP("dve", lambda e: e.tensor_tensor(out=hsl, in0=xin, in1=rs.unsqueeze(1).to_broadcast([128, 8, n]),
                                                op=ALU.mult), reads=xb + [rsb], writes=[hb])
            tgt = hv if out16 is None else out16
            tgtb = hb if out16 is None else out16b

            def mod_act(e):
                ins = None
                for c in range(0, 4):
                    ins = e.activation(out=tgt[:, c, off:off + n], in_=hv[:, c, off:off + n], func=AF.Identity,
                                       scale=A[:, c:c + 1], bias=Bv[:, c:c + 1])
                return ins

            def mod_dve(e):
                ins = None
                for c in range(4, 8):
                    ins = e.tensor_scalar(out=tgt[:, c, off:off + n], in0=hv[:, c, off:off + n], scalar1=A[:, c:c + 1],
                                          scalar2=Bv[:, c:c + 1], op0=ALU.mult, op1=ALU.add)
                return ins
            side = Buf()
            side.w, side.r, side.w2 = tgtb.w, dict(tgtb.r), tgtb.w2
            OP("dve", mod_dve, reads=[side if out16 is None else hb, ABB, VB, SMB], writes=[side])
            OP("act", mod_act, reads=[hb, ABB, VB, SMB], writes=[tgtb])
            tgtb.w2 = side.w
            return hv, hb

        def prep_mods(ph, l):
            for sub in range(3):
                sh = MODSv[:, ph, l, (3 * sub) * 8:(3 * sub) * 8 + 8]
                sc = MODSv[:, ph, l, (3 * sub + 1) * 8:(3 * sub + 1) * 8 + 8]
                gt = MODSv[:, ph, l, (3 * sub + 2) * 8:(3 * sub + 2) * 8 + 8]
                ng = V("ng%d_%d" % (l, sub))
                OP("dve", lambda e, sub=sub, sc=sc, ng=ng: e.scalar_tensor_tensor(
                    out=ABv[:, sub, 0, :], in0=sc, scalar=1.0, in1=ng, op0=ALU.add, op1=ALU.mult),
                   reads=[MB, VB], writes=[ABB])
                OP("dve", lambda e, sub=sub, sh=sh: e.tensor_copy(out=ABv[:, sub, 1, :], in_=sh), reads=[MB], writes=[ABB])
                OP("dve", lambda e, sub=sub, gt=gt: e.tensor_scalar(
                    out=ABv[:, sub, 2, :], in0=gt, scalar1=(1.0 if sub == 1 else 0.5), scalar2=None, op0=ALU.mult),
                   reads=[MB], writes=[ABB])

        def ab(sub):
            return ABv[:, sub, 0, :], ABv[:, sub, 1, :], ABv[:, sub, 2, :]

        def resid_epi(G, t0, n, extra=None):
            def epi(m, pso, pb):
                xs = xT[:, m, t0:t0 + n]
                xb = xbufs(t0, n)
                OP("dve", lambda e: e.scalar_tensor_tensor(out=xs, in0=pso, scalar=G[:, m:m + 1], in1=xs,
                                                           op0=ALU.mult, op1=ALU.add),
                   reads=[pb, ABB] + xb, writes=xb)
                if extra is not None:
                    OP("dve", lambda e: e.tensor_scalar(out=xs, in0=xs, scalar1=extra[:, m:m + 1], scalar2=None,
                                                        op0=ALU.add), reads=[SMB] + xb, writes=xb)
            return epi

        def compute_mods():
            SC = SM[:, 0:16].rearrange("p (c w) -> p c w", w=2)
            cond = V("cond").rearrange("p (c w) -> p c w", w=2)
            OP("act", lambda e: e.activation(out=SC, in_=cond, func=AF.Silu), reads=[VB], writes=[SMB])
            for l in range(4):
                mb = V("mb%d" % l)

                def epi(j, pso, pb, l=l, mb=mb):
                    OP("dve", lambda e: e.tensor_scalar(out=MODSv[:, :, l, j], in0=pso, scalar1=mb[:, j:j + 1],
                                                        scalar2=None, op0=ALU.add), reads=[pb, VB], writes=[MB])
                linear(I["mod_w"][l], 8, 128, 0, 9216, 128, lambda kc: (SC[:, kc, :], [SMB]), 2, epi)

        def ffn(ph, l, i, T):
            A, Bv, G = ab(0 if i == 0 else 2)
            act = view(8192, (128, 22, 512))
            ACTB = [Buf() for _ in range(22)]
            for blk in range(T // 512):
                t0 = blk * 512
                hv, hb = make_h(t0, 512, A, Bv)

                def epi1(j, pso, pb):
                    if j < 22:
                        OP("act", lambda e: e.activation(out=act[:, j, :], in_=pso, func=AF.Silu),
                           reads=[pb], writes=[ACTB[j]])
                    else:
                        jj = j - 22
                        OP("dve", lambda e: e.tensor_tensor(out=act[:, jj, :], in0=act[:, jj, :], in1=pso, op=ALU.mult),
                           reads=[pb, ACTB[jj]], writes=[ACTB[jj]])
                linear(I["ffn_w_in"][l, i], 8, 128, 0, 2 * D_FF, 128, lambda kc: (hv[:, kc, :], [hb]), 512, epi1)
                linear(I["ffn_w_down"][l, i], 22, 128, 0, 1024, 128, lambda kc: (act[:, kc, :], [ACTB[kc]]), 512,
                       resid_epi(G, t0, 512))

        def view16(off, shape):
            n = 1
            for s_ in shape[1:]:
                n *= s_
            a = AR[0:shape[0], off:off + n // 2].bitcast(BF16)
            if len(shape) == 3:
                a = a.rearrange("p (a b) -> p a b", b=shape[2])
            return a

        def ffn16(ph, l, i, T):
            A, Bv, G = ab(0 if i == 0 else 2)
            W16 = [view16(4096, (128, 8192)), view16(8192, (128, 8192))]
            W16B = [Buf(), Buf()]
            H16 = [view16(12288, (128, 8, 512)), view16(14336, (128, 8, 512))]
            H16B = [Buf(), Buf()]
            act16 = view16(16384, (128, 22, 512))
            ACTB = [Buf() for _ in range(22)]
            SGt = [view(22016, (128, 512)), view(22528, (128, 512))]
            SGB = [Buf(), Buf()]
            W32B = [[Buf(), Buf()], [Buf(), Buf()]]
            Win = I["ffn_w_in"][l, i].rearrange("(kc p) n -> p kc n", p=128)
            Wdn = I["ffn_w_down"][l, i].rearrange("(kc p) n -> p kc n", p=128)
            banks = [(0, 1), (4, 5), (6, 7)]
            st_ = {"t": 0, "b": 0, "g": 0}
            for blk in range(T // 512):
                t0 = blk * 512
                h16v, h16b = H16[blk % 2], H16B[blk % 2]
                make_h(t0, 512, A, Bv, slot=0, out16=h16v, out16b=h16b)
                for j0 in range(0, 22, 2):
                    s = st_["t"] % 2
                    st_["t"] += 1
                    s32 = WS[s][:, 0:4096].rearrange("p (kc n) -> p kc n", n=512)
                    if os.environ.get("K_NODMA") != "1":
                        K.dma("sp", s32[:, :, 0:256], Win[:, :, j0 * 128:j0 * 128 + 256], writes=[W32B[s][0]])
                        K.dma("sp", s32[:, :, 256:512], Win[:, :, D_FF + j0 * 128:D_FF + j0 * 128 + 256], writes=[W32B[s][1]])
                    w16 = W16[s][:, 0:4096].rearrange("p (kc n) -> p kc n", n=512)
                    if os.environ.get("K_NOCAST") != "1":
                        OP("pool", lambda e, w16=w16, s32=s32: e.tensor_copy(out=w16, in_=s32), reads=W32B[s], writes=[W16B[s]])
                    for jj in range(2):
                        j = j0 + jj
                        ba, bb = banks[st_["b"] % 3]
                        st_["b"] += 1

                        def mm(e, w16=w16, jj=jj, ba=ba, bb=bb, h16v=h16v):
                            for kc in range(8):
                                e.matmul(ps[ba][:, :], w16[:, kc, jj * 128:(jj + 1) * 128], h16v[:, kc, :], start=(kc == 0), stop=(kc == 7))
                            ins = None
                            for kc in range(8):
                                ins = e.matmul(ps[bb][:, :], w16[:, kc, 256 + jj * 128:256 + (jj + 1) * 128], h16v[:, kc, :],
                                               start=(kc == 0), stop=(kc == 7))
                            return ins
                        OP("pe", mm, reads=[W16B[s], h16b], writes=[PB[ba], PB[bb]])
                        g = st_["g"] % 2
                        st_["g"] += 1
                        OP("act", lambda e, g=g, ba=ba: e.activation(out=SGt[g], in_=ps[ba][:, :], func=AF.Silu),
                           reads=[PB[ba]], writes=[SGB[g]])
                        OP("dve", lambda e, g=g, bb=bb, j=j: e.tensor_tensor(out=act16[:, j, :], in0=SGt[g], in1=ps[bb][:, :], op=ALU.mult),
                           reads=[SGB[g], PB[bb]], writes=[ACTB[j]])
                for m in range(8):
                    s = st_["t"] % 2
                    st_["t"] += 1
                    s32 = WS[s][:, 0:2816].rearrange("p (kc n) -> p kc n", n=128)
                    if os.environ.get("K_NODMA") != "1":
                        K.dma("sp", s32, Wdn[:, :, m * 128:(m + 1) * 128], writes=[W32B[s][0]])
                    w16 = W16[s][:, 0:2816].rearrange("p (kc n) -> p kc n", n=128)
                    OP("pool", lambda e, w16=w16, s32=s32: e.tensor_copy(out=w16, in_=s32), reads=W32B[s], writes=[W16B[s]])
                    ba = banks[st_["b"] % 3][0]
                    st_["b"] += 1

                    def mm2(e, w16=w16, ba=ba):
                        ins = None
                        for kc in range(22):
                            ins = e.matmul(ps[ba][:, :], w16[:, kc, :], act16[:, kc, :], start=(kc == 0), stop=(kc == 21))
                        return ins
                    OP("pe", mm2, reads=[W16B[s]] + ACTB, writes=[PB[ba]])
                    resid_epi(G, t0, 512)(m, ps[ba][:, :], PB[ba])

        def proj_pass(W, KC, KP, src, T, G, extra=None):
            K.barrier()
            tl = view(8192, (KP, KC, 512))
            tb = Buf()
            for blk in range(T // 512):
                t0 = blk * 512
                K.dma("sp", tl, src[:, :, t0:t0 + 512], writes=[tb])
                linear(W, KC, KP, 0, 1024, 128, lambda kc: (tl[:, kc, :], [tb]), 512, resid_epi(G, t0, 512, extra))

        def load_rope(t0, n):
            if rope_loaded["t0"] != (t0, n):
                K.dma("sp", RP[:, :, 0:n], I["rope"][:, :, t0:t0 + n], writes=[RPB])
                rope_loaded["t0"] = (t0, n)

        def rope(src, srcb, dest, destb, t0, n, T2, T2B, T3, T3B):
            load_rope(t0, n)
            OP("pe", lambda e: e.matmul(ps[3][0:64, 0:n], pswap, src, start=True, stop=True),
               reads=[srcb, CMB], writes=[PB[3]])
            OP("dve", lambda e: e.tensor_tensor(out=T2[:, 0:n], in0=ps[3][0:64, 0:n], in1=RP[:, 1, 0:n], op=ALU.mult),
               reads=[PB[3], RPB], writes=[T2B])
            OP("pool", lambda e: e.tensor_tensor(out=T3[:, 0:n], in0=src, in1=RP[:, 0, 0:n], op=ALU.mult),
               reads=[srcb, RPB], writes=[T3B])
            OP("dve", lambda e: e.tensor_tensor(out=dest, in0=T2[:, 0:n], in1=T3[:, 0:n], op=ALU.add),
               reads=[T2B, T3B], writes=destb)

        PT16 = [view16(22528, (128, 512)), view16(22784, (128, 512))]
        ones16 = view16(23040, (128, 128))
        O16B = Buf()

        def init_ones16():
            OP("dve", lambda e: e.tensor_copy(out=ones16, in_=ones), reads=[CMB], writes=[O16B])

        def attn_core(q_ap, qbufs, chunks, dv, nq, scale):
            Oa = ps[6][0:dv, 0:nq]
            Da = ps[7][0:dv, 0:nq]
            n = len(chunks)

            def issue_s(ci):
                k_ap, v_ap, cb = chunks[ci]
                r = cnt["s"] % 2
                cnt["s"] += 1
                sbk = 4 + r
                OP("pe", lambda e, sbk=sbk, k_ap=k_ap: e.matmul(ps[sbk][:, 0:nq], k_ap, q_ap, start=True, stop=True),
                   reads=cb + qbufs, writes=[PB[sbk]])
                return sbk, PT16[r], PTB[r]
            cur = issue_s(0)
            for ci in range(n):
                k_ap, v_ap, cb = chunks[ci]
                nxt = issue_s(ci + 1) if ci + 1 < n else None
                sbk, pt, ptb = cur
                OP("act", lambda e, sbk=sbk, pt=pt: e.activation(out=pt[:, 0:nq], in_=ps[sbk][:, 0:nq], func=AF.Exp,
                                                                 scale=float(scale)), reads=[PB[sbk]], writes=[ptb])

                def mm2(e, ci=ci, v_ap=v_ap, pt=pt):
                    e.matmul(Oa, v_ap, pt[:, 0:nq], start=(ci == 0), stop=(ci == n - 1))
                    return e.matmul(Da, ones16[:, 0:dv], pt[:, 0:nq], start=(ci == 0), stop=(ci == n - 1))
                OP("pe", mm2, reads=cb + [ptb, O16B], writes=[PB[6], PB[7]])
                cur = nxt
            return Oa, Da

        def softmax_out(Oa, Da, dv, nq, dest, destb):
            r = cnt["rs"] % 2
            cnt["rs"] += 1
            rd = RS[r][0:dv, 0:nq]
            OP("dve", lambda e: e.reciprocal(out=rd, in_=Da), reads=[PB[7]], writes=[RSB[r]])
            OP("dve", lambda e: e.tensor_tensor(out=dest, in0=Oa, in1=rd, op=ALU.mult),
               reads=[PB[6], RSB[r]], writes=destb)

        def gqa(ph, l, NS, L):
            T = NS * L
            rope_loaded["t0"] = None
            A, Bv, G = ab(1)
            S = L + (512 if ph else 0)
            nkc = S // 128
            o = 8192
            KT = view(o, (64, NS * S)); o += NS * S
            Vt = view(o, (128, NS * nkc, 64)); o += NS * nkc * 64
            Q16 = view16(o, (64, 4, 512)); o += 1024
            K16 = view16(o, (64, NS * S)); o += NS * S // 2
            V16 = view16(o, (128, NS * nkc, 64)); o += NS * nkc * 32
            K16B, V16B = Buf(), Buf()
            init_ones16()
            KR = view(o, (64, 512)); o += 512
            SQ = view(o, (64, 512)); o += 512
            T1 = view(o, (64, 512)); o += 512
            T2 = view(o, (64, 512)); o += 512
            T3 = view(o, (64, 512)); o += 512
            OTg = view(o, (64, 4, 512)); o += 2048
            assert o <= 22528
            QTB4 = [Buf() for _ in range(4)]
            KTB, VtB, QTB, KRB, SQB, T1B, T2B, T3B, OTB = [Buf() for _ in range(9)]
            coff = 512 if ph else 0

            def qknorm(pso, pb, gname, dest, destb, t0, n):
                OP("act", lambda e: e.activation(out=KR[:, 0:n], in_=pso, func=AF.Copy), reads=[pb], writes=[KRB])
                OP("act", lambda e: e.activation(out=SQ[:, 0:n], in_=pso, func=AF.Square), reads=[pb], writes=[SQB])
                OP("pe", lambda e: e.matmul(ps[3][0:64, 0:n], ones[0:64, 0:64], SQ[:, 0:n], start=True, stop=True),
                   reads=[SQB, CMB], writes=[PB[3]])
                rs, rsb = rstd_from(ps[3][0:64, 0:n], PB[3], 64, n, 1.0 / 64, NORM_EPS)
                tgt, tgtb = (T1[:, 0:n], [T1B]) if ph else (dest, destb)
                OP("dve", lambda e: e.scalar_tensor_tensor(out=tgt, in0=KR[:, 0:n], scalar=V(gname)[0:64, 0:1], in1=rs,
                                                           op0=ALU.mult, op1=ALU.mult), reads=[KRB, rsb, VB], writes=tgtb)
                if ph:
                    rope(T1[:, 0:n], T1B, dest, destb, t0, n, T2, T2B, T3, T3B)

            for g in range(4):
                if ph:
                    K.dma("sp", KT[:, 0:512], I["k0c"][:, g, :], writes=[KTB])
                    K.dma("sp", Vt[:, 0:4, :], I["v0c"][:, :, g * 64:(g + 1) * 64], writes=[VtB])
                for blk in range(T // 512):
                    t0 = blk * 512
                    hv, hb = make_h(t0, 512, A, Bv)
                    linear(I["gq_w_qkv"], 8, 128, 1024 + g * 64, 64, 64, lambda kc: (hv[:, kc, :], [hb]), 512,
                           lambda j, pso, pb: qknorm(pso, pb, "gkn", KT[:, coff + t0:coff + t0 + 512], [KTB], t0, 512))

                    def epi_v(sub, pso, pb):
                        ch = (coff + t0) // 128 + sub
                        OP("act", lambda e: e.activation(out=Vt[:, ch, :], in_=pso, func=AF.Copy), reads=[pb], writes=[VtB])
                    linear_tok(I["gq_w_qkv"], 1280 + g * 64, 64, hv, [hb], 4, epi_v)
                OP("pool", lambda e: e.tensor_copy(out=K16, in_=KT), reads=[KTB], writes=[K16B])
                OP("pool", lambda e: e.tensor_copy(out=V16, in_=Vt), reads=[VtB], writes=[V16B])
                if ph == 0:
                    K.dma("sp", O["nk0"][:, g, :], KT[:, 0:1024], reads=[KTB])
                    K.dma("sp", O["nv0"].rearrange("(c p) g d -> p c g d", p=128)[:, :, g, :], Vt[:, 0:8, :], reads=[VtB])
                for blk in range(T // 512):
                    t0 = blk * 512
                    hv, hb = make_h(t0, 512, A, Bv)

                    def epi_q(j, pso, pb):
                        qknorm(pso, pb, "gqn", Q16[:, j, :], [QTB4[j]], t0, 512)
                        if j < 3:
                            return
                        for jq in range(4):
                            if ph == 0:
                                for u in range(2):
                                    s = blk * 2 + u
                                    chunks = [(K16[:, s * 256 + c * 128:s * 256 + (c + 1) * 128], V16[:, s * 2 + c, :], [K16B, V16B])
                                              for c in range(2)]
                                    Oa, Da = attn_core(Q16[:, jq, u * 256:(u + 1) * 256], [QTB4[jq]], chunks, 64, 256, GQ_SCALE)
                                    softmax_out(Oa, Da, 64, 256, OTg[:, jq, u * 256:(u + 1) * 256], [OTB])
                            else:
                                chunks = [(K16[:, c * 128:(c + 1) * 128], V16[:, c, :], [K16B, V16B]) for c in range(nkc)]
                                Oa, Da = attn_core(Q16[:, jq, :], [QTB4[jq]], chunks, 64, 512, GQ_SCALE)
                                softmax_out(Oa, Da, 64, 512, OTg[:, jq, :], [OTB])
                    linear(I["gq_w_qkv"], 8, 128, g * 256, 256, 64, lambda kc: (hv[:, kc, :], [hb]), 512, epi_q)
                    K.dma("sp", DOT[:, 4 * g:4 * g + 4, t0:t0 + 512], OTg[:, :, :], reads=[OTB])
            proj_pass(I["gq_w_o"], 16, 64, DOT, T, G)

        def diffattn(ph, l, NS, L):
            T = NS * L
            rope_loaded["t0"] = None
            A, Bv, G = ab(1)
            S = L + (512 if ph else 0)
            nkc = S // 128
            o = 4096
            KT = view(o, (64, 2, NS * S)); o += 2 * NS * S
            Vt = view(o, (128, NS * nkc, 128)); o += NS * nkc * 128
            QT = view16(o, (64, 2, 512)); o += 512
            K16 = view16(o, (64, 2, NS * S)); o += NS * S
            V16 = view16(o, (128, NS * nkc, 128)); o += NS * nkc * 64
            K16B, V16B = Buf(), Buf()
            init_ones16()
            T1 = view(o, (64, 512)); o += 512
            T2 = view(o, (64, 512)); o += 512
            T3 = view(o, (64, 512)); o += 512
            O1 = view(o, (128, 512)); o += 512
            O2 = view(o, (128, 512)); o += 512
            OTh = view(o, (128, 512)); o += 512
            assert o <= 22528
            KTB, VtB, QTB, T1B, T2B, T3B, O1B, O2B, OTB = [Buf() for _ in range(9)]
            coff = 512 if ph else 0
            lamv = V("lam").rearrange("p (a b) -> p a b", b=64)
            neglam = None

            def lam_full():
                for a in range(2):
                    OP("dve", lambda e, a=a: e.tensor_tensor(out=O1[:, 0:64], in0=lamv[:, 2 * a, :], in1=lamv[:, 2 * a + 1, :],
                                                             op=ALU.mult), reads=[VB], writes=[O1B])
                    OP("dve", lambda e, a=a: e.tensor_reduce(out=SM[:, 44 + a:45 + a], in_=O1[:, 0:64], axis=AX.X, op=ALU.add),
                       reads=[O1B], writes=[SMB])
                OP("act", lambda e: e.activation(out=SM[:, 44:46], in_=SM[:, 44:46], func=AF.Exp), reads=[SMB], writes=[SMB])
                OP("dve", lambda e: e.tensor_tensor(out=SM[:, 46:47], in0=SM[:, 44:45], in1=SM[:, 45:46], op=ALU.subtract),
                   reads=[SMB], writes=[SMB])
                OP("dve", lambda e: e.tensor_scalar(out=SM[:, 47:48], in0=SM[:, 46:47], scalar1=DF_LAMBDA_INIT, scalar2=-1.0,
                                                    op0=ALU.add, op1=ALU.mult), reads=[SMB], writes=[SMB])
                return SM[:, 47:48]
            neglam = lam_full()

            def proj_rope(pso, pb, dest, destb, t0, n):
                if ph:
                    OP("act", lambda e: e.activation(out=T1[:, 0:n], in_=pso, func=AF.Copy), reads=[pb], writes=[T1B])
                    rope(T1[:, 0:n], T1B, dest, destb, t0, n, T2, T2B, T3, T3B)
                else:
                    OP("act", lambda e: e.activation(out=dest, in_=pso, func=AF.Copy), reads=[pb], writes=destb)

            for h in range(8):
                if ph:
                    K.dma("sp", KT[:, :, 0:512], I["k2c"][:, h, :, :], writes=[KTB])
                    K.dma("sp", Vt[:, 0:4, :], I["v2c"][:, :, h, :], writes=[VtB])
                for blk in range(T // 512):
                    t0 = blk * 512
                    hv, hb = make_h(t0, 512, A, Bv, slot=0)
                    linear(I["df_w_qkv"], 8, 128, 1024 + h * 128, 128, 64, lambda kc: (hv[:, kc, :], [hb]), 512,
                           lambda j, pso, pb: proj_rope(pso, pb, KT[:, j, coff + t0:coff + t0 + 512], [KTB], t0, 512))

                    def epi_v(sub, pso, pb):
                        ch = (coff + t0) // 128 + sub
                        OP("act", lambda e: e.activation(out=Vt[:, ch, :], in_=pso, func=AF.Copy), reads=[pb], writes=[VtB])
                    linear_tok(I["df_w_qkv"], 2048 + h * 128, 128, hv, [hb], 4, epi_v)
                OP("pool", lambda e: e.tensor_copy(out=K16, in_=KT), reads=[KTB], writes=[K16B])
                OP("pool", lambda e: e.tensor_copy(out=V16, in_=Vt), reads=[VtB], writes=[V16B])
                if ph == 0:
                    K.dma("sp", O["nk2"][:, h, :, :], KT[:, :, 0:1024], reads=[KTB])
                    K.dma("sp", O["nv2"].rearrange("(c p) h e -> p c h e", p=128)[:, :, h, :], Vt[:, 0:8, :], reads=[VtB])
                for blk in range(T // 512):
                    t0 = blk * 512
                    hv, hb = make_h(t0, 512, A, Bv, slot=0)

                    def finish(cols, nq, chunk_fn):
                        for j, Ox, OxB in ((0, O1, O1B), (1, O2, O2B)):
                            Oa, Da = attn_core(QT[:, j, cols], [QTB], chunk_fn(j), 128, nq, DF_SCALE)
                            softmax_out(Oa, Da, 128, nq, Ox[:, cols], [OxB])
                        OP("dve", lambda e: e.scalar_tensor_tensor(out=O1[:, cols], in0=O2[:, cols], scalar=neglam, in1=O1[:, cols],
                                                                   op0=ALU.mult, op1=ALU.add), reads=[O1B, O2B, SMB], writes=[O1B])
                        OP("act", lambda e: e.activation(out=O2[:, cols], in_=O1[:, cols], func=AF.Square), reads=[O1B], writes=[O2B])
                        OP("pe", lambda e: e.matmul(ps[3][:, 0:nq], ones, O2[:, cols], start=True, stop=True),
                           reads=[O2B, CMB], writes=[PB[3]])
                        rs, rsb = rstd_from(ps[3][:, 0:nq], PB[3], 128, nq, 1.0 / 128, DF_SUBLN_EPS)
                        OP("dve", lambda e: e.scalar_tensor_tensor(out=OTh[:, cols], in0=O1[:, cols], scalar=V("dsg")[:, 0:1], in1=rs,
                                                                   op0=ALU.mult, op1=ALU.mult), reads=[O1B, rsb, VB], writes=[OTB])
                        OP("dve", lambda e: e.tensor_scalar(out=OTh[:, cols], in0=OTh[:, cols], scalar1=1.0 - DF_LAMBDA_INIT,
                                                            scalar2=None, op0=ALU.mult), reads=[OTB], writes=[OTB])

                    def epi_q(j, pso, pb):
                        proj_rope(pso, pb, QT[:, j, :], [QTB], t0, 512)
                        if j == 1:
                            if ph == 0:
                                for u in range(2):
                                    s = blk * 2 + u
                                    finish(slice(u * 256, (u + 1) * 256), 256,
                                           lambda jj, s=s: [(K16[:, jj, s * 256 + c * 128:s * 256 + (c + 1) * 128],
                                                             V16[:, s * 2 + c, :], [K16B, V16B]) for c in range(2)])
                            else:
                                finish(slice(0, 512), 512,
                                       lambda jj: [(K16[:, jj, c * 128:(c + 1) * 128], V16[:, c, :], [K16B, V16B]) for c in range(nkc)])
                    linear(I["df_w_qkv"], 8, 128, h * 128, 128, 64, lambda kc: (hv[:, kc, :], [hb]), 512, epi_q)
                    K.dma("sp", DM[:, h, t0:t0 + 512], OTh[:, :], reads=[OTB])
            proj_pass(I["df_w_o"], 8, 128, DM, T, G)

        def convmod(ph, l, NS, L):
            T = NS * L
            A, Bv, G = ab(1)
            o = 8192
            U = view(o, (128, 8, 286)); o += 8 * 286
            ACC = view(o, (128, 8, 256)); o += 2048
            SQ = view(o, (128, 8, 256)); o += 2048
            MEAN = view(o, (128, 256)); o += 256
            SG = view(o, (128, 512)); o += 512
            UB = [Buf() for _ in range(8)]
            ACB = [Buf() for _ in range(8)]
            SQB, MEB, SGB = Buf(), Buf(), Buf()
            cbi, cwd = V("cbi"), V("cwd").rearrange("p (c j) -> p c j", j=31)
            GBt = SM[:, 48:56]
            OP("dve", lambda e: e.tensor_tensor(out=GBt, in0=G, in1=V("cbo"), op=ALU.mult), reads=[ABB, VB], writes=[SMB])
            upl = L // 256
            def unit(u):
                s, oo = u // upl, (u % upl) * 256
                lo, hi = max(oo - 15, 0), min(oo + 271, L)
                n = hi - lo
                off = lo - (oo - 15)
                hv, hb = make_h(s * L + lo, n, A, Bv)
                if off > 0:
                    OP("pool", lambda e: e.memset(U[:, :, 0:off], 0.0), writes=UB)
                if off + n < 286:
                    OP("pool", lambda e: e.memset(U[:, :, off + n:286], 0.0), writes=UB)

                def epi_in(j, pso, pb):
                    if j < 8:
                        OP("act", lambda e: e.activation(out=U[:, j, off:off + n], in_=pso, func=AF.Identity,
                                                         bias=cbi[:, j:j + 1], scale=1.0), reads=[pb, VB], writes=[UB[j]])
                    else:
                        jj = j - 8
                        OP("act", lambda e: e.activation(out=SG[:, 0:n], in_=pso, func=AF.Sigmoid, bias=cbi[:, j:j + 1], scale=1.0),
                           reads=[pb, VB], writes=[SGB])
                        OP("dve", lambda e: e.tensor_tensor(out=U[:, jj, off:off + n], in0=U[:, jj, off:off + n], in1=SG[:, 0:n],
                                                            op=ALU.mult), reads=[SGB, UB[jj]], writes=[UB[jj]])
                linear(I["cv_w_in"], 8, 128, 0, 2048, 128, lambda kc: (hv[:, kc, 0:n], [hb]), n, epi_in)
                for c in range(8):
                    OP("dve", lambda e, c=c: e.tensor_scalar(out=ACC[:, c, :], in0=U[:, c, 0:256], scalar1=cwd[:, c, 0:1],
                                                             scalar2=V("cbd")[:, c:c + 1], op0=ALU.mult, op1=ALU.add),
                       reads=[UB[c], VB], writes=[ACB[c]])
                for j in range(1, 31):
                    for c in range(8):
                        OP("dve", lambda e, c=c, j=j: e.scalar_tensor_tensor(out=ACC[:, c, :], in0=U[:, c, j:j + 256],
                                                                              scalar=cwd[:, c, j:j + 1], in1=ACC[:, c, :],
                                                                              op0=ALU.mult, op1=ALU.add),
                           reads=[UB[c], VB, ACB[c]], writes=[ACB[c]])
                OP("act", lambda e: e.activation(out=SQ[:, :, :], in_=ACC[:, :, :], func=AF.Square), reads=ACB, writes=[SQB])

                def mm1(e):
                    ins = None
                    for c in range(8):
                        ins = e.matmul(ps[2][:, 0:256], ones, ACC[:, c, :], start=(c == 0), stop=(c == 7))
                    return ins
                OP("pe", mm1, reads=ACB + [CMB], writes=[PB[2]])

                def mm2(e):
                    ins = None
                    for c in range(8):
                        ins = e.matmul(ps[3][:, 0:256], ones, SQ[:, c, :], start=(c == 0), stop=(c == 7))
                    return ins
                OP("pe", mm2, reads=[SQB, CMB], writes=[PB[3]])
                OP("dve", lambda e: e.tensor_scalar(out=MEAN[:, :], in0=ps[2][:, 0:256], scalar1=1.0 / 1024, scalar2=None,
                                                    op0=ALU.mult), reads=[PB[2]], writes=[MEB])
                OP("dve", lambda e: e.tensor_tensor(out=SG[:, 0:256], in0=MEAN[:, :], in1=MEAN[:, :], op=ALU.mult),
                   reads=[MEB], writes=[SGB])
                OP("dve", lambda e: e.scalar_tensor_tensor(out=SG[:, 0:256], in0=ps[3][:, 0:256], scalar=1.0 / 1024,
                                                           in1=SG[:, 0:256], op0=ALU.mult, op1=ALU.subtract),
                   reads=[PB[3], SGB], writes=[SGB])
                rs, rsb = rstd_from(SG[:, 0:256], SGB, 128, 256, 1.0, LN_EPS)
                OP("dve", lambda e: e.tensor_tensor(out=ACC[:, :, :], in0=ACC[:, :, :],
                                                    in1=MEAN[:, :].unsqueeze(1).to_broadcast([128, 8, 256]), op=ALU.subtract),
                   reads=ACB + [MEB], writes=ACB)
                OP("dve", lambda e: e.tensor_tensor(out=ACC[:, :, :], in0=ACC[:, :, :],
                                                    in1=rs.unsqueeze(1).to_broadcast([128, 8, 256]), op=ALU.mult),
                   reads=ACB + [rsb], writes=ACB)
                for c in range(8):
                    OP("act", lambda e, c=c: e.activation(out=SQ[:, c, :], in_=ACC[:, c, :], func=AF.Silu,
                                                          scale=V("clg")[:, c:c + 1], bias=V("clb")[:, c:c + 1]),
                       reads=[ACB[c], VB], writes=[SQB])
                K.dma("sp", DM[:, :, s * L + oo:s * L + oo + 256], SQ[:, :, :], reads=[SQB])
            for u in range(T // 256):
                unit(u)
            proj_pass(I["cv_w_out"], 8, 128, DM, T, G, extra=GBt)

        def rwkv(ph, l, NS, L):
            T = NS * L
            A, Bv, G = ab(1)
            upl = L // 256
            o = 8192
            XX = view(o, (128, 8, 256)); o += 2048
            XI = view(o, (128, 8, 256)); o += 2048
            Kt = view(o, (128, 8, 256)); o += 2048
            Vtt = view(o, (128, 8, 256)); o += 2048
            KK = view(o, (128, 8, 256)); o += 2048
            As = view(o, (128, 8, 256)); o += 2048
            KDS = view(o, (128, 8, 256)); o += 2048
            SG = view(o, (128, 256)); o += 256
            TW = view(o, (64, 256)); o += 256
            CH = []
            for _ in range(4):
                CH.append(view(o, (128, 256))); o += 256
            assert o <= 24576
            XXB, XIB, KtB, VtB, KKB, AsB, KDSB, SGB, TWB = [Buf() for _ in range(9)]
            CHB = [Buf() for _ in range(4)]
            mix = V("mix")
            OMK = SM[:, 56:64]
            OP("dve", lambda e: e.tensor_scalar(out=OMK, in0=V("ka"), scalar1=-1.0, scalar2=1.0, op0=ALU.mult, op1=ALU.add),
               reads=[VB], writes=[SMB])

            def chtile():
                i = cnt["ch"] % 4
                cnt["ch"] += 1
                return CH[i], CHB[i]

            def dsl(name, m, tk0):
                return DS[name][:, m, tk0:tk0 + 256]

            def preunit(u):
                s, oo = u // upl, (u % upl) * 256
                tk0 = s * L + oo
                lo, hi = max(oo - 1, 0), min(oo + 257, L)
                n = hi - lo
                off = lo - (oo - 1)
                hv, hb = make_h(s * L + lo, n, A, Bv, off=off)
                if off > 0:
                    OP("pool", lambda e: e.memset(hv[:, :, 0:1], 0.0), writes=[hb])
                if off + n < 258:
                    OP("pool", lambda e: e.memset(hv[:, :, 257:258], 0.0), writes=[hb])
                hmid = hv[:, :, 1:257]
                OP("dve", lambda e: e.tensor_tensor(out=XX[:, :, :], in0=hv[:, :, 0:256], in1=hv[:, :, 2:258], op=ALU.add),
                   reads=[hb], writes=[XXB])
                OP("dve", lambda e: e.scalar_tensor_tensor(out=XX[:, :, :], in0=XX[:, :, :], scalar=0.5, in1=hmid,
                                                           op0=ALU.mult, op1=ALU.subtract), reads=[XXB, hb], writes=[XXB])

                def mk_xi(i):
                    for c in range(8):
                        OP("dve", lambda e, c=c: e.scalar_tensor_tensor(out=XI[:, c, :], in0=XX[:, c, :],
                                                                        scalar=mix[:, i * 8 + c:i * 8 + c + 1], in1=hmid[:, c, :],
                                                                        op0=ALU.mult, op1=ALU.add),
                           reads=[XXB, hb, VB], writes=[XIB])
                rhsXI = lambda kc: (XI[:, kc, :], [XIB])

                def copy_epi(dst, dstb):
                    def epi(m, pso, pb):
                        OP("act", lambda e: e.activation(out=dst[:, m, :], in_=pso, func=AF.Copy), reads=[pb], writes=[dstb])
                    return epi
                mk_xi(2)
                linear(I["rw_w_k"], 8, 128, 0, 1024, 128, rhsXI, 256, copy_epi(Kt, KtB))
                mk_xi(3)
                linear(I["rw_w_v"], 8, 128, 0, 1024, 128, rhsXI, 256, copy_epi(Vtt, VtB))
                K.dma("sp", DS["V"][:, :, tk0:tk0 + 256], Vtt[:, :, :], reads=[VtB])
                mk_xi(5)
                linear(I["rw_g1"], 8, 128, 0, 128, 128, rhsXI, 256,
                       lambda j, pso, pb: OP("act", lambda e: e.activation(out=SG[:, :], in_=pso, func=AF.Sigmoid),
                                             reads=[pb], writes=[SGB]))

                def epi_g(m, pso, pb):
                    t, tb = chtile()
                    OP("act", lambda e: e.activation(out=t, in_=pso, func=AF.Copy), reads=[pb], writes=[tb])
                    K.dma("sp", dsl("G", m, tk0), t, reads=[tb])
                linear(I["rw_g2"], 1, 128, 0, 1024, 128, lambda kc: (SG[:, :], [SGB]), 256, epi_g)
                for c in range(8):
                    OP("dve", lambda e, c=c: e.tensor_scalar(out=KK[:, c, :], in0=Kt[:, c, :], scalar1=V("kk")[:, c:c + 1],
                                                             scalar2=None, op0=ALU.mult), reads=[KtB, VB], writes=[KKB])
                    t, tb = chtile()
                    OP("act", lambda e, c=c, t=t: e.activation(out=t, in_=KK[:, c, :], func=AF.Square), reads=[KKB], writes=[tb])
                    OP("pe", lambda e, t=t: e.matmul(ps[3][:, 0:256], BO, t, start=True, stop=True), reads=[tb, CMB], writes=[PB[3]])
                    OP("dve", lambda e, t=t: e.tensor_scalar(out=t, in0=ps[3][:, 0:256], scalar1=1e-24, scalar2=None, op0=ALU.max),
                       reads=[PB[3]], writes=[tb])
                    OP("act", lambda e, t=t: e.activation(out=t, in_=t, func=AF.Sqrt), reads=[tb], writes=[tb])
                    OP("dve", lambda e, t=t: e.reciprocal(out=t, in_=t), reads=[tb], writes=[tb])
                    OP("dve", lambda e, c=c, t=t: e.tensor_tensor(out=KK[:, c, :], in0=KK[:, c, :], in1=t, op=ALU.mult),
                       reads=[KKB, tb], writes=[KKB])
                    t2, t2b = chtile()
                    OP("act", lambda e, c=c, t2=t2: e.activation(out=t2, in_=KK[:, c, :], func=AF.Copy, scale=-1.0),
                       reads=[KKB], writes=[t2b])
                    K.dma("sp", dsl("A", c, tk0), t2, reads=[t2b])
                for d in range(2):
                    mk_xi(1)
                    linear(I["rw_w1"][d], 8, 128, 0, 64, 64, rhsXI, 256,
                           lambda j, pso, pb: OP("act", lambda e: e.activation(out=TW[:, :], in_=pso, func=AF.Tanh),
                                                 reads=[pb], writes=[TWB]))

                    def epi_w(m, pso, pb, d=d):
                        t, tb = chtile()
                        OP("act", lambda e: e.activation(out=t, in_=pso, func=AF.Sigmoid, bias=V("w0")[:, d * 8 + m:d * 8 + m + 1],
                                                         scale=1.0), reads=[pb, VB], writes=[tb])
                        OP("act", lambda e: e.activation(out=t, in_=t, func=AF.Exp, scale=-math.exp(-0.5)), reads=[tb], writes=[tb])
                        K.dma("sp", dsl("W%d" % d, m, tk0), t, reads=[tb])
                    linear(I["rw_w2"][d], 1, 64, 0, 1024, 128, lambda kc: (TW[:, :], [TWB]), 256, epi_w)
                    mk_xi(4)
                    linear(I["rw_a1"][d], 8, 128, 0, 64, 64, rhsXI, 256,
                           lambda j, pso, pb: OP("act", lambda e: e.activation(out=TW[:, :], in_=pso, func=AF.Copy),
                                                 reads=[pb], writes=[TWB]))

                    def epi_a(m, pso, pb, d=d):
                        OP("act", lambda e: e.activation(out=As[:, m, :], in_=pso, func=AF.Sigmoid,
                                                         bias=V("a0")[:, d * 8 + m:d * 8 + m + 1], scale=1.0),
                           reads=[pb, VB], writes=[AsB])
                        t, tb = chtile()
                        OP("dve", lambda e: e.tensor_scalar(out=t, in0=As[:, m, :], scalar1=V("ka")[:, m:m + 1], scalar2=OMK[:, m:m + 1],
                                                            op0=ALU.mult, op1=ALU.add), reads=[AsB, VB, SMB], writes=[tb])
                        OP("dve", lambda e: e.tensor_tensor(out=t, in0=t, in1=Kt[:, m, :], op=ALU.mult), reads=[tb, KtB], writes=[tb])
                        K.dma("sp", dsl("KD%d" % d, m, tk0), t, reads=[tb])
                        if d == 0:
                            OP("pool", lambda e: e.tensor_copy(out=KDS[:, m, :], in_=t), reads=[tb], writes=[KDSB])
                        else:
                            OP("pool", lambda e: e.tensor_tensor(out=KDS[:, m, :], in0=KDS[:, m, :], in1=t, op=ALU.add),
                               reads=[tb, KDSB], writes=[KDSB])
                        t2, t2b = chtile()
                        OP("pool", lambda e: e.tensor_tensor(out=t2, in0=KK[:, m, :], in1=As[:, m, :], op=ALU.mult),
                           reads=[KKB, AsB], writes=[t2b])
                        K.dma("sp", dsl("B%d" % d, m, tk0), t2, reads=[t2b])
                    linear(I["rw_a2"][d], 1, 64, 0, 1024, 128, lambda kc: (TW[:, :], [TWB]), 256, epi_a)
                mk_xi(0)

                def epi_r(m, pso, pb):
                    t, tb = chtile()
                    OP("act", lambda e: e.activation(out=t, in_=pso, func=AF.Copy), reads=[pb], writes=[tb])
                    K.dma("sp", dsl("R", m, tk0), t, reads=[tb])
                    t2, t2b = chtile()
                    OP("dve", lambda e: e.scalar_tensor_tensor(out=t2, in0=t, scalar=V("rk")[:, m:m + 1], in1=KDS[:, m, :],
                                                               op0=ALU.mult, op1=ALU.mult), reads=[tb, VB, KDSB], writes=[t2b])
                    OP("pe", lambda e: e.matmul(ps[3][:, 0:256], BO, t2, start=True, stop=True), reads=[t2b, CMB], writes=[PB[3]])
                    OP("dve", lambda e: e.tensor_tensor(out=t2, in0=ps[3][:, 0:256], in1=Vtt[:, m, :], op=ALU.mult),
                       reads=[PB[3], VtB], writes=[t2b])
                    K.dma("sp", dsl("BON", m, tk0), t2, reads=[t2b])
                linear(I["rw_w_r"], 8, 128, 0, 1024, 128, rhsXI, 256, epi_r)
            for u in range(T // 256):
                preunit(u)
            K.barrier()
            if os.environ.get('K_RW') == 'pre':
                return

            def seq_scan_all():
                TC = 64
                nch = L // TC
                Hs = view(0, (128, 2, 8, 64))
                Hs2 = view(0, (128, 1024))
                HsB = [Buf(), Buf()]
                o = 1024
                names = ["a", "w", "b", "kd", "r", "v"]
                dsn = {"a": ("A", "A"), "w": ("W0", "W1"), "b": ("B0", "B1"), "kd": ("KD0", "KD1"), "r": ("R", "R"), "v": ("V", "V")}
                CTt = [[{} for _ in range(2)] for _ in range(2)]
                CTB = [[{} for _ in range(2)] for _ in range(2)]
                for par in range(2):
                    for d in range(2):
                        for nm in names:
                            CTt[par][d][nm] = view(o, (128, 8, TC)); o += 512
                            CTB[par][d][nm] = Buf()
                YC = [[None, None], [None, None]]
                YCB = [[Buf(), Buf()], [Buf(), Buf()]]
                for par in range(2):
                    for d in range(2):
                        YC[par][d] = view(o, (128, 8, TC)); o += 512
                TA, VD, VS, TR = [], [], [], []
                for lst in (TA, VD, VS, TR):
                    for gp in range(2):
                        lst.append((view(o, (128, 8, 64)), view(o, (128, 512)), Buf())); o += 512
                assert o <= 24576
                mask_bc = maskI.unsqueeze(1).to_broadcast([128, 8, 64])

                def scan_seq(s):
                    base = s * L
                    if ph == 0:
                        OP("pool", lambda e: e.memset(Hs2, 0.0), writes=HsB)
                    else:
                        K.dma("sp", Hs, I["h0"], writes=HsB)

                    def load_chunk(ci):
                        par = ci % 2
                        for d in range(2):
                            tok0 = base + (ci * TC if d == 0 else L - (ci + 1) * TC)
                            for nm in names:
                                K.dma("sp", CTt[par][d][nm], DS[dsn[nm][d]][:, :, tok0:tok0 + TC], writes=[CTB[par][d][nm]])
                    load_chunk(0)
                    for ci in range(nch):
                        par = ci % 2
                        if ci + 1 < nch:
                            load_chunk(ci + 1)
                        for j in range(TC):
                            for d in range(2):
                                col = j if d == 0 else TC - 1 - j
                                gp = d
                                Hg = Hs[:, d, :, :]
                                hb_ = HsB[d]
                                ct, cb = CTt[par][d], CTB[par][d]

                                def bc(nm, ct=ct, col=col):
                                    return ct[nm][:, :, col:col + 1].to_broadcast([128, 8, 64])
                                ta3, ta2, tab = TA[gp]
                                vd3, vd2, vdb = VD[gp]
                                vs3, vs2, vsb = VS[gp]
                                tr3, tr2, trb = TR[gp]
                                pu, pv, py = ps[gp], ps[2 + gp], ps[4 + gp]
                                pu3 = pu[:, :].rearrange("p (h v) -> p h v", v=64)
                                py3 = py[:, :].rearrange("p (h v) -> p h v", v=64)
                                OP("dve", lambda e, ta3=ta3, Hg=Hg, a=bc("a"): e.tensor_tensor(out=ta3, in0=Hg, in1=a, op=ALU.mult),
                                   reads=[hb_, cb["a"]], writes=[tab])
                                OP("pe", lambda e, pu=pu, ta2=ta2: e.matmul(pu[:, :], BO, ta2, start=True, stop=True),
                                   reads=[tab, CMB], writes=[PB[gp]])
                                OP("pool", lambda e, Hg=Hg, w=bc("w"): e.tensor_tensor(out=Hg, in0=Hg, in1=w, op=ALU.mult),
                                   reads=[hb_, cb["w"]], writes=[hb_])
                                OP("dve", lambda e, vd3=vd3, v=bc("v"): e.tensor_tensor(out=vd3, in0=mask_bc, in1=v, op=ALU.mult),
                                   reads=[cb["v"], CMB], writes=[vdb])
                                OP("pe", lambda e, pv=pv, vd2=vd2: e.matmul(pv[:, :], BO, vd2, start=True, stop=True),
                                   reads=[vdb, CMB], writes=[PB[2 + gp]])
                                OP("act", lambda e, pv=pv, vs2=vs2: e.activation(out=vs2, in_=pv[:, :], func=AF.Copy),
                                   reads=[PB[2 + gp]], writes=[vsb])
                                OP("pool", lambda e, vd3=vd3, vs3=vs3, kd=bc("kd"): e.tensor_tensor(out=vd3, in0=vs3, in1=kd, op=ALU.mult),
                                   reads=[vsb, cb["kd"]], writes=[vdb])
                                OP("pool", lambda e, Hg=Hg, vd3=vd3: e.tensor_tensor(out=Hg, in0=Hg, in1=vd3, op=ALU.add),
                                   reads=[hb_, vdb], writes=[hb_])
                                OP("dve", lambda e, ta3=ta3, pu3=pu3, b=bc("b"): e.tensor_tensor(out=ta3, in0=pu3, in1=b, op=ALU.mult),
                                   reads=[PB[gp], cb["b"]], writes=[tab])
                                OP("dve", lambda e, Hg=Hg, ta3=ta3: e.tensor_tensor(out=Hg, in0=Hg, in1=ta3, op=ALU.add),
                                   reads=[hb_, tab], writes=[hb_])
                                OP("dve", lambda e, tr3=tr3, Hg=Hg, r=bc("r"): e.tensor_tensor(out=tr3, in0=Hg, in1=r, op=ALU.mult),
                                   reads=[hb_, cb["r"]], writes=[trb])
                                OP("pe", lambda e, py=py, tr2=tr2: e.matmul(py[:, :], BO, tr2, start=True, stop=True),
                                   reads=[trb, CMB], writes=[PB[4 + gp]])
                                OP("dve", lambda e, tr3=tr3, py3=py3: e.tensor_tensor(out=tr3, in0=py3, in1=mask_bc, op=ALU.mult),
                                   reads=[PB[4 + gp], CMB], writes=[trb])
                                yout = YC[par][d][:, :, col]
                                OP("dve", lambda e, yout=yout, tr3=tr3: e.tensor_reduce(out=yout, in_=tr3, axis=AX.X, op=ALU.add),
                                   reads=[trb], writes=[YCB[par][d]])
                        for d in range(2):
                            tok0 = base + (ci * TC if d == 0 else L - (ci + 1) * TC)
                            K.dma("sp", DS["Y%d" % d][:, :, tok0:tok0 + TC], YC[par][d], reads=[YCB[par][d]])
                    if ph == 0:
                        K.dma("sp", O["hst"][:, s, :, :, :], Hs, reads=HsB)
                for s in range(NS):
                    scan_seq(s)
                K.barrier()
                if os.environ.get('K_RW') == 'scan':
                    return


            if os.environ.get("K_SCAN", "chunk") == "chunk":
                CL = 64
                nchk = L // CL
                oo = [0]

                def al(shape):
                    n = 1
                    for x_ in shape[1:]:
                        n *= x_
                    v_ = view(oo[0], shape)
                    oo[0] += n
                    return v_, Buf()
                Hs, _ = al((128, 2, 8, 64))
                HsB = [Buf(), Buf()]
                IN = []
                SETS = []
                for par in range(2):
                    ARt, _ = al((128, 8, 128))
                    IN.append(dict(AR=ARt, ARa=Buf(), ARr=Buf(), B=al((128, 8, 64)), KD=al((128, 8, 64)),
                                   V=al((128, 8, 64)), W=al((128, 8, 64))))
                    st_ = {}
                    for nm in ("C0", "C1", "EX", "WC", "WI", "Nn", "Qt", "Btok", "Ktok", "Vtok", "Zt", "Ut", "Ych"):
                        st_[nm] = al((128, 8, 64))
                    st_["X1s"] = al((128, 8, 128))
                    st_["X2s"] = al((128, 8, 128))
                    SETS.append(st_)
                assert oo[0] <= 24576
                I_bc = maskI.unsqueeze(1).to_broadcast([128, 8, 64])

                def ps3(bank):
                    return ps[bank][:, :].rearrange("p (h v) -> p h v", v=64)

                def mm_hp(e, bank, lhs_fn, rhs_fn, start=True, stop=True, width=64):
                    ins = None
                    for hp in range(8):
                        for eh in range(2):
                            bs = eh * 64
                            if width == 64:
                                out = ps[bank][bs:bs + 64, hp * 64:(hp + 1) * 64]
                            else:
                                out = ps[bank + hp // 4][bs:bs + 64, (hp % 4) * 128:(hp % 4 + 1) * 128]
                            ins = e.matmul(out, lhs_fn(hp, bs), rhs_fn(hp, bs), start=start, stop=stop)
                    return ins

                def chunk(s, d, c, par):
                    tk0 = s * L + (c * CL if d == 0 else L - (c + 1) * CL)
                    inn = IN[par]
                    S_ = SETS[par]
                    C0, C0B = S_["C0"]; C1, C1B = S_["C1"]; EX, EXB = S_["EX"]; WC, WCB = S_["WC"]; WI, WIB = S_["WI"]
                    Nn, NnB = S_["Nn"]; X1s, X1B = S_["X1s"]; X2s, X2B = S_["X2s"]; Qt, QB = S_["Qt"]
                    Btok, BtokB = S_["Btok"]; Ktok, KtokB = S_["Ktok"]; Vtok, VtokB = S_["Vtok"]
                    Zt, ZB = S_["Zt"]; Ut, UB_ = S_["Ut"]; Ych, YchB = S_["Ych"]
                    PP = [S_["C0"], S_["C1"]]
                    NNt = [S_["EX"], S_["WI"]]
                    AR = inn["AR"]
                    Bt, BtB = inn["B"]; KDt, KDB = inn["KD"]; Vt_, VtB_ = inn["V"]; Wt, WtB = inn["W"]
                    ARa, ARr = inn["ARa"], inn["ARr"]
                    K.dma("sp", AR[:, :, 0:64], DS["A"][:, :, tk0:tk0 + CL], writes=[ARa])
                    K.dma("sp", AR[:, :, 64:128], DS["R"][:, :, tk0:tk0 + CL], writes=[ARr])
                    K.dma("sp", Bt, DS["B%d" % d][:, :, tk0:tk0 + CL], writes=[BtB])
                    K.dma("sp", KDt, DS["KD%d" % d][:, :, tk0:tk0 + CL], writes=[KDB])
                    K.dma("sp", Vt_, DS["V"][:, :, tk0:tk0 + CL], writes=[VtB_])
                    K.dma("sp", Wt, DS["W%d" % d][:, :, tk0:tk0 + CL], writes=[WtB])
                    OP("act", lambda e: e.activation(out=Wt, in_=Wt, func=AF.Ln), reads=[WtB], writes=[WtB])
                    seq = [(Wt, WtB), (C0, C0B), (C1, C1B), (C0, C0B), (C1, C1B), (C0, C0B), (C1, C1B)]
                    for i, sh in enumerate((1, 2, 4, 8, 16, 32)):
                        (src, srcB), (dst, dstB) = seq[i], seq[i + 1]
                        if d == 0:
                            OP("dve", lambda e, src=src, dst=dst, sh=sh: e.tensor_tensor(
                                out=dst[:, :, sh:64], in0=src[:, :, sh:64], in1=src[:, :, 0:64 - sh], op=ALU.add),
                               reads=[srcB], writes=[dstB])
                            OP("pool", lambda e, src=src, dst=dst, sh=sh: e.tensor_copy(out=dst[:, :, 0:sh], in_=src[:, :, 0:sh]),
                               reads=[srcB], writes=[dstB])
                        else:
                            OP("dve", lambda e, src=src, dst=dst, sh=sh: e.tensor_tensor(
                                out=dst[:, :, 0:64 - sh], in0=src[:, :, 0:64 - sh], in1=src[:, :, sh:64], op=ALU.add),
                               reads=[srcB], writes=[dstB])
                            OP("pool", lambda e, src=src, dst=dst, sh=sh: e.tensor_copy(out=dst[:, :, 64 - sh:64], in_=src[:, :, 64 - sh:64]),
                               reads=[srcB], writes=[dstB])
                    CU, CUB = C1, C1B
                    OP("dve", lambda e: e.tensor_tensor(out=C0, in0=CU, in1=Wt, op=ALU.subtract), reads=[CUB, WtB], writes=[C0B])
                    OP("act", lambda e: e.activation(out=EX, in_=C0, func=AF.Exp), reads=[C0B], writes=[EXB])
                    OP("act", lambda e: e.activation(out=WC, in_=CU, func=AF.Exp), reads=[CUB], writes=[WCB])
                    OP("act", lambda e: e.activation(out=WI, in_=CU, func=AF.Exp, scale=-1.0), reads=[CUB], writes=[WIB])
                    OP("dve", lambda e: e.tensor_tensor(out=AR[:, :, 0:64], in0=AR[:, :, 0:64], in1=EX, op=ALU.mult),
                       reads=[ARa, EXB], writes=[ARa])
                    OP("pool", lambda e: e.tensor_tensor(out=AR[:, :, 64:128], in0=AR[:, :, 64:128], in1=WC, op=ALU.mult),
                       reads=[ARr, WCB], writes=[ARr])
                    OP("dve", lambda e: e.tensor_tensor(out=Bt, in0=Bt, in1=WI, op=ALU.mult), reads=[BtB, WIB], writes=[BtB])
                    OP("pool", lambda e: e.tensor_tensor(out=KDt, in0=KDt, in1=WI, op=ALU.mult), reads=[KDB, WIB], writes=[KDB])
                    yield
                    NM = cm[:, 4 + d, 0:64].unsqueeze(1).to_broadcast([128, 8, 64])
                    MK = cm[:, 6 + d, :].unsqueeze(1).to_broadcast([128, 4, 128])
                    OP("pe", lambda e: mm_hp(e, 0, lambda hp, bs: AR[bs:bs + 64, hp, 0:64], lambda hp, bs: Bt[bs:bs + 64, hp, :]),
                       reads=[ARa, BtB], writes=[PB[0]])
                    OP("dve", lambda e: e.tensor_tensor(out=Nn, in0=ps3(0), in1=NM, op=ALU.mult), reads=[PB[0], CMB], writes=[NnB])
                    OP("pe", lambda e: mm_hp(e, 1, lambda hp, bs: Bt[bs:bs + 64, hp, :], lambda hp, bs: AR[bs:bs + 64, hp, :], width=128),
                       reads=[ARa, ARr, BtB], writes=[PB[1], PB[2]])
                    OP("pe", lambda e: mm_hp(e, 3, lambda hp, bs: KDt[bs:bs + 64, hp, :], lambda hp, bs: AR[bs:bs + 64, hp, :], width=128),
                       reads=[ARa, ARr, KDB], writes=[PB[3], PB[4]])
                    for half in range(2):
                        OP("dve", lambda e, half=half: e.tensor_tensor(
                            out=X1s[:, half * 4:(half + 1) * 4, :], in0=ps[1 + half][:, :].rearrange("p (h v) -> p h v", v=128),
                            in1=MK, op=ALU.mult), reads=[PB[1 + half], CMB], writes=[X1B])
                        OP("dve", lambda e, half=half: e.tensor_tensor(
                            out=X2s[:, half * 4:(half + 1) * 4, :], in0=ps[3 + half][:, :].rearrange("p (h v) -> p h v", v=128),
                            in1=MK, op=ALU.mult), reads=[PB[3 + half], CMB], writes=[X2B])
                    yield
                    OP("dve", lambda e: e.tensor_tensor(out=Qt, in0=X1s[:, :, 0:64], in1=I_bc, op=ALU.add), reads=[X1B, CMB], writes=[QB])
                    Pp, PpB = X1s[:, :, 0:64], X1B
                    Np, NpB = Nn, NnB
                    for lvl in range(1, 6):
                        Pn, PnB = PP[lvl % 2]
                        Nx, NxB = NNt[lvl % 2]
                        if lvl < 5:
                            OP("pe", lambda e, Np=Np, Pp=Pp: mm_hp(e, 5, lambda hp, bs: Np[bs:bs + 64, hp, :], lambda hp, bs: Pp[bs:bs + 64, hp, :]),
                               reads=[NpB, PpB], writes=[PB[5]])
                            OP("act", lambda e, Pn=Pn: e.activation(out=Pn, in_=ps3(5), func=AF.Copy), reads=[PB[5]], writes=[PnB])
                        OP("pe", lambda e, Np=Np, Pp=Pp: mm_hp(e, 6, lambda hp, bs: Pp[bs:bs + 64, hp, :], lambda hp, bs: Np[bs:bs + 64, hp, :]),
                           reads=[NpB, PpB], writes=[PB[6]])
                        OP("act", lambda e, Nx=Nx: e.activation(out=Nx, in_=ps3(6), func=AF.Copy), reads=[PB[6]], writes=[NxB])
                        OP("pe", lambda e, Nx=Nx: mm_hp(e, 7, lambda hp, bs: Nx[bs:bs + 64, hp, :], lambda hp, bs: Qt[bs:bs + 64, hp, :]),
                           reads=[NxB, QB], writes=[PB[7]])
                        OP("dve", lambda e: e.tensor_tensor(out=Qt, in0=Qt, in1=ps3(7), op=ALU.add), reads=[QB, PB[7]], writes=[QB])
                        Pp, PpB, Np, NpB = Pn, PnB, Nx, NxB
                        yield
                    yield
                    for src, srcB, dst, dstB, bank in ((Bt, BtB, Btok, BtokB, 0), (KDt, KDB, Ktok, KtokB, 1), (Vt_, VtB_, Vtok, VtokB, 2)):
                        OP("pe", lambda e, src=src, bank=bank: mm_hp(e, bank, lambda hp, bs: src[bs:bs + 64, hp, :],
                                                                      lambda hp, bs: maskI[bs:bs + 64, :]),
                           reads=[srcB, CMB], writes=[PB[bank]])
                        OP("act", lambda e, dst=dst, bank=bank: e.activation(out=dst, in_=ps3(bank), func=AF.Copy),
                           reads=[PB[bank]], writes=[dstB])
                    yield
                    Hd = Hs[:, d, :, :]
                    hb_ = HsB[d]

                    def mm_acc(e, bank, terms):
                        ins = None
                        for hp in range(8):
                            for eh in range(2):
                                bs = eh * 64
                                out = ps[bank][bs:bs + 64, hp * 64:(hp + 1) * 64]
                                for i, (lf, rf) in enumerate(terms):
                                    ins = e.matmul(out, lf(hp, bs), rf(hp, bs), start=(i == 0), stop=(i == len(terms) - 1))
                        return ins

                    def mmZ(e):
                        return mm_acc(e, 3, [(lambda hp, bs: AR[bs:bs + 64, hp, 0:64], lambda hp, bs: Hd[bs:bs + 64, hp, :]),
                                             (lambda hp, bs: X2s[bs:bs + 64, hp, 0:64], lambda hp, bs: Vtok[bs:bs + 64, hp, :])])
                    OP("pe", mmZ, reads=[ARa, hb_, X2B, VtokB], writes=[PB[3]])
                    OP("act", lambda e: e.activation(out=Zt, in_=ps3(3), func=AF.Copy), reads=[PB[3]], writes=[ZB])
                    yield
                    OP("pe", lambda e: mm_hp(e, 4, lambda hp, bs: Qt[bs:bs + 64, hp, :], lambda hp, bs: Zt[bs:bs + 64, hp, :]),
                       reads=[QB, ZB], writes=[PB[4]])
                    OP("act", lambda e: e.activation(out=Ut, in_=ps3(4), func=AF.Copy), reads=[PB[4]], writes=[UB_])
                    yield

                    def mmY(e):
                        return mm_acc(e, 5, [(lambda hp, bs: Hd[bs:bs + 64, hp, :], lambda hp, bs: AR[bs:bs + 64, hp, 64:128]),
                                             (lambda hp, bs: Ut[bs:bs + 64, hp, :], lambda hp, bs: X1s[bs:bs + 64, hp, 64:128]),
                                             (lambda hp, bs: Vtok[bs:bs + 64, hp, :], lambda hp, bs: X2s[bs:bs + 64, hp, 64:128])])
                    OP("pe", mmY, reads=[hb_, ARr, UB_, X1B, VtokB, X2B], writes=[PB[5]])
                    OP("act", lambda e: e.activation(out=Ych, in_=ps3(5), func=AF.Copy), reads=[PB[5]], writes=[YchB])
                    K.dma("sp", DS["Y%d" % d][:, :, tk0:tk0 + CL], Ych, reads=[YchB])

                    def mmH(e):
                        return mm_acc(e, 6, [(lambda hp, bs: Btok[bs:bs + 64, hp, :], lambda hp, bs: Ut[bs:bs + 64, hp, :]),
                                             (lambda hp, bs: Ktok[bs:bs + 64, hp, :], lambda hp, bs: Vtok[bs:bs + 64, hp, :])])
                    OP("pe", mmH, reads=[BtokB, UB_, KtokB, VtokB], writes=[PB[6]])
                    last = 63 if d == 0 else 0
                    OP("dve", lambda e: e.tensor_tensor(out=Hd, in0=Hd, in1=ps3(6), op=ALU.add), reads=[hb_, PB[6]], writes=[hb_])
                    OP("dve", lambda e: e.tensor_tensor(out=Hd, in0=Hd, in1=WC[:, :, last:last + 1].to_broadcast([128, 8, 64]), op=ALU.mult),
                       reads=[hb_, WCB], writes=[hb_])

                cnum = 0
                for s in range(NS):
                    if ph == 0:
                        OP("pool", lambda e: e.memset(view(0, (128, 1024)), 0.0), writes=HsB)
                    else:
                        K.dma("sp", Hs, I["h0"], writes=HsB)
                    for c in range(nchk):
                        gens = [chunk(s, 0, c, 0), chunk(s, 1, c, 1)]
                        while gens:
                            for g_ in list(gens):
                                try:
                                    next(g_)
                                except StopIteration:
                                    gens.remove(g_)
                    if ph == 0:
                        K.dma("sp", O["hst"][:, s, :, :, :], Hs, reads=HsB)
                K.barrier()
                if os.environ.get('K_RW') == 'scan':
                    return
            else:
                seq_scan_all()
            o = 8192
            P0 = view(o, (128, 8, 256)); o += 2048
            P1 = view(o, (128, 8, 256)); o += 2048
            P2 = view(o, (128, 8, 256)); o += 2048
            P3 = view(o, (128, 8, 256)); o += 2048
            CA = view(o, (128, 256)); o += 256
            CBt = view(o, (128, 256)); o += 256
            P0B, P1B, P2B, P3B, CAB, CBB = [Buf() for _ in range(6)]
            for u in range(T // 256):
                tk0 = u * 256
                K.dma("sp", P0[:, :, :], DS["Y0"][:, :, tk0:tk0 + 256], writes=[P0B])
                K.dma("sp", P1[:, :, :], DS["Y1"][:, :, tk0:tk0 + 256], writes=[P1B])
                K.dma("sp", P2[:, :, :], DS["BON"][:, :, tk0:tk0 + 256], writes=[P2B])
                K.dma("sp", P3[:, :, :], DS["G"][:, :, tk0:tk0 + 256], writes=[P3B])
                OP("dve", lambda e: e.tensor_tensor(out=P0[:, :, :], in0=P0[:, :, :], in1=P1[:, :, :], op=ALU.add),
                   reads=[P0B, P1B], writes=[P0B])
                for m in range(8):
                    OP("pe", lambda e, m=m: e.matmul(ps[2][:, 0:256], BO, P0[:, m, :], start=True, stop=True),
                       reads=[P0B, CMB], writes=[PB[2]])
                    OP("dve", lambda e, m=m: e.scalar_tensor_tensor(out=CA[:, :], in0=ps[2][:, 0:256], scalar=-1.0 / 64,
                                                                    in1=P0[:, m, :], op0=ALU.mult, op1=ALU.add),
                       reads=[PB[2], P0B], writes=[CAB])
                    OP("act", lambda e: e.activation(out=CBt[:, :], in_=CA[:, :], func=AF.Square), reads=[CAB], writes=[CBB])
                    OP("pe", lambda e: e.matmul(ps[3][:, 0:256], BO, CBt[:, :], start=True, stop=True),
                       reads=[CBB, CMB], writes=[PB[3]])
                    rs, rsb = rstd_from(ps[3][:, 0:256], PB[3], 128, 256, 1.0 / 64, RW_GN_EPS)
                    OP("dve", lambda e, rs=rs: e.tensor_tensor(out=CA[:, :], in0=CA[:, :], in1=rs, op=ALU.mult),
                       reads=[CAB, rsb], writes=[CAB])
                    OP("dve", lambda e, m=m: e.tensor_scalar(out=CA[:, :], in0=CA[:, :], scalar1=V("rlg")[:, m:m + 1],
                                                             scalar2=V("rlb")[:, m:m + 1], op0=ALU.mult, op1=ALU.add),
                       reads=[CAB, VB], writes=[CAB])
                    OP("dve", lambda e, m=m: e.tensor_tensor(out=CA[:, :], in0=CA[:, :], in1=P2[:, m, :], op=ALU.add),
                       reads=[CAB, P2B], writes=[CAB])
                    OP("dve", lambda e, m=m: e.tensor_tensor(out=P1[:, m, :], in0=CA[:, :], in1=P3[:, m, :], op=ALU.mult),
                       reads=[CAB, P3B], writes=[P1B])
                linear(I["rw_w_o"], 8, 128, 0, 1024, 128, lambda kc: (P1[:, kc, :], [P1B]), 256, resid_epi(G, tk0, 256))

        def final(ph, T):
            Z8 = SM[:, 24:32]
            OP("dve", lambda e: e.memset(Z8, 0.0), writes=[SMB])
            dst = O["yp"] if ph == 0 else O["ys"]
            for blk in range(T // 512):
                t0 = blk * 512
                hv, hb = make_h(t0, 512, V("fg"), Z8)
                K.dma("sp", dst[:, :, t0:t0 + 512], hv[:, :, :], reads=[hb])

        compute_mods()
        K.barrier()
        mixers = [gqa, convmod, diffattn, rwkv]
        for ph in K_PHASES:
            NS, L = (4, 256) if ph == 0 else (1, 2048)
            T = NS * L
            K.dma("sp", xT[:, :, 0:T], I["xp"] if ph == 0 else I["xs"], writes=XB)
            for l in range(K_LAYERS):
                prep_mods(ph, l)
                if os.environ.get('K_NOFFN') != '1':
                    (ffn16 if K_FFN16 else ffn)(ph, l, 0, T)
                K.barrier()
                if os.environ.get('K_NOMIX') != '1':
                    mixers[l](ph, l, NS, L)
                K.barrier()
                if os.environ.get('K_NOFFN') != '1':
                    (ffn16 if K_FFN16 else ffn)(ph, l, 1, T)
                K.barrier()
            final(ph, T)
            K.barrier()
        _PROG['counts'] = {n: (e.count, len(e.prog)) for n, e in K.eng.items()}
        _PROG['dmav'] = dict(K.dma_val)
        K.emit(nc)
    return nc


def kernel(**inp):
    inp = {k: np.asarray(v) for k, v in inp.items()}
    if "nc" not in _PROG:
        _PROG["nc"] = build_program()
    nc = _PROG["nc"]
    cmat = const_mats()
    rope = rope_tables()
    shared = {k: np.ascontiguousarray(inp[k], dtype=np.float32) for k in (
        "mod_w", "ffn_w_in", "ffn_w_down", "gq_w_qkv", "gq_w_o", "cv_w_in", "cv_w_out", "df_w_qkv", "df_w_o",
        "rw_w_r", "rw_w_k", "rw_w_v", "rw_w_o", "rw_g1", "rw_g2", "rw_w1", "rw_w2", "rw_a1", "rw_a2")}
    in_maps = []
    for c in range(K_CORES):
        b = c // 4
        m = dict(shared)
        xp = inp["x_prompt"][4 * c:4 * c + 4].reshape(1024, 1024)
        m["xp"] = np.ascontiguousarray(xp.T.reshape(8, 128, 1024).transpose(1, 0, 2))
        xs = inp["x_sample"][b]
        m["xs"] = np.ascontiguousarray(xs.T.reshape(8, 128, 2048).transpose(1, 0, 2))
        m["vecs"] = pack_vecs(inp, inp["c"][b])
        m["cmat"] = cmat
        m["rope"] = rope
        m["k0c"] = np.ascontiguousarray(inp["cache_k0"][b].transpose(2, 1, 0))
        m["v0c"] = np.ascontiguousarray(inp["cache_v0"][b].reshape(4, 128, 256).transpose(1, 0, 2))
        m["k2c"] = np.ascontiguousarray(inp["cache_k2"][b].transpose(3, 1, 2, 0))
        m["v2c"] = np.ascontiguousarray(inp["cache_v2"][b].reshape(4, 128, 8, 128).transpose(1, 0, 2, 3))
        st = inp["state_wkv3"][b].reshape(2, 8, 2, 64, 64)
        m["h0"] = np.ascontiguousarray(st.transpose(2, 4, 0, 1, 3)).reshape(128, 2, 8, 64)
        in_maps.append(m)
    res = run_bass_kernel_spmd(nc, in_maps, core_ids=list(range(K_CORES)))
    R = res.results
    y_prompt = np.zeros((32, 256, 1024), np.float32)
    y_sample = np.zeros((2, 2048, 1024), np.float32)
    nk0 = np.zeros((32, 256, 4, 64), np.float32)
    nv0 = np.zeros((32, 256, 4, 64), np.float32)
    nk2 = np.zeros((32, 256, 8, 2, 64), np.float32)
    nv2 = np.zeros((32, 256, 8, 128), np.float32)
    nwkv = np.zeros((32, 2, 16, 64, 64), np.float32)
    for c in range(K_CORES):
        r = R[c]
        y_prompt[4 * c:4 * c + 4] = np.asarray(r["yp"]).transpose(2, 1, 0).reshape(4, 256, 1024)
        if c % 4 == 0:
            y_sample[c // 4] = np.asarray(r["ys"]).transpose(2, 1, 0).reshape(2048, 1024)
        nk0[4 * c:4 * c + 4] = np.asarray(r["nk0"]).transpose(2, 1, 0).reshape(4, 256, 4, 64)
        nv0[4 * c:4 * c + 4] = np.asarray(r["nv0"]).reshape(4, 256, 4, 64)
        nk2[4 * c:4 * c + 4] = np.asarray(r["nk2"]).transpose(3, 1, 2, 0).reshape(4, 256, 8, 2, 64)
        nv2[4 * c:4 * c + 4] = np.asarray(r["nv2"]).reshape(4, 256, 8, 128)
        hs = np.asarray(r["hst"]).reshape(2, 64, 4, 2, 8, 64)
        nwkv[4 * c:4 * c + 4] = hs.transpose(2, 3, 4, 0, 5, 1).reshape(4, 2, 16, 64, 64)
    return (y_prompt, y_sample, nk0, nv0, nk2, nv2, nwkv)
```

```python
import contextlib
import math
import os
import numpy as np
import concourse.bass as bass
import concourse.mybir as mybir
from concourse.bass_utils import run_bass_kernel_spmd

F32 = mybir.dt.float32
BF16 = mybir.dt.bfloat16
AF = mybir.ActivationFunctionType
ALU = mybir.AluOpType
AX = mybir.AxisListType

D = 1024
D_FF = 2816
NORM_EPS = 1e-6
LN_EPS = 1e-5
GQ_SCALE = 64 ** -0.5
DF_SCALE = 64 ** -0.5
DF_LAMBDA_INIT = 0.470713018
DF_SUBLN_EPS = 1e-5
RW_GN_EPS = 64e-5
ROPE_THETA = 10000.0
GRID_W = 64

K_LAYERS = int(os.environ.get("K_LAYERS", "4"))
K_PHASES = [int(c) for c in os.environ.get("K_PHASES", "01")]
K_CORES = int(os.environ.get("K_CORES", "8"))
K_FFN16 = os.environ.get("K_FFN16", "1") == "1"
_PROG = {}


class Buf:
    __slots__ = ("w", "r", "w2")

    def __init__(self):
        self.w = None
        self.r = {}
        self.w2 = None


class Eng:
    def __init__(self, name):
        self.name = name
        self.prog = []
        self.count = 0
        self.waited = {}


class KB:
    NDMA = int(os.environ.get("K_NDMA", "8"))

    def __init__(self):
        self.eng = {n: Eng(n) for n in ("pe", "act", "dve", "pool", "sp")}
        self.dma_val = {}
        self.dma_rr = 0

    def _deps(self, e, reads, writes):
        deps = {}
        for b in list(reads) + list(writes):
            if b.w2 is not None and deps.get(b.w2[0], 0) < b.w2[1]:
                deps[b.w2[0]] = b.w2[1]
        for b in reads:
            if b.w is not None and deps.get(b.w[0], 0) < b.w[1]:
                deps[b.w[0]] = b.w[1]
        for b in writes:
            if b.w is not None and deps.get(b.w[0], 0) < b.w[1]:
                deps[b.w[0]] = b.w[1]
            for k, v in b.r.items():
                if deps.get(k, 0) < v:
                    deps[k] = v
        waits = []
        for k, v in deps.items():
            if k == "pe" and e.name == "pe":
                continue
            if e.waited.get(k, 0) < v:
                e.waited[k] = v
                waits.append((k, v))
        return waits

    def op(self, en, fn, reads=(), writes=()):
        e = self.eng[en]
        waits = self._deps(e, reads, writes)
        e.count += 1
        idx = e.count
        e.prog.append((waits, fn, (en, 1)))
        for b in reads:
            if b.r.get(en, 0) < idx:
                b.r[en] = idx
        for b in writes:
            b.w = (en, idx)
            b.w2 = None
            b.r = {}

    def dma(self, qn, out_ap, in_ap, reads=(), writes=()):
        e = self.eng[qn]
        key = "d%d" % (self.dma_rr % self.NDMA)
        self.dma_rr += 1
        prev = self.dma_val.get(key, 0)
        waits = self._deps(e, reads, writes)
        if prev and e.waited.get(key, 0) < prev:
            e.waited[key] = prev
            waits.append((key, prev))
        val = prev + 16
        self.dma_val[key] = val
        e.prog.append((waits, (lambda eng, o=out_ap, i=in_ap: eng.dma_start(out=o, in_=i)), (key, 16)))
        for b in reads:
            if b.r.get(key, 0) < val:
                b.r[key] = val
        for b in writes:
            b.w = (key, val)
            b.w2 = None
            b.r = {}

    def barrier(self):
        cur = {n: e.count for n, e in self.eng.items()}
        for n, e in self.eng.items():
            waits = []
            for k, v in list(cur.items()) + list(self.dma_val.items()):
                if k == n or v == 0:
                    continue
                if e.waited.get(k, 0) < v:
                    e.waited[k] = v
                    waits.append((k, v))
            if waits:
                e.prog.append((waits, None, None))

    def emit(self, nc):
        handles = {"pe": "tensor", "act": "scalar", "dve": "vector", "pool": "gpsimd", "sp": "sync"}
        self.barrier()
        with contextlib.ExitStack() as st:
            sems = {}
            for n in list(self.eng) + ["d%d" % i for i in range(self.NDMA)]:
                sems[n] = st.enter_context(nc.semaphore("s_" + n))
            block = st.enter_context(nc.Block())
            for n, e in self.eng.items():
                def body(engh, e=e):
                    for waits, fn, inc in e.prog:
                        for wk, wv in waits:
                            engh.wait_ge(sems[wk], wv)
                        if fn is not None:
                            fn(engh).then_inc(sems[inc[0]], inc[1])
                getattr(block, handles[n])(body)


def fm(v):
    v = np.asarray(v, np.float32)
    lead = v.shape[:-1]
    C = v.shape[-1] // 128
    return np.ascontiguousarray(np.moveaxis(v.reshape(lead + (C, 128)), -1, 0))


VEC_SPEC = (
    [("ng%d_%d" % (l, i), 8) for l in range(4) for i in range(3)]
    + [("mb%d" % l, 72) for l in range(4)]
    + [("fg", 8), ("gqn", 1), ("gkn", 1), ("cbi", 16), ("cwd", 248), ("cbd", 8), ("clg", 8), ("clb", 8),
       ("cbo", 8), ("dsg", 1), ("lam", 256), ("mix", 48), ("kk", 8), ("ka", 8), ("rk", 8), ("rlg", 8),
       ("rlb", 8), ("w0", 16), ("a0", 16), ("cond", 16)]
)
VEC_OFF = {}
_o = 0
for _n, _w in VEC_SPEC:
    VEC_OFF[_n] = (_o, _w)
    _o += _w
NV = _o


def pack_vecs(inp, c_vec):
    parts = {}
    for l in range(4):
        for i in range(3):
            parts["ng%d_%d" % (l, i)] = fm(inp["norm_g"][l, i])
        parts["mb%d" % l] = fm(inp["mod_b"][l])
    parts["fg"] = fm(inp["final_norm_g"])
    parts["gqn"] = np.tile(np.asarray(inp["gq_q_norm"], np.float32), 2)[:, None]
    parts["gkn"] = np.tile(np.asarray(inp["gq_k_norm"], np.float32), 2)[:, None]
    parts["cbi"] = fm(inp["cv_b_in"])
    parts["cwd"] = np.ascontiguousarray(fm(inp["cv_w_dw"]).transpose(0, 2, 1)).reshape(128, 248)
    parts["cbd"] = fm(inp["cv_b_dw"])
    parts["clg"] = fm(inp["cv_ln_g"])
    parts["clb"] = fm(inp["cv_ln_b"])
    parts["cbo"] = fm(inp["cv_b_out"])
    parts["dsg"] = np.asarray(inp["df_subln_g"], np.float32)[:, None]
    lam = np.concatenate([np.asarray(inp[k], np.float32) for k in
                          ("df_lambda_q1", "df_lambda_k1", "df_lambda_q2", "df_lambda_k2")])
    parts["lam"] = np.broadcast_to(lam[None, :], (128, 256))
    parts["mix"] = fm(inp["rw_mix"]).reshape(128, 48)
    parts["kk"] = fm(inp["rw_k_k"])
    parts["ka"] = fm(inp["rw_k_a"])
    parts["rk"] = fm(np.asarray(inp["rw_r_k"]).reshape(1024))
    parts["rlg"] = fm(inp["rw_ln_g"])
    parts["rlb"] = fm(inp["rw_ln_b"])
    parts["w0"] = fm(inp["rw_w0"]).reshape(128, 16)
    parts["a0"] = fm(inp["rw_a0"]).reshape(128, 16)
    parts["cond"] = np.stack([fm(inp["c_ctx"]), fm(c_vec)], axis=-1).reshape(128, 16)
    out = np.zeros((128, NV), np.float32)
    for n, w in VEC_SPEC:
        o, _ = VEC_OFF[n]
        out[:, o:o + w] = np.asarray(parts[n], np.float32).reshape(128, w)
    return out


def const_mats():
    cm = np.zeros((128, 8, 128), np.float32)
    cm[:, 0, :] = 1.0
    cm[0:64, 1, 0:64] = 1.0
    cm[64:128, 1, 64:128] = 1.0
    cm[0:64, 2, 0:64] = np.eye(64)
    cm[64:128, 2, 0:64] = np.eye(64)
    sw = np.zeros((64, 64), np.float32)
    for i in range(32):
        sw[2 * i, 2 * i + 1] = 1.0
        sw[2 * i + 1, 2 * i] = 1.0
    cm[0:64, 3, 0:64] = sw
    row = np.arange(64)[:, None]
    col = np.arange(64)[None, :]
    lo = (col < row).astype(np.float32)
    up = (col > row).astype(np.float32)
    loi = (col <= row).astype(np.float32)
    upi = (col >= row).astype(np.float32)
    for h in range(2):
        sl = slice(h * 64, (h + 1) * 64)
        cm[sl, 4, 0:64] = lo
        cm[sl, 5, 0:64] = up
        cm[sl, 6, 0:64] = up
        cm[sl, 6, 64:128] = upi
        cm[sl, 7, 0:64] = lo
        cm[sl, 7, 64:128] = loi
    return cm


def rope_tables(n_tokens=2048, head_dim=64):
    t = np.arange(n_tokens)
    row = (t // GRID_W).astype(np.float32)
    col = (t % GRID_W).astype(np.float32)
    axis_dim = head_dim // 2
    freqs = (np.float32(ROPE_THETA) ** (-np.arange(0, axis_dim, 2, dtype=np.float32) / np.float32(axis_dim))).astype(np.float32)
    ang = np.concatenate([row[:, None] * freqs, col[:, None] * freqs], axis=-1).astype(np.float32)
    cos, sin = np.cos(ang).astype(np.float32), np.sin(ang).astype(np.float32)
    tab = np.zeros((64, 2, n_tokens), np.float32)
    for i in range(32):
        tab[2 * i, 0] = cos[:, i]
        tab[2 * i + 1, 0] = cos[:, i]
        tab[2 * i, 1] = -sin[:, i]
        tab[2 * i + 1, 1] = sin[:, i]
    return tab


def build_program():
    nc = bass.Bass("TRN2", target_bir_lowering=False)
    K = KB()
    OP = K.op

    def din(name, shape):
        return nc.dram_tensor(name, list(shape), F32, kind="ExternalInput").ap()

    def dout(name, shape):
        return nc.dram_tensor(name, list(shape), F32, kind="ExternalOutput").ap()

    def dscr(name, shape):
        return nc.dram_tensor(name, list(shape), F32, kind="Internal").ap()

    I = {}
    for name, shape in [
        ("xp", (128, 8, 1024)), ("xs", (128, 8, 2048)), ("vecs", (128, NV)), ("cmat", (128, 8, 128)),
        ("rope", (64, 2, 2048)), ("k0c", (64, 4, 512)), ("v0c", (128, 4, 256)), ("k2c", (64, 8, 2, 512)),
        ("v2c", (128, 4, 8, 128)), ("h0", (128, 2, 8, 64)),
        ("mod_w", (4, 1024, 9216)), ("ffn_w_in", (4, 2, 1024, 2 * D_FF)), ("ffn_w_down", (4, 2, D_FF, 1024)),
        ("gq_w_qkv", (1024, 1536)), ("gq_w_o", (1024, 1024)), ("cv_w_in", (1024, 2048)), ("cv_w_out", (1024, 1024)),
        ("df_w_qkv", (1024, 3072)), ("df_w_o", (1024, 1024)),
        ("rw_w_r", (1024, 1024)), ("rw_w_k", (1024, 1024)), ("rw_w_v", (1024, 1024)), ("rw_w_o", (1024, 1024)),
        ("rw_g1", (1024, 128)), ("rw_g2", (128, 1024)), ("rw_w1", (2, 1024, 64)), ("rw_w2", (2, 64, 1024)),
        ("rw_a1", (2, 1024, 64)), ("rw_a2", (2, 64, 1024)),
    ]:
        I[name] = din(name, shape)
    O = {}
    for name, shape in [("yp", (128, 8, 1024)), ("ys", (128, 8, 2048)), ("nk0", (64, 4, 1024)), ("nv0", (1024, 4, 64)),
                        ("nk2", (64, 8, 2, 1024)), ("nv2", (1024, 8, 128)), ("hst", (128, 4, 2, 8, 64))]:
        O[name] = dout(name, shape)
    SCRN = ["R", "A", "V", "W0", "W1", "KD0", "KD1", "B0", "B1", "Y0", "Y1", "BON", "G"]
    DS = {n: dscr("scr_" + n, (128, 8, 2048)) for n in SCRN}
    DM = dscr("scr_M", (128, 8, 2048))
    DOT = dscr("scr_OT", (64, 16, 2048))

    with contextlib.ExitStack() as st:
        def sb(name, shape):
            return st.enter_context(nc.sbuf_tensor(name, list(shape), F32))

        xT = sb("xT", (128, 8, 2048))
        WS = [sb("ws0", (128, 4096)), sb("ws1", (128, 4096))]
        AR = sb("arena", (128, 24576))
        vecs = sb("vecs_t", (128, NV))
        cm = sb("cmat_t", (128, 8, 128))
        MODS = sb("mods", (128, 2 * 4 * 72))
        ABT = sb("abt", (128, 3 * 3 * 8))
        RS = [sb("rs0", (128, 512)), sb("rs1", (128, 512))]
        SM = sb("small", (128, 64))
        ps = [st.enter_context(nc.psum_tensor("ps%d" % i, [128, 512], F32)) for i in range(8)]
        PB = [Buf() for _ in range(8)]
        WB = [Buf(), Buf()]
        VB, CMB, MB, ABB, RPB, SMB = Buf(), Buf(), Buf(), Buf(), Buf(), Buf()
        RSB = [Buf(), Buf()]
        PTB = [Buf(), Buf()]
        XB = [Buf() for _ in range(8)]
        cnt = {"w": 0, "lb": 0, "h": 0, "rs": 0, "s": 0, "ch": 0}
        rope_loaded = {"t0": None}

        ones = cm[:, 0, :]
        BO = cm[:, 1, :]
        maskI = cm[:, 2, 0:64]
        pswap = cm[0:64, 3, 0:64]
        MODSv = MODS[:, :].rearrange("p (w l j) -> p w l j", w=2, l=4)
        ABv = ABT[:, :].rearrange("p (s k c) -> p s k c", s=3, k=3)

        def V(name):
            o, w = VEC_OFF[name]
            return vecs[:, o:o + w]

        def view(off, shape, p0=0):
            n = 1
            for s_ in shape[1:]:
                n *= s_
            a = AR[p0:p0 + shape[0], off:off + n]
            if len(shape) == 3:
                a = a.rearrange("p (a b) -> p a b", b=shape[2])
            elif len(shape) == 4:
                a = a.rearrange("p (a b c) -> p a b c", b=shape[2], c=shape[3])
            elif len(shape) == 5:
                a = a.rearrange("p (a b c d) -> p a b c d", b=shape[2], c=shape[3], d=shape[4])
            return a

        HV = [view(0, (128, 8, 512)), view(4096, (128, 8, 512))]
        PT = [view(22528, (128, 512)), view(23040, (128, 512))]
        RP = view(23552, (64, 2, 512))
        HB = [Buf(), Buf()]

        def xbufs(t0, n):
            return [XB[u] for u in range(t0 // 256, (t0 + n + 255) // 256)]

        K.dma("sp", vecs[:, :], I["vecs"], writes=[VB])
        K.dma("sp", cm[:, :, :], I["cmat"], writes=[CMB])

        def linear(W, KC, KP, c0, ncols, M, rhs_fn, N, epi):
            CT = max(M, min(ncols, (4096 // KC) // M * M))
            Wv = W.rearrange("(kc p) n -> p kc n", p=KP)
            off = 0
            j = 0
            while off < ncols:
                ct = min(CT, ncols - off)
                s = cnt["w"] % 2
                cnt["w"] += 1
                slot = WS[s][0:KP, 0:KC * ct].rearrange("p (kc n) -> p kc n", n=ct)
                K.dma("sp", slot, Wv[:, :, c0 + off:c0 + off + ct], writes=[WB[s]])
                for mc in range(ct // M):
                    b = cnt["lb"] % 2
                    cnt["lb"] += 1
                    pso = ps[b][0:M, 0:N]
                    rh = [rhs_fn(kc) for kc in range(KC)]
                    rbufs = []
                    for x in rh:
                        rbufs.extend(x[1])

                    def mm(e, slot=slot, mc=mc, pso=pso, rh=rh):
                        ins = None
                        for kc in range(KC):
                            ins = e.matmul(pso, slot[:, kc, mc * M:(mc + 1) * M], rh[kc][0],
                                           start=(kc == 0), stop=(kc == KC - 1))
                        return ins
                    OP("pe", mm, reads=[WB[s], CMB] + rbufs, writes=[PB[b]])
                    epi(j, pso, PB[b])
                    j += 1
                off += ct

        def linear_tok(W, c0, ncols, h_ap, hbufs, nsub, epi):
            Wv = W.rearrange("(kc p) n -> p kc n", p=128)
            s = cnt["w"] % 2
            cnt["w"] += 1
            slot = WS[s][:, 0:8 * ncols].rearrange("p (kc n) -> p kc n", n=ncols)
            K.dma("sp", slot, Wv[:, :, c0:c0 + ncols], writes=[WB[s]])
            for sub in range(nsub):
                b = cnt["lb"] % 2
                cnt["lb"] += 1
                pso = ps[b][:, 0:ncols]

                def mm(e, sub=sub, pso=pso):
                    ins = None
                    for kc in range(8):
                        ins = e.matmul(pso, h_ap[:, kc, sub * 128:(sub + 1) * 128], slot[:, kc, :],
                                       start=(kc == 0), stop=(kc == 7))
                    return ins
                OP("pe", mm, reads=[WB[s]] + hbufs, writes=[PB[b]])
                epi(sub, pso, PB[b])

        def rstd_from(pin, pb, nparts, n, scale, eps):
            r = cnt["rs"] % 2
            cnt["rs"] += 1
            t = RS[r][0:nparts, 0:n]
            OP("act", lambda e: e.activation(out=t, in_=pin, func=AF.Sqrt, bias=float(eps), scale=float(scale)),
               reads=[pb], writes=[RSB[r]])
            OP("dve", lambda e: e.reciprocal(out=t, in_=t), reads=[RSB[r]], writes=[RSB[r]])
            return t, RSB[r]

        def make_h(t0, n, A, Bv, off=0, slot=None, out16=None, out16b=None):
            if slot is None:
                s = cnt["h"] % 2
                cnt["h"] += 1
            else:
                s = slot
            hv, hb = HV[s], HB[s]
            xb = xbufs(t0, n)
            xin = xT[:, :, t0:t0 + n]
            hsl = hv[:, :, off:off + n]
            OP("act", lambda e: e.activation(out=hsl, in_=xin, func=AF.Square), reads=xb, writes=[hb])

            def mm(e):
                ins = None
                for c in range(8):
                    ins = e.matmul(ps[2][:, 0:n], ones, hv[:, c, off:off + n], start=(c == 0), stop=(c == 7))
                return ins
            OP("pe", mm, reads=[hb, CMB], writes=[PB[2]])
            rs, rsb = rstd_from(ps[2][:, 0:n], PB[2], 128, n, 1.0 / 1024, NORM_EPS)
            OP("dve", lambda e: e.tensor_tensor(out=hsl, in0=xin, in1=rs.unsqueeze(1).to_broadcast([128, 8, n]),
                                                op=ALU.mult), reads=xb + [rsb], writes=[hb])
            tgt = hv if out16 is None else out16
            tgtb = hb if out16 is None else out16b

            def mod_act(e):
                ins = None
                for c in range(0, 4):
                    ins = e.activation(out=tgt[:, c, off:off + n], in_=hv[:, c, off:off + n], func=AF.Identity,
                                       scale=A[:, c:c + 1], bias=Bv[:, c:c + 1])
                return ins

            def mod_dve(e):
                ins = None
                for c in range(4, 8):
                    ins = e.tensor_scalar(out=tgt[:, c, off:off + n], in0=hv[:, c, off:off + n], scalar1=A[:, c:c + 1],
                                          scalar2=Bv[:, c:c + 1], op0=ALU.mult, op1=ALU.add)
                return ins
            side = Buf()
            side.w, side.r, side.w2 = tgtb.w, dict(tgtb.r), tgtb.w2
            OP("dve", mod_dve, reads=[side if out16 is None else hb, ABB, VB, SMB], writes=[side])
            OP("act", mod_act, reads=[hb, ABB, VB, SMB], writes=[tgtb])
            tgtb.w2 = side.w
            return hv, hb

        def prep_mods(ph, l):
            for sub in range(3):
                sh = MODSv[:, ph, l, (3 * sub) * 8:(3 * sub) * 8 + 8]
                sc = MODSv[:, ph, l, (3 * sub + 1) * 8:(3 * sub + 1) * 8 + 8]
                gt = MODSv[:, ph, l, (3 * sub + 2) * 8:(3 * sub + 2) * 8 + 8]
                ng = V("ng%d_%d" % (l, sub))
                OP("dve", lambda e, sub=sub, sc=sc, ng=ng: e.scalar_tensor_tensor(
                    out=ABv[:, sub, 0, :], in0=sc, scalar=1.0, in1=ng, op0=ALU.add, op1=ALU.mult),
                   reads=[MB, VB], writes=[ABB])
                OP("dve", lambda e, sub=sub, sh=sh: e.tensor_copy(out=ABv[:, sub, 1, :], in_=sh), reads=[MB], writes=[ABB])
                OP("dve", lambda e, sub=sub, gt=gt: e.tensor_scalar(
                    out=ABv[:, sub, 2, :], in0=gt, scalar1=(1.0 if sub == 1 else 0.5), scalar2=None, op0=ALU.mult),
                   reads=[MB], writes=[ABB])

        def ab(sub):
            return ABv[:, sub, 0, :], ABv[:, sub, 1, :], ABv[:, sub, 2, :]

        def resid_epi(G, t0, n, extra=None):
            def epi(m, pso, pb):
                xs = xT[:, m, t0:t0 + n]
                xb = xbufs(t0, n)
                OP("dve", lambda e: e.scalar_tensor_tensor(out=xs, in0=pso, scalar=G[:, m:m + 1], in1=xs,
                                                           op0=ALU.mult, op1=ALU.add),
                   reads=[pb, ABB] + xb, writes=xb)
                if extra is not None:
                    OP("dve", lambda e: e.tensor_scalar(out=xs, in0=xs, scalar1=extra[:, m:m + 1], scalar2=None,
                                                        op0=ALU.add), reads=[SMB] + xb, writes=xb)
            return epi

        def compute_mods():
            SC = SM[:, 0:16].rearrange("p (c w) -> p c w", w=2)
            cond = V("cond").rearrange("p (c w) -> p c w", w=2)
            OP("act", lambda e: e.activation(out=SC, in_=cond, func=AF.Silu), reads=[VB], writes=[SMB])
            for l in range(4):
                mb = V("mb%d" % l)

                def epi(j, pso, pb, l=l, mb=mb):
                    OP("dve", lambda e: e.tensor_scalar(out=MODSv[:, :, l, j], in0=pso, scalar1=mb[:, j:j + 1],
                                                        scalar2=None, op0=ALU.add), reads=[pb, VB], writes=[MB])
                linear(I["mod_w"][l], 8, 128, 0, 9216, 128, lambda kc: (SC[:, kc, :], [SMB]), 2, epi)

        def ffn(ph, l, i, T):
            A, Bv, G = ab(0 if i == 0 else 2)
            act = view(8192, (128, 22, 512))
            ACTB = [Buf() for _ in range(22)]
            for blk in range(T // 512):
                t0 = blk * 512
                hv, hb = make_h(t0, 512, A, Bv)

                def epi1(j, pso, pb):
                    if j < 22:
                        OP("act", lambda e: e.activation(out=act[:, j, :], in_=pso, func=AF.Silu),
                           reads=[pb], writes=[ACTB[j]])
                    else:
                        jj = j - 22
                        OP("dve", lambda e: e.tensor_tensor(out=act[:, jj, :], in0=act[:, jj, :], in1=pso, op=ALU.mult),
                           reads=[pb, ACTB[jj]], writes=[ACTB[jj]])
                linear(I["ffn_w_in"][l, i], 8, 128, 0, 2 * D_FF, 128, lambda kc: (hv[:, kc, :], [hb]), 512, epi1)
                linear(I["ffn_w_down"][l, i], 22, 128, 0, 1024, 128, lambda kc: (act[:, kc, :], [ACTB[kc]]), 512,
                       resid_epi(G, t0, 512))

        def view16(off, shape):
            n = 1
            for s_ in shape[1:]:
                n *= s_
            a = AR[0:shape[0], off:off + n // 2].bitcast(BF16)
            if len(shape) == 3:
                a = a.rearrange("p (a b) -> p a b", b=shape[2])
            return a

        def ffn16(ph, l, i, T):
            A, Bv, G = ab(0 if i == 0 else 2)
            W16 = [view16(4096, (128, 8192)), view16(8192, (128, 8192))]
            W16B = [Buf(), Buf()]
            H16 = [view16(12288, (128, 8, 512)), view16(14336, (128, 8, 512))]
            H16B = [Buf(), Buf()]
            act16 = view16(16384, (128, 22, 512))
            ACTB = [Buf() for _ in range(22)]
            SGt = [view(22016, (128, 512)), view(22528, (128, 512))]
            SGB = [Buf(), Buf()]
            W32B = [[Buf(), Buf()], [Buf(), Buf()]]
            Win = I["ffn_w_in"][l, i].rearrange("(kc p) n -> p kc n", p=128)
            Wdn = I["ffn_w_down"][l, i].rearrange("(kc p) n -> p kc n", p=128)
            banks = [(0, 1), (4, 5), (6, 7)]
            st_ = {"t": 0, "b": 0, "g": 0}

            def cast_split(w16, s32, kcn, srcbufs, dstb):
                h_ = kcn // 2
                side = Buf()
                side.w, side.r, side.w2 = dstb.w, dict(dstb.r), dstb.w2
                OP("act", lambda e: e.activation(out=w16[:, h_:kcn, :], in_=s32[:, h_:kcn, :], func=AF.Copy),
                   reads=srcbufs, writes=[side])
                OP("pool", lambda e: e.tensor_copy(out=w16[:, 0:h_, :], in_=s32[:, 0:h_, :]), reads=srcbufs, writes=[dstb])
                dstb.w2 = side.w
            for blk in range(T // 512):
                t0 = blk * 512
                h16v, h16b = H16[blk % 2], H16B[blk % 2]
                make_h(t0, 512, A, Bv, slot=0, out16=h16v, out16b=h16b)
                for j0 in range(0, 22, 2):
                    s = st_["t"] % 2
                    st_["t"] += 1
                    s32 = WS[s][:, 0:4096].rearrange("p (kc n) -> p kc n", n=512)
                    if os.environ.get("K_NODMA") != "1":
                        K.dma("sp", s32[:, :, 0:256], Win[:, :, j0 * 128:j0 * 128 + 256], writes=[W32B[s][0]])
                        K.dma("sp", s32[:, :, 256:512], Win[:, :, D_FF + j0 * 128:D_FF + j0 * 128 + 256], writes=[W32B[s][1]])
                    w16 = W16[s][:, 0:4096].rearrange("p (kc n) -> p kc n", n=512)
                    cast_split(w16, s32, 8, W32B[s], W16B[s])
                    for jj in range(2):
                        j = j0 + jj
                        ba, bb = banks[st_["b"] % 3]
                        st_["b"] += 1

                        def mm(e, w16=w16, jj=jj, ba=ba, bb=bb, h16v=h16v):
                            for kc in range(8):
                                e.matmul(ps[ba][:, :], w16[:, kc, jj * 128:(jj + 1) * 128], h16v[:, kc, :], start=(kc == 0), stop=(kc == 7))
                            ins = None
                            for kc in range(8):
                                ins = e.matmul(ps[bb][:, :], w16[:, kc, 256 + jj * 128:256 + (jj + 1) * 128], h16v[:, kc, :],
                                               start=(kc == 0), stop=(kc == 7))
                            return ins
                        OP("pe", mm, reads=[W16B[s], h16b], writes=[PB[ba], PB[bb]])
                        g = st_["g"] % 2
                        st_["g"] += 1
                        OP("act", lambda e, g=g, ba=ba: e.activation(out=SGt[g], in_=ps[ba][:, :], func=AF.Silu),
                           reads=[PB[ba]], writes=[SGB[g]])
                        OP("dve", lambda e, g=g, bb=bb, j=j: e.tensor_tensor(out=act16[:, j, :], in0=SGt[g], in1=ps[bb][:, :], op=ALU.mult),
                           reads=[SGB[g], PB[bb]], writes=[ACTB[j]])
                for m in range(8):
                    s = st_["t"] % 2
                    st_["t"] += 1
                    s32 = WS[s][:, 0:2816].rearrange("p (kc n) -> p kc n", n=128)
                    if os.environ.get("K_NODMA") != "1":
                        K.dma("sp", s32, Wdn[:, :, m * 128:(m + 1) * 128], writes=[W32B[s][0]])
                    w16 = W16[s][:, 0:2816].rearrange("p (kc n) -> p kc n", n=128)
                    cast_split(w16, s32, 22, W32B[s], W16B[s])
                    ba = banks[st_["b"] % 3][0]
                    st_["b"] += 1

                    def mm2(e, w16=w16, ba=ba):
                        ins = None
                        for kc in range(22):
                            ins = e.matmul(ps[ba][:, :], w16[:, kc, :], act16[:, kc, :], start=(kc == 0), stop=(kc == 21))
                        return ins
                    OP("pe", mm2, reads=[W16B[s]] + ACTB, writes=[PB[ba]])
                    resid_epi(G, t0, 512)(m, ps[ba][:, :], PB[ba])

        def proj_pass(W, KC, KP, src, T, G, extra=None):
            K.barrier()
            tl = view(8192, (KP, KC, 512))
            tb = Buf()
            for blk in range(T // 512):
                t0 = blk * 512
                K.dma("sp", tl, src[:, :, t0:t0 + 512], writes=[tb])
                linear(W, KC, KP, 0, 1024, 128, lambda kc: (tl[:, kc, :], [tb]), 512, resid_epi(G, t0, 512, extra))

        def load_rope(t0, n):
            if rope_loaded["t0"] != (t0, n):
                K.dma("sp", RP[:, :, 0:n], I["rope"][:, :, t0:t0 + n], writes=[RPB])
                rope_loaded["t0"] = (t0, n)

        def rope(src, srcb, dest, destb, t0, n, T2, T2B, T3, T3B):
            load_rope(t0, n)
            OP("pe", lambda e: e.matmul(ps[3][0:64, 0:n], pswap, src, start=True, stop=True),
               reads=[srcb, CMB], writes=[PB[3]])
            OP("dve", lambda e: e.tensor_tensor(out=T2[:, 0:n], in0=ps[3][0:64, 0:n], in1=RP[:, 1, 0:n], op=ALU.mult),
               reads=[PB[3], RPB], writes=[T2B])
            OP("pool", lambda e: e.tensor_tensor(out=T3[:, 0:n], in0=src, in1=RP[:, 0, 0:n], op=ALU.mult),
               reads=[srcb, RPB], writes=[T3B])
            OP("dve", lambda e: e.tensor_tensor(out=dest, in0=T2[:, 0:n], in1=T3[:, 0:n], op=ALU.add),
               reads=[T2B, T3B], writes=destb)

        PT16 = [view16(22528, (128, 512)), view16(22784, (128, 512))]
        ones16 = view16(23040, (128, 128))
        O16B = Buf()

        def init_ones16():
            OP("dve", lambda e: e.tensor_copy(out=ones16, in_=ones), reads=[CMB], writes=[O16B])

        def attn_core(q_ap, qbufs, chunks, dv, nq, scale):
            Oa = ps[6][0:dv, 0:nq]
            Da = ps[7][0:dv, 0:nq]
            n = len(chunks)

            def issue_s(ci):
                k_ap, v_ap, cb = chunks[ci]
                r = cnt["s"] % 2
                cnt["s"] += 1
                sbk = 4 + r
                OP("pe", lambda e, sbk=sbk, k_ap=k_ap: e.matmul(ps[sbk][:, 0:nq], k_ap, q_ap, start=True, stop=True),
                   reads=cb + qbufs, writes=[PB[sbk]])
                return sbk, PT16[r], PTB[r]
            cur = issue_s(0)
            for ci in range(n):
                k_ap, v_ap, cb = chunks[ci]
                nxt = issue_s(ci + 1) if ci + 1 < n else None
                sbk, pt, ptb = cur
                OP("act", lambda e, sbk=sbk, pt=pt: e.activation(out=pt[:, 0:nq], in_=ps[sbk][:, 0:nq], func=AF.Exp,
                                                                 scale=float(scale)), reads=[PB[sbk]], writes=[ptb])

                def mm2(e, ci=ci, v_ap=v_ap, pt=pt):
                    e.matmul(Oa, v_ap, pt[:, 0:nq], start=(ci == 0), stop=(ci == n - 1))
                    return e.matmul(Da, ones16[:, 0:dv], pt[:, 0:nq], start=(ci == 0), stop=(ci == n - 1))
                OP("pe", mm2, reads=cb + [ptb, O16B], writes=[PB[6], PB[7]])
                cur = nxt
            return Oa, Da

        def softmax_out(Oa, Da, dv, nq, dest, destb):
            r = cnt["rs"] % 2
            cnt["rs"] += 1
            rd = RS[r][0:dv, 0:nq]
            OP("dve", lambda e: e.reciprocal(out=rd, in_=Da), reads=[PB[7]], writes=[RSB[r]])
            OP("dve", lambda e: e.tensor_tensor(out=dest, in0=Oa, in1=rd, op=ALU.mult),
               reads=[PB[6], RSB[r]], writes=destb)

        def gqa(ph, l, NS, L):
            T = NS * L
            rope_loaded["t0"] = None
            A, Bv, G = ab(1)
            S = L + (512 if ph else 0)
            nkc = S // 128
            o = 8192
            KT = view(o, (64, NS * S)); o += NS * S
            Vt = view(o, (128, NS * nkc, 64)); o += NS * nkc * 64
            Q16 = view16(o, (64, 4, 512)); o += 1024
            K16 = view16(o, (64, NS * S)); o += NS * S // 2
            V16 = view16(o, (128, NS * nkc, 64)); o += NS * nkc * 32
            K16B, V16B = Buf(), Buf()
            init_ones16()
            KR = view(o, (64, 512)); o += 512
            SQ = view(o, (64, 512)); o += 512
            T1 = view(o, (64, 512)); o += 512
            T2 = view(o, (64, 512)); o += 512
            T3 = view(o, (64, 512)); o += 512
            OTg = view(o, (64, 4, 512)); o += 2048
            assert o <= 22528
            QTB4 = [Buf() for _ in range(4)]
            KTB, VtB, QTB, KRB, SQB, T1B, T2B, T3B, OTB = [Buf() for _ in range(9)]
            coff = 512 if ph else 0

            def qknorm(pso, pb, gname, dest, destb, t0, n):
                OP("act", lambda e: e.activation(out=KR[:, 0:n], in_=pso, func=AF.Copy), reads=[pb], writes=[KRB])
                OP("act", lambda e: e.activation(out=SQ[:, 0:n], in_=pso, func=AF.Square), reads=[pb], writes=[SQB])
                OP("pe", lambda e: e.matmul(ps[3][0:64, 0:n], ones[0:64, 0:64], SQ[:, 0:n], start=True, stop=True),
                   reads=[SQB, CMB], writes=[PB[3]])
                rs, rsb = rstd_from(ps[3][0:64, 0:n], PB[3], 64, n, 1.0 / 64, NORM_EPS)
                tgt, tgtb = (T1[:, 0:n], [T1B]) if ph else (dest, destb)
                OP("dve", lambda e: e.scalar_tensor_tensor(out=tgt, in0=KR[:, 0:n], scalar=V(gname)[0:64, 0:1], in1=rs,
                                                           op0=ALU.mult, op1=ALU.mult), reads=[KRB, rsb, VB], writes=tgtb)
                if ph:
                    rope(T1[:, 0:n], T1B, dest, destb, t0, n, T2, T2B, T3, T3B)

            for g in range(4):
                if ph:
                    K.dma("sp", KT[:, 0:512], I["k0c"][:, g, :], writes=[KTB])
                    K.dma("sp", Vt[:, 0:4, :], I["v0c"][:, :, g * 64:(g + 1) * 64], writes=[VtB])
                for blk in range(T // 512):
                    t0 = blk * 512
                    hv, hb = make_h(t0, 512, A, Bv)
                    linear(I["gq_w_qkv"], 8, 128, 1024 + g * 64, 64, 64, lambda kc: (hv[:, kc, :], [hb]), 512,
                           lambda j, pso, pb: qknorm(pso, pb, "gkn", KT[:, coff + t0:coff + t0 + 512], [KTB], t0, 512))

                    def epi_v(sub, pso, pb):
                        ch = (coff + t0) // 128 + sub
                        OP("act", lambda e: e.activation(out=Vt[:, ch, :], in_=pso, func=AF.Copy), reads=[pb], writes=[VtB])
                    linear_tok(I["gq_w_qkv"], 1280 + g * 64, 64, hv, [hb], 4, epi_v)
                OP("pool", lambda e: e.tensor_copy(out=K16, in_=KT), reads=[KTB], writes=[K16B])
                OP("pool", lambda e: e.tensor_copy(out=V16, in_=Vt), reads=[VtB], writes=[V16B])
                if ph == 0:
                    K.dma("sp", O["nk0"][:, g, :], KT[:, 0:1024], reads=[KTB])
                    K.dma("sp", O["nv0"].rearrange("(c p) g d -> p c g d", p=128)[:, :, g, :], Vt[:, 0:8, :], reads=[VtB])
                for blk in range(T // 512):
                    t0 = blk * 512
                    hv, hb = make_h(t0, 512, A, Bv)

                    def epi_q(j, pso, pb):
                        qknorm(pso, pb, "gqn", Q16[:, j, :], [QTB4[j]], t0, 512)
                        if j < 3:
                            return
                        for jq in range(4):
                            if ph == 0:
                                for u in range(2):
                                    s = blk * 2 + u
                                    chunks = [(K16[:, s * 256 + c * 128:s * 256 + (c + 1) * 128], V16[:, s * 2 + c, :], [K16B, V16B])
                                              for c in range(2)]
                                    Oa, Da = attn_core(Q16[:, jq, u * 256:(u + 1) * 256], [QTB4[jq]], chunks, 64, 256, GQ_SCALE)
                                    softmax_out(Oa, Da, 64, 256, OTg[:, jq, u * 256:(u + 1) * 256], [OTB])
                            else:
                                chunks = [(K16[:, c * 128:(c + 1) * 128], V16[:, c, :], [K16B, V16B]) for c in range(nkc)]
                                Oa, Da = attn_core(Q16[:, jq, :], [QTB4[jq]], chunks, 64, 512, GQ_SCALE)
                                softmax_out(Oa, Da, 64, 512, OTg[:, jq, :], [OTB])
                    linear(I["gq_w_qkv"], 8, 128, g * 256, 256, 64, lambda kc: (hv[:, kc, :], [hb]), 512, epi_q)
                    K.dma("sp", DOT[:, 4 * g:4 * g + 4, t0:t0 + 512], OTg[:, :, :], reads=[OTB])
            proj_pass(I["gq_w_o"], 16, 64, DOT, T, G)

        def diffattn(ph, l, NS, L):
            T = NS * L
            rope_loaded["t0"] = None
            A, Bv, G = ab(1)
            S = L + (512 if ph else 0)
            nkc = S // 128
            o = 4096
            KT = view(o, (64, 2, NS * S)); o += 2 * NS * S
            Vt = view(o, (128, NS * nkc, 128)); o += NS * nkc * 128
            QT = view16(o, (64, 2, 512)); o += 512
            K16 = view16(o, (64, 2, NS * S)); o += NS * S
            V16 = view16(o, (128, NS * nkc, 128)); o += NS * nkc * 64
            K16B, V16B = Buf(), Buf()
            init_ones16()
            T1 = view(o, (64, 512)); o += 512
            T2 = view(o, (64, 512)); o += 512
            T3 = view(o, (64, 512)); o += 512
            O1 = view(o, (128, 512)); o += 512
            O2 = view(o, (128, 512)); o += 512
            OTh = view(o, (128, 512)); o += 512
            assert o <= 22528
            KTB, VtB, QTB, T1B, T2B, T3B, O1B, O2B, OTB = [Buf() for _ in range(9)]
            coff = 512 if ph else 0
            lamv = V("lam").rearrange("p (a b) -> p a b", b=64)
            neglam = None

            def lam_full():
                for a in range(2):
                    OP("dve", lambda e, a=a: e.tensor_tensor(out=O1[:, 0:64], in0=lamv[:, 2 * a, :], in1=lamv[:, 2 * a + 1, :],
                                                             op=ALU.mult), reads=[VB], writes=[O1B])
                    OP("dve", lambda e, a=a: e.tensor_reduce(out=SM[:, 44 + a:45 + a], in_=O1[:, 0:64], axis=AX.X, op=ALU.add),
                       reads=[O1B], writes=[SMB])
                OP("act", lambda e: e.activation(out=SM[:, 44:46], in_=SM[:, 44:46], func=AF.Exp), reads=[SMB], writes=[SMB])
                OP("dve", lambda e: e.tensor_tensor(out=SM[:, 46:47], in0=SM[:, 44:45], in1=SM[:, 45:46], op=ALU.subtract),
                   reads=[SMB], writes=[SMB])
                OP("dve", lambda e: e.tensor_scalar(out=SM[:, 47:48], in0=SM[:, 46:47], scalar1=DF_LAMBDA_INIT, scalar2=-1.0,
                                                    op0=ALU.add, op1=ALU.mult), reads=[SMB], writes=[SMB])
                return SM[:, 47:48]
            neglam = lam_full()

            def proj_rope(pso, pb, dest, destb, t0, n):
                if ph:
                    OP("act", lambda e: e.activation(out=T1[:, 0:n], in_=pso, func=AF.Copy), reads=[pb], writes=[T1B])
                    rope(T1[:, 0:n], T1B, dest, destb, t0, n, T2, T2B, T3, T3B)
                else:
                    OP("act", lambda e: e.activation(out=dest, in_=pso, func=AF.Copy), reads=[pb], writes=destb)

            for h in range(8):
                if ph:
                    K.dma("sp", KT[:, :, 0:512], I["k2c"][:, h, :, :], writes=[KTB])
                    K.dma("sp", Vt[:, 0:4, :], I["v2c"][:, :, h, :], writes=[VtB])
                for blk in range(T // 512):
                    t0 = blk * 512
                    hv, hb = make_h(t0, 512, A, Bv, slot=0)
                    linear(I["df_w_qkv"], 8, 128, 1024 + h * 128, 128, 64, lambda kc: (hv[:, kc, :], [hb]), 512,
                           lambda j, pso, pb: proj_rope(pso, pb, KT[:, j, coff + t0:coff + t0 + 512], [KTB], t0, 512))

                    def epi_v(sub, pso, pb):
                        ch = (coff + t0) // 128 + sub
                        OP("act", lambda e: e.activation(out=Vt[:, ch, :], in_=pso, func=AF.Copy), reads=[pb], writes=[VtB])
                    linear_tok(I["df_w_qkv"], 2048 + h * 128, 128, hv, [hb], 4, epi_v)
                OP("pool", lambda e: e.tensor_copy(out=K16, in_=KT), reads=[KTB], writes=[K16B])
                OP("pool", lambda e: e.tensor_copy(out=V16, in_=Vt), reads=[VtB], writes=[V16B])
                if ph == 0:
                    K.dma("sp", O["nk2"][:, h, :, :], KT[:, :, 0:1024], reads=[KTB])
                    K.dma("sp", O["nv2"].rearrange("(c p) h e -> p c h e", p=128)[:, :, h, :], Vt[:, 0:8, :], reads=[VtB])
                for blk in range(T // 512):
                    t0 = blk * 512
                    hv, hb = make_h(t0, 512, A, Bv, slot=0)

                    def finish(cols, nq, chunk_fn):
                        for j, Ox, OxB in ((0, O1, O1B), (1, O2, O2B)):
                            Oa, Da = attn_core(QT[:, j, cols], [QTB], chunk_fn(j), 128, nq, DF_SCALE)
                            softmax_out(Oa, Da, 128, nq, Ox[:, cols], [OxB])
                        OP("dve", lambda e: e.scalar_tensor_tensor(out=O1[:, cols], in0=O2[:, cols], scalar=neglam, in1=O1[:, cols],
                                                                   op0=ALU.mult, op1=ALU.add), reads=[O1B, O2B, SMB], writes=[O1B])
                        OP("act", lambda e: e.activation(out=O2[:, cols], in_=O1[:, cols], func=AF.Square), reads=[O1B], writes=[O2B])
                        OP("pe", lambda e: e.matmul(ps[3][:, 0:nq], ones, O2[:, cols], start=True, stop=True),
                           reads=[O2B, CMB], writes=[PB[3]])
                        rs, rsb = rstd_from(ps[3][:, 0:nq], PB[3], 128, nq, 1.0 / 128, DF_SUBLN_EPS)
                        OP("dve", lambda e: e.scalar_tensor_tensor(out=OTh[:, cols], in0=O1[:, cols], scalar=V("dsg")[:, 0:1], in1=rs,
                                                                   op0=ALU.mult, op1=ALU.mult), reads=[O1B, rsb, VB], writes=[OTB])
                        OP("dve", lambda e: e.tensor_scalar(out=OTh[:, cols], in0=OTh[:, cols], scalar1=1.0 - DF_LAMBDA_INIT,
                                                            scalar2=None, op0=ALU.mult), reads=[OTB], writes=[OTB])

                    def epi_q(j, pso, pb):
                        proj_rope(pso, pb, QT[:, j, :], [QTB], t0, 512)
                        if j == 1:
                            if ph == 0:
                                for u in range(2):
                                    s = blk * 2 + u
                                    finish(slice(u * 256, (u + 1) * 256), 256,
                                           lambda jj, s=s: [(K16[:, jj, s * 256 + c * 128:s * 256 + (c + 1) * 128],
                                                             V16[:, s * 2 + c, :], [K16B, V16B]) for c in range(2)])
                            else:
                                finish(slice(0, 512), 512,
                                       lambda jj: [(K16[:, jj, c * 128:(c + 1) * 128], V16[:, c, :], [K16B, V16B]) for c in range(nkc)])
                    linear(I["df_w_qkv"], 8, 128, h * 128, 128, 64, lambda kc: (hv[:, kc, :], [hb]), 512, epi_q)
                    K.dma("sp", DM[:, h, t0:t0 + 512], OTh[:, :], reads=[OTB])
            proj_pass(I["df_w_o"], 8, 128, DM, T, G)

        def convmod(ph, l, NS, L):
            T = NS * L
            A, Bv, G = ab(1)
            o = 8192
            U = view(o, (128, 8, 286)); o += 8 * 286
            ACC = view(o, (128, 8, 256)); o += 2048
            SQ = view(o, (128, 8, 256)); o += 2048
            MEAN = view(o, (128, 256)); o += 256
            SG = view(o, (128, 512)); o += 512
            UB = [Buf() for _ in range(8)]
            ACB = [Buf() for _ in range(8)]
            SQB, MEB, SGB = Buf(), Buf(), Buf()
            cbi, cwd = V("cbi"), V("cwd").rearrange("p (c j) -> p c j", j=31)
            GBt = SM[:, 48:56]
            OP("dve", lambda e: e.tensor_tensor(out=GBt, in0=G, in1=V("cbo"), op=ALU.mult), reads=[ABB, VB], writes=[SMB])
            upl = L // 256
            def unit(u):
                s, oo = u // upl, (u % upl) * 256
                lo, hi = max(oo - 15, 0), min(oo + 271, L)
                n = hi - lo
                off = lo - (oo - 15)
                hv, hb = make_h(s * L + lo, n, A, Bv)
                if off > 0:
                    OP("pool", lambda e: e.memset(U[:, :, 0:off], 0.0), writes=UB)
                if off + n < 286:
                    OP("pool", lambda e: e.memset(U[:, :, off + n:286], 0.0), writes=UB)

                def epi_in(j, pso, pb):
                    if j < 8:
                        OP("act", lambda e: e.activation(out=U[:, j, off:off + n], in_=pso, func=AF.Identity,
                                                         bias=cbi[:, j:j + 1], scale=1.0), reads=[pb, VB], writes=[UB[j]])
                    else:
                        jj = j - 8
                        OP("act", lambda e: e.activation(out=SG[:, 0:n], in_=pso, func=AF.Sigmoid, bias=cbi[:, j:j + 1], scale=1.0),
                           reads=[pb, VB], writes=[SGB])
                        OP("dve", lambda e: e.tensor_tensor(out=U[:, jj, off:off + n], in0=U[:, jj, off:off + n], in1=SG[:, 0:n],
                                                            op=ALU.mult), reads=[SGB, UB[jj]], writes=[UB[jj]])
                linear(I["cv_w_in"], 8, 128, 0, 2048, 128, lambda kc: (hv[:, kc, 0:n], [hb]), n, epi_in)
                for c in range(8):
                    OP("dve", lambda e, c=c: e.tensor_scalar(out=ACC[:, c, :], in0=U[:, c, 0:256], scalar1=cwd[:, c, 0:1],
                                                             scalar2=V("cbd")[:, c:c + 1], op0=ALU.mult, op1=ALU.add),
                       reads=[UB[c], VB], writes=[ACB[c]])
                for j in range(1, 31):
                    for c in range(8):
                        OP("dve", lambda e, c=c, j=j: e.scalar_tensor_tensor(out=ACC[:, c, :], in0=U[:, c, j:j + 256],
                                                                              scalar=cwd[:, c, j:j + 1], in1=ACC[:, c, :],
                                                                              op0=ALU.mult, op1=ALU.add),
                           reads=[UB[c], VB, ACB[c]], writes=[ACB[c]])
                OP("act", lambda e: e.activation(out=SQ[:, :, :], in_=ACC[:, :, :], func=AF.Square), reads=ACB, writes=[SQB])

                def mm1(e):
                    ins = None
                    for c in range(8):
                        ins = e.matmul(ps[2][:, 0:256], ones, ACC[:, c, :], start=(c == 0), stop=(c == 7))
                    return ins
                OP("pe", mm1, reads=ACB + [CMB], writes=[PB[2]])

                def mm2(e):
                    ins = None
                    for c in range(8):
                        ins = e.matmul(ps[3][:, 0:256], ones, SQ[:, c, :], start=(c == 0), stop=(c == 7))
                    return ins
                OP("pe", mm2, reads=[SQB, CMB], writes=[PB[3]])
                OP("dve", lambda e: e.tensor_scalar(out=MEAN[:, :], in0=ps[2][:, 0:256], scalar1=1.0 / 1024, scalar2=None,
                                                    op0=ALU.mult), reads=[PB[2]], writes=[MEB])
                OP("dve", lambda e: e.tensor_tensor(out=SG[:, 0:256], in0=MEAN[:, :], in1=MEAN[:, :], op=ALU.mult),
                   reads=[MEB], writes=[SGB])
                OP("dve", lambda e: e.scalar_tensor_tensor(out=SG[:, 0:256], in0=ps[3][:, 0:256], scalar=1.0 / 1024,
                                                           in1=SG[:, 0:256], op0=ALU.mult, op1=ALU.subtract),
                   reads=[PB[3], SGB], writes=[SGB])
                rs, rsb = rstd_from(SG[:, 0:256], SGB, 128, 256, 1.0, LN_EPS)
                OP("dve", lambda e: e.tensor_tensor(out=ACC[:, :, :], in0=ACC[:, :, :],
                                                    in1=MEAN[:, :].unsqueeze(1).to_broadcast([128, 8, 256]), op=ALU.subtract),
                   reads=ACB + [MEB], writes=ACB)
                OP("dve", lambda e: e.tensor_tensor(out=ACC[:, :, :], in0=ACC[:, :, :],
                                                    in1=rs.unsqueeze(1).to_broadcast([128, 8, 256]), op=ALU.mult),
                   reads=ACB + [rsb], writes=ACB)
                for c in range(8):
                    OP("act", lambda e, c=c: e.activation(out=SQ[:, c, :], in_=ACC[:, c, :], func=AF.Silu,
                                                          scale=V("clg")[:, c:c + 1], bias=V("clb")[:, c:c + 1]),
                       reads=[ACB[c], VB], writes=[SQB])
                K.dma("sp", DM[:, :, s * L + oo:s * L + oo + 256], SQ[:, :, :], reads=[SQB])
            for u in range(T // 256):
                unit(u)
            proj_pass(I["cv_w_out"], 8, 128, DM, T, G, extra=GBt)

        def rwkv(ph, l, NS, L):
            T = NS * L
            A, Bv, G = ab(1)
            upl = L // 256
            o = 8192
            XX = view(o, (128, 8, 256)); o += 2048
            XI = view(o, (128, 8, 256)); o += 2048
            Kt = view(o, (128, 8, 256)); o += 2048
            Vtt = view(o, (128, 8, 256)); o += 2048
            KK = view(o, (128, 8, 256)); o += 2048
            As = view(o, (128, 8, 256)); o += 2048
            KDS = view(o, (128, 8, 256)); o += 2048
            SG = view(o, (128, 256)); o += 256
            TW = view(o, (64, 256)); o += 256
            CH = []
            for _ in range(4):
                CH.append(view(o, (128, 256))); o += 256
            assert o <= 24576
            XXB, XIB, KtB, VtB, KKB, AsB, KDSB, SGB, TWB = [Buf() for _ in range(9)]
            CHB = [Buf() for _ in range(4)]
            mix = V("mix")
            OMK = SM[:, 56:64]
            OP("dve", lambda e: e.tensor_scalar(out=OMK, in0=V("ka"), scalar1=-1.0, scalar2=1.0, op0=ALU.mult, op1=ALU.add),
               reads=[VB], writes=[SMB])

            def chtile():
                i = cnt["ch"] % 4
                cnt["ch"] += 1
                return CH[i], CHB[i]

            def dsl(name, m, tk0):
                return DS[name][:, m, tk0:tk0 + 256]

            def preunit(u):
                s, oo = u // upl, (u % upl) * 256
                tk0 = s * L + oo
                lo, hi = max(oo - 1, 0), min(oo + 257, L)
                n = hi - lo
                off = lo - (oo - 1)
                hv, hb = make_h(s * L + lo, n, A, Bv, off=off)
                if off > 0:
                    OP("pool", lambda e: e.memset(hv[:, :, 0:1], 0.0), writes=[hb])
                if off + n < 258:
                    OP("pool", lambda e: e.memset(hv[:, :, 257:258], 0.0), writes=[hb])
                hmid = hv[:, :, 1:257]
                OP("dve", lambda e: e.tensor_tensor(out=XX[:, :, :], in0=hv[:, :, 0:256], in1=hv[:, :, 2:258], op=ALU.add),
                   reads=[hb], writes=[XXB])
                OP("dve", lambda e: e.scalar_tensor_tensor(out=XX[:, :, :], in0=XX[:, :, :], scalar=0.5, in1=hmid,
                                                           op0=ALU.mult, op1=ALU.subtract), reads=[XXB, hb], writes=[XXB])

                def mk_xi(i):
                    for c in range(8):
                        OP("dve", lambda e, c=c: e.scalar_tensor_tensor(out=XI[:, c, :], in0=XX[:, c, :],
                                                                        scalar=mix[:, i * 8 + c:i * 8 + c + 1], in1=hmid[:, c, :],
                                                                        op0=ALU.mult, op1=ALU.add),
                           reads=[XXB, hb, VB], writes=[XIB])
                rhsXI = lambda kc: (XI[:, kc, :], [XIB])

                def copy_epi(dst, dstb):
                    def epi(m, pso, pb):
                        OP("act", lambda e: e.activation(out=dst[:, m, :], in_=pso, func=AF.Copy), reads=[pb], writes=[dstb])
                    return epi
                mk_xi(2)
                linear(I["rw_w_k"], 8, 128, 0, 1024, 128, rhsXI, 256, copy_epi(Kt, KtB))
                mk_xi(3)
                linear(I["rw_w_v"], 8, 128, 0, 1024, 128, rhsXI, 256, copy_epi(Vtt, VtB))
                K.dma("sp", DS["V"][:, :, tk0:tk0 + 256], Vtt[:, :, :], reads=[VtB])
                mk_xi(5)
                linear(I["rw_g1"], 8, 128, 0, 128, 128, rhsXI, 256,
                       lambda j, pso, pb: OP("act", lambda e: e.activation(out=SG[:, :], in_=pso, func=AF.Sigmoid),
                                             reads=[pb], writes=[SGB]))

                def epi_g(m, pso, pb):
                    t, tb = chtile()
                    OP("act", lambda e: e.activation(out=t, in_=pso, func=AF.Copy), reads=[pb], writes=[tb])
                    K.dma("sp", dsl("G", m, tk0), t, reads=[tb])
                linear(I["rw_g2"], 1, 128, 0, 1024, 128, lambda kc: (SG[:, :], [SGB]), 256, epi_g)
                for c in range(8):
                    OP("dve", lambda e, c=c: e.tensor_scalar(out=KK[:, c, :], in0=Kt[:, c, :], scalar1=V("kk")[:, c:c + 1],
                                                             scalar2=None, op0=ALU.mult), reads=[KtB, VB], writes=[KKB])
                    t, tb = chtile()
                    OP("act", lambda e, c=c, t=t: e.activation(out=t, in_=KK[:, c, :], func=AF.Square), reads=[KKB], writes=[tb])
                    OP("pe", lambda e, t=t: e.matmul(ps[3][:, 0:256], BO, t, start=True, stop=True), reads=[tb, CMB], writes=[PB[3]])
                    OP("dve", lambda e, t=t: e.tensor_scalar(out=t, in0=ps[3][:, 0:256], scalar1=1e-24, scalar2=None, op0=ALU.max),
                       reads=[PB[3]], writes=[tb])
                    OP("act", lambda e, t=t: e.activation(out=t, in_=t, func=AF.Sqrt), reads=[tb], writes=[tb])
                    OP("dve", lambda e, t=t: e.reciprocal(out=t, in_=t), reads=[tb], writes=[tb])
                    OP("dve", lambda e, c=c, t=t: e.tensor_tensor(out=KK[:, c, :], in0=KK[:, c, :], in1=t, op=ALU.mult),
                       reads=[KKB, tb], writes=[KKB])
                    t2, t2b = chtile()
                    OP("act", lambda e, c=c, t2=t2: e.activation(out=t2, in_=KK[:, c, :], func=AF.Copy, scale=-1.0),
                       reads=[KKB], writes=[t2b])
                    K.dma("sp", dsl("A", c, tk0), t2, reads=[t2b])
                for d in range(2):
                    mk_xi(1)
                    linear(I["rw_w1"][d], 8, 128, 0, 64, 64, rhsXI, 256,
                           lambda j, pso, pb: OP("act", lambda e: e.activation(out=TW[:, :], in_=pso, func=AF.Tanh),
                                                 reads=[pb], writes=[TWB]))

                    def epi_w(m, pso, pb, d=d):
                        t, tb = chtile()
                        OP("act", lambda e: e.activation(out=t, in_=pso, func=AF.Sigmoid, bias=V("w0")[:, d * 8 + m:d * 8 + m + 1],
                                                         scale=1.0), reads=[pb, VB], writes=[tb])
                        OP("act", lambda e: e.activation(out=t, in_=t, func=AF.Exp, scale=-math.exp(-0.5)), reads=[tb], writes=[tb])
                        K.dma("sp", dsl("W%d" % d, m, tk0), t, reads=[tb])
                    linear(I["rw_w2"][d], 1, 64, 0, 1024, 128, lambda kc: (TW[:, :], [TWB]), 256, epi_w)
                    mk_xi(4)
                    linear(I["rw_a1"][d], 8, 128, 0, 64, 64, rhsXI, 256,
                           lambda j, pso, pb: OP("act", lambda e: e.activation(out=TW[:, :], in_=pso, func=AF.Copy),
                                                 reads=[pb], writes=[TWB]))

                    def epi_a(m, pso, pb, d=d):
                        OP("act", lambda e: e.activation(out=As[:, m, :], in_=pso, func=AF.Sigmoid,
                                                         bias=V("a0")[:, d * 8 + m:d * 8 + m + 1], scale=1.0),
                           reads=[pb, VB], writes=[AsB])
                        t, tb = chtile()
                        OP("dve", lambda e: e.tensor_scalar(out=t, in0=As[:, m, :], scalar1=V("ka")[:, m:m + 1], scalar2=OMK[:, m:m + 1],
                                                            op0=ALU.mult, op1=ALU.add), reads=[AsB, VB, SMB], writes=[tb])
                        OP("dve", lambda e: e.tensor_tensor(out=t, in0=t, in1=Kt[:, m, :], op=ALU.mult), reads=[tb, KtB], writes=[tb])
                        K.dma("sp", dsl("KD%d" % d, m, tk0), t, reads=[tb])
                        if d == 0:
                            OP("pool", lambda e: e.tensor_copy(out=KDS[:, m, :], in_=t), reads=[tb], writes=[KDSB])
                        else:
                            OP("pool", lambda e: e.tensor_tensor(out=KDS[:, m, :], in0=KDS[:, m, :], in1=t, op=ALU.add),
                               reads=[tb, KDSB], writes=[KDSB])
                        t2, t2b = chtile()
                        OP("pool", lambda e: e.tensor_tensor(out=t2, in0=KK[:, m, :], in1=As[:, m, :], op=ALU.mult),
                           reads=[KKB, AsB], writes=[t2b])
                        K.dma("sp", dsl("B%d" % d, m, tk0), t2, reads=[t2b])
                    linear(I["rw_a2"][d], 1, 64, 0, 1024, 128, lambda kc: (TW[:, :], [TWB]), 256, epi_a)
                mk_xi(0)

                def epi_r(m, pso, pb):
                    t, tb = chtile()
                    OP("act", lambda e: e.activation(out=t, in_=pso, func=AF.Copy), reads=[pb], writes=[tb])
                    K.dma("sp", dsl("R", m, tk0), t, reads=[tb])
                    t2, t2b = chtile()
                    OP("dve", lambda e: e.scalar_tensor_tensor(out=t2, in0=t, scalar=V("rk")[:, m:m + 1], in1=KDS[:, m, :],
                                                               op0=ALU.mult, op1=ALU.mult), reads=[tb, VB, KDSB], writes=[t2b])
                    OP("pe", lambda e: e.matmul(ps[3][:, 0:256], BO, t2, start=True, stop=True), reads=[t2b, CMB], writes=[PB[3]])
                    OP("dve", lambda e: e.tensor_tensor(out=t2, in0=ps[3][:, 0:256], in1=Vtt[:, m, :], op=ALU.mult),
                       reads=[PB[3], VtB], writes=[t2b])
                    K.dma("sp", dsl("BON", m, tk0), t2, reads=[t2b])
                linear(I["rw_w_r"], 8, 128, 0, 1024, 128, rhsXI, 256, epi_r)
            for u in range(T // 256):
                preunit(u)
            K.barrier()
            if os.environ.get('K_RW') == 'pre':
                return

            def seq_scan_all():
                TC = 64
                nch = L // TC
                Hs = view(0, (128, 2, 8, 64))
                Hs2 = view(0, (128, 1024))
                HsB = [Buf(), Buf()]
                o = 1024
                names = ["a", "w", "b", "kd", "r", "v"]
                dsn = {"a": ("A", "A"), "w": ("W0", "W1"), "b": ("B0", "B1"), "kd": ("KD0", "KD1"), "r": ("R", "R"), "v": ("V", "V")}
                CTt = [[{} for _ in range(2)] for _ in range(2)]
                CTB = [[{} for _ in range(2)] for _ in range(2)]
                for par in range(2):
                    for d in range(2):
                        for nm in names:
                            CTt[par][d][nm] = view(o, (128, 8, TC)); o += 512
                            CTB[par][d][nm] = Buf()
                YC = [[None, None], [None, None]]
                YCB = [[Buf(), Buf()], [Buf(), Buf()]]
                for par in range(2):
                    for d in range(2):
                        YC[par][d] = view(o, (128, 8, TC)); o += 512
                TA, VD, VS, TR = [], [], [], []
                for lst in (TA, VD, VS, TR):
                    for gp in range(2):
                        lst.append((view(o, (128, 8, 64)), view(o, (128, 512)), Buf())); o += 512
                assert o <= 24576
                mask_bc = maskI.unsqueeze(1).to_broadcast([128, 8, 64])

                def scan_seq(s):
                    base = s * L
                    if ph == 0:
                        OP("pool", lambda e: e.memset(Hs2, 0.0), writes=HsB)
                    else:
                        K.dma("sp", Hs, I["h0"], writes=HsB)

                    def load_chunk(ci):
                        par = ci % 2
                        for d in range(2):
                            tok0 = base + (ci * TC if d == 0 else L - (ci + 1) * TC)
                            for nm in names:
                                K.dma("sp", CTt[par][d][nm], DS[dsn[nm][d]][:, :, tok0:tok0 + TC], writes=[CTB[par][d][nm]])
                    load_chunk(0)
                    for ci in range(nch):
                        par = ci % 2
                        if ci + 1 < nch:
                            load_chunk(ci + 1)
                        for j in range(TC):
                            for d in range(2):
                                col = j if d == 0 else TC - 1 - j
                                gp = d
                                Hg = Hs[:, d, :, :]
                                hb_ = HsB[d]
                                ct, cb = CTt[par][d], CTB[par][d]

                                def bc(nm, ct=ct, col=col):
                                    return ct[nm][:, :, col:col + 1].to_broadcast([128, 8, 64])
                                ta3, ta2, tab = TA[gp]
                                vd3, vd2, vdb = VD[gp]
                                vs3, vs2, vsb = VS[gp]
                                tr3, tr2, trb = TR[gp]
                                pu, pv, py = ps[gp], ps[2 + gp], ps[4 + gp]
                                pu3 = pu[:, :].rearrange("p (h v) -> p h v", v=64)
                                py3 = py[:, :].rearrange("p (h v) -> p h v", v=64)
                                OP("dve", lambda e, ta3=ta3, Hg=Hg, a=bc("a"): e.tensor_tensor(out=ta3, in0=Hg, in1=a, op=ALU.mult),
                                   reads=[hb_, cb["a"]], writes=[tab])
                                OP("pe", lambda e, pu=pu, ta2=ta2: e.matmul(pu[:, :], BO, ta2, start=True, stop=True),
                                   reads=[tab, CMB], writes=[PB[gp]])
                                OP("pool", lambda e, Hg=Hg, w=bc("w"): e.tensor_tensor(out=Hg, in0=Hg, in1=w, op=ALU.mult),
                                   reads=[hb_, cb["w"]], writes=[hb_])
                                OP("dve", lambda e, vd3=vd3, v=bc("v"): e.tensor_tensor(out=vd3, in0=mask_bc, in1=v, op=ALU.mult),
                                   reads=[cb["v"], CMB], writes=[vdb])
                                OP("pe", lambda e, pv=pv, vd2=vd2: e.matmul(pv[:, :], BO, vd2, start=True, stop=True),
                                   reads=[vdb, CMB], writes=[PB[2 + gp]])
                                OP("act", lambda e, pv=pv, vs2=vs2: e.activation(out=vs2, in_=pv[:, :], func=AF.Copy),
                                   reads=[PB[2 + gp]], writes=[vsb])
                                OP("pool", lambda e, vd3=vd3, vs3=vs3, kd=bc("kd"): e.tensor_tensor(out=vd3, in0=vs3, in1=kd, op=ALU.mult),
                                   reads=[vsb, cb["kd"]], writes=[vdb])
                                OP("pool", lambda e, Hg=Hg, vd3=vd3: e.tensor_tensor(out=Hg, in0=Hg, in1=vd3, op=ALU.add),
                                   reads=[hb_, vdb], writes=[hb_])
                                OP("dve", lambda e, ta3=ta3, pu3=pu3, b=bc("b"): e.tensor_tensor(out=ta3, in0=pu3, in1=b, op=ALU.mult),
                                   reads=[PB[gp], cb["b"]], writes=[tab])
                                OP("dve", lambda e, Hg=Hg, ta3=ta3: e.tensor_tensor(out=Hg, in0=Hg, in1=ta3, op=ALU.add),
                                   reads=[hb_, tab], writes=[hb_])
                                OP("dve", lambda e, tr3=tr3, Hg=Hg, r=bc("r"): e.tensor_tensor(out=tr3, in0=Hg, in1=r, op=ALU.mult),
                                   reads=[hb_, cb["r"]], writes=[trb])
                                OP("pe", lambda e, py=py, tr2=tr2: e.matmul(py[:, :], BO, tr2, start=True, stop=True),
                                   reads=[trb, CMB], writes=[PB[4 + gp]])
                                OP("dve", lambda e, tr3=tr3, py3=py3: e.tensor_tensor(out=tr3, in0=py3, in1=mask_bc, op=ALU.mult),
                                   reads=[PB[4 + gp], CMB], writes=[trb])
                                yout = YC[par][d][:, :, col]
                                OP("dve", lambda e, yout=yout, tr3=tr3: e.tensor_reduce(out=yout, in_=tr3, axis=AX.X, op=ALU.add),
                                   reads=[trb], writes=[YCB[par][d]])
                        for d in range(2):
                            tok0 = base + (ci * TC if d == 0 else L - (ci + 1) * TC)
                            K.dma("sp", DS["Y%d" % d][:, :, tok0:tok0 + TC], YC[par][d], reads=[YCB[par][d]])
                    if ph == 0:
                        K.dma("sp", O["hst"][:, s, :, :, :], Hs, reads=HsB)
                for s in range(NS):
                    scan_seq(s)
                K.barrier()
                if os.environ.get('K_RW') == 'scan':
                    return


            if os.environ.get("K_SCAN", "chunk") == "chunk":
                CL = 64
                nchk = L // CL
                oo = [0]

                def al(shape):
                    n = 1
                    for x_ in shape[1:]:
                        n *= x_
                    v_ = view(oo[0], shape)
                    oo[0] += n
                    return v_, Buf()
                Hs, _ = al((128, 2, 8, 64))
                HsB = [Buf(), Buf()]
                IN = []
                SETS = []
                for par in range(2):
                    ARt, _ = al((128, 8, 128))
                    IN.append(dict(AR=ARt, ARa=Buf(), ARr=Buf(), B=al((128, 8, 64)), KD=al((128, 8, 64)),
                                   V=al((128, 8, 64)), W=al((128, 8, 64))))
                    st_ = {}
                    for nm in ("C0", "C1", "EX", "WC", "WI", "Nn", "Qt", "Btok", "Ktok", "Vtok", "Zt", "Ut", "Ych"):
                        st_[nm] = al((128, 8, 64))
                    st_["X1s"] = al((128, 8, 128))
                    st_["X2s"] = al((128, 8, 128))
                    SETS.append(st_)
                assert oo[0] <= 24576
                I_bc = maskI.unsqueeze(1).to_broadcast([128, 8, 64])

                def ps3(bank):
                    return ps[bank][:, :].rearrange("p (h v) -> p h v", v=64)

                def mm_hp(e, bank, lhs_fn, rhs_fn, start=True, stop=True, width=64):
                    ins = None
                    for hp in range(8):
                        for eh in range(2):
                            bs = eh * 64
                            if width == 64:
                                out = ps[bank][bs:bs + 64, hp * 64:(hp + 1) * 64]
                            else:
                                out = ps[bank + hp // 4][bs:bs + 64, (hp % 4) * 128:(hp % 4 + 1) * 128]
                            ins = e.matmul(out, lhs_fn(hp, bs), rhs_fn(hp, bs), start=start, stop=stop)
                    return ins

                def chunk(s, d, c, par):
                    tk0 = s * L + (c * CL if d == 0 else L - (c + 1) * CL)
                    inn = IN[par]
                    S_ = SETS[par]
                    C0, C0B = S_["C0"]; C1, C1B = S_["C1"]; EX, EXB = S_["EX"]; WC, WCB = S_["WC"]; WI, WIB = S_["WI"]
                    Nn, NnB = S_["Nn"]; X1s, X1B = S_["X1s"]; X2s, X2B = S_["X2s"]; Qt, QB = S_["Qt"]
                    Btok, BtokB = S_["Btok"]; Ktok, KtokB = S_["Ktok"]; Vtok, VtokB = S_["Vtok"]
                    Zt, ZB = S_["Zt"]; Ut, UB_ = S_["Ut"]; Ych, YchB = S_["Ych"]
                    PP = [S_["C0"], S_["C1"]]
                    NNt = [S_["EX"], S_["WI"]]
                    AR = inn["AR"]
                    Bt, BtB = inn["B"]; KDt, KDB = inn["KD"]; Vt_, VtB_ = inn["V"]; Wt, WtB = inn["W"]
                    ARa, ARr = inn["ARa"], inn["ARr"]
                    K.dma("sp", AR[:, :, 0:64], DS["A"][:, :, tk0:tk0 + CL], writes=[ARa])
                    K.dma("sp", AR[:, :, 64:128], DS["R"][:, :, tk0:tk0 + CL], writes=[ARr])
                    K.dma("sp", Bt, DS["B%d" % d][:, :, tk0:tk0 + CL], writes=[BtB])
                    K.dma("sp", KDt, DS["KD%d" % d][:, :, tk0:tk0 + CL], writes=[KDB])
                    K.dma("sp", Vt_, DS["V"][:, :, tk0:tk0 + CL], writes=[VtB_])
                    K.dma("sp", Wt, DS["W%d" % d][:, :, tk0:tk0 + CL], writes=[WtB])
                    OP("act", lambda e: e.activation(out=Wt, in_=Wt, func=AF.Ln), reads=[WtB], writes=[WtB])
                    seq = [(Wt, WtB), (C0, C0B), (C1, C1B), (C0, C0B), (C1, C1B), (C0, C0B), (C1, C1B)]
                    for i, sh in enumerate((1, 2, 4, 8, 16, 32)):
                        (src, srcB), (dst, dstB) = seq[i], seq[i + 1]
                        if d == 0:
                            OP("dve", lambda e, src=src, dst=dst, sh=sh: e.tensor_tensor(
                                out=dst[:, :, sh:64], in0=src[:, :, sh:64], in1=src[:, :, 0:64 - sh], op=ALU.add),
                               reads=[srcB], writes=[dstB])
                            OP("pool", lambda e, src=src, dst=dst, sh=sh: e.tensor_copy(out=dst[:, :, 0:sh], in_=src[:, :, 0:sh]),
                               reads=[srcB], writes=[dstB])
                        else:
                            OP("dve", lambda e, src=src, dst=dst, sh=sh: e.tensor_tensor(
                                out=dst[:, :, 0:64 - sh], in0=src[:, :, 0:64 - sh], in1=src[:, :, sh:64], op=ALU.add),
                               reads=[srcB], writes=[dstB])
                            OP("pool", lambda e, src=src, dst=dst, sh=sh: e.tensor_copy(out=dst[:, :, 64 - sh:64], in_=src[:, :, 64 - sh:64]),
                               reads=[srcB], writes=[dstB])
                    CU, CUB = C1, C1B
                    OP("dve", lambda e: e.tensor_tensor(out=C0, in0=CU, in1=Wt, op=ALU.subtract), reads=[CUB, WtB], writes=[C0B])
                    OP("act", lambda e: e.activation(out=EX, in_=C0, func=AF.Exp), reads=[C0B], writes=[EXB])
                    OP("act", lambda e: e.activation(out=WC, in_=CU, func=AF.Exp), reads=[CUB], writes=[WCB])
                    OP("act", lambda e: e.activation(out=WI, in_=CU, func=AF.Exp, scale=-1.0), reads=[CUB], writes=[WIB])
                    OP("dve", lambda e: e.tensor_tensor(out=AR[:, :, 0:64], in0=AR[:, :, 0:64], in1=EX, op=ALU.mult),
                       reads=[ARa, EXB], writes=[ARa])
                    OP("pool", lambda e: e.tensor_tensor(out=AR[:, :, 64:128], in0=AR[:, :, 64:128], in1=WC, op=ALU.mult),
                       reads=[ARr, WCB], writes=[ARr])
                    OP("dve", lambda e: e.tensor_tensor(out=Bt, in0=Bt, in1=WI, op=ALU.mult), reads=[BtB, WIB], writes=[BtB])
                    OP("pool", lambda e: e.tensor_tensor(out=KDt, in0=KDt, in1=WI, op=ALU.mult), reads=[KDB, WIB], writes=[KDB])
                    yield
                    NM = cm[:, 4 + d, 0:64].unsqueeze(1).to_broadcast([128, 8, 64])
                    MK = cm[:, 6 + d, :].unsqueeze(1).to_broadcast([128, 4, 128])
                    OP("pe", lambda e: mm_hp(e, 0, lambda hp, bs: AR[bs:bs + 64, hp, 0:64], lambda hp, bs: Bt[bs:bs + 64, hp, :]),
                       reads=[ARa, BtB], writes=[PB[0]])
                    OP("dve", lambda e: e.tensor_tensor(out=Nn, in0=ps3(0), in1=NM, op=ALU.mult), reads=[PB[0], CMB], writes=[NnB])
                    OP("pe", lambda e: mm_hp(e, 1, lambda hp, bs: Bt[bs:bs + 64, hp, :], lambda hp, bs: AR[bs:bs + 64, hp, :], width=128),
                       reads=[ARa, ARr, BtB], writes=[PB[1], PB[2]])
                    OP("pe", lambda e: mm_hp(e, 3, lambda hp, bs: KDt[bs:bs + 64, hp, :], lambda hp, bs: AR[bs:bs + 64, hp, :], width=128),
                       reads=[ARa, ARr, KDB], writes=[PB[3], PB[4]])
                    for half in range(2):
                        OP("dve", lambda e, half=half: e.tensor_tensor(
                            out=X1s[:, half * 4:(half + 1) * 4, :], in0=ps[1 + half][:, :].rearrange("p (h v) -> p h v", v=128),
                            in1=MK, op=ALU.mult), reads=[PB[1 + half], CMB], writes=[X1B])
                        OP("dve", lambda e, half=half: e.tensor_tensor(
                            out=X2s[:, half * 4:(half + 1) * 4, :], in0=ps[3 + half][:, :].rearrange("p (h v) -> p h v", v=128),
                            in1=MK, op=ALU.mult), reads=[PB[3 + half], CMB], writes=[X2B])
                    yield
                    OP("dve", lambda e: e.tensor_tensor(out=Qt, in0=X1s[:, :, 0:64], in1=I_bc, op=ALU.add), reads=[X1B, CMB], writes=[QB])
                    Pp, PpB = X1s[:, :, 0:64], X1B
                    Np, NpB = Nn, NnB
                    for lvl in range(1, 6):
                        Pn, PnB = PP[lvl % 2]
                        Nx, NxB = NNt[lvl % 2]
                        if lvl < 5:
                            OP("pe", lambda e, Np=Np, Pp=Pp: mm_hp(e, 5, lambda hp, bs: Np[bs:bs + 64, hp, :], lambda hp, bs: Pp[bs:bs + 64, hp, :]),
                               reads=[NpB, PpB], writes=[PB[5]])
                            OP("act", lambda e, Pn=Pn: e.activation(out=Pn, in_=ps3(5), func=AF.Copy), reads=[PB[5]], writes=[PnB])
                        OP("pe", lambda e, Np=Np, Pp=Pp: mm_hp(e, 6, lambda hp, bs: Pp[bs:bs + 64, hp, :], lambda hp, bs: Np[bs:bs + 64, hp, :]),
                           reads=[NpB, PpB], writes=[PB[6]])
                        OP("act", lambda e, Nx=Nx: e.activation(out=Nx, in_=ps3(6), func=AF.Copy), reads=[PB[6]], writes=[NxB])
                        OP("pe", lambda e, Nx=Nx: mm_hp(e, 7, lambda hp, bs: Nx[bs:bs + 64, hp, :], lambda hp, bs: Qt[bs:bs + 64, hp, :]),
                           reads=[NxB, QB], writes=[PB[7]])
                        OP("dve", lambda e: e.tensor_tensor(out=Qt, in0=Qt, in1=ps3(7), op=ALU.add), reads=[QB, PB[7]], writes=[QB])
                        Pp, PpB, Np, NpB = Pn, PnB, Nx, NxB
                        yield
                    yield
                    for src, srcB, dst, dstB, bank in ((Bt, BtB, Btok, BtokB, 0), (KDt, KDB, Ktok, KtokB, 1), (Vt_, VtB_, Vtok, VtokB, 2)):
                        OP("pe", lambda e, src=src, bank=bank: mm_hp(e, bank, lambda hp, bs: src[bs:bs + 64, hp, :],
                                                                      lambda hp, bs: maskI[bs:bs + 64, :]),
                           reads=[srcB, CMB], writes=[PB[bank]])
                        OP("act", lambda e, dst=dst, bank=bank: e.activation(out=dst, in_=ps3(bank), func=AF.Copy),
                           reads=[PB[bank]], writes=[dstB])
                    yield
                    Hd = Hs[:, d, :, :]
                    hb_ = HsB[d]

                    def mm_acc(e, bank, terms):
                        ins = None
                        for hp in range(8):
                            for eh in range(2):
                                bs = eh * 64
                                out = ps[bank][bs:bs + 64, hp * 64:(hp + 1) * 64]
                                for i, (lf, rf) in enumerate(terms):
                                    ins = e.matmul(out, lf(hp, bs), rf(hp, bs), start=(i == 0), stop=(i == len(terms) - 1))
                        return ins

                    def mmZ(e):
                        return mm_acc(e, 3, [(lambda hp, bs: AR[bs:bs + 64, hp, 0:64], lambda hp, bs: Hd[bs:bs + 64, hp, :]),
                                             (lambda hp, bs: X2s[bs:bs + 64, hp, 0:64], lambda hp, bs: Vtok[bs:bs + 64, hp, :])])
                    OP("pe", mmZ, reads=[ARa, hb_, X2B, VtokB], writes=[PB[3]])
                    OP("act", lambda e: e.activation(out=Zt, in_=ps3(3), func=AF.Copy), reads=[PB[3]], writes=[ZB])
                    yield
                    OP("pe", lambda e: mm_hp(e, 4, lambda hp, bs: Qt[bs:bs + 64, hp, :], lambda hp, bs: Zt[bs:bs + 64, hp, :]),
                       reads=[QB, ZB], writes=[PB[4]])
                    OP("act", lambda e: e.activation(out=Ut, in_=ps3(4), func=AF.Copy), reads=[PB[4]], writes=[UB_])
                    yield

                    def mmY(e):
                        return mm_acc(e, 5, [(lambda hp, bs: Hd[bs:bs + 64, hp, :], lambda hp, bs: AR[bs:bs + 64, hp, 64:128]),
                                             (lambda hp, bs: Ut[bs:bs + 64, hp, :], lambda hp, bs: X1s[bs:bs + 64, hp, 64:128]),
                                             (lambda hp, bs: Vtok[bs:bs + 64, hp, :], lambda hp, bs: X2s[bs:bs + 64, hp, 64:128])])
                    OP("pe", mmY, reads=[hb_, ARr, UB_, X1B, VtokB, X2B], writes=[PB[5]])
                    OP("act", lambda e: e.activation(out=Ych, in_=ps3(5), func=AF.Copy), reads=[PB[5]], writes=[YchB])
                    K.dma("sp", DS["Y%d" % d][:, :, tk0:tk0 + CL], Ych, reads=[YchB])

                    def mmH(e):
                        return mm_acc(e, 6, [(lambda hp, bs: Btok[bs:bs + 64, hp, :], lambda hp, bs: Ut[bs:bs + 64, hp, :]),
                                             (lambda hp, bs: Ktok[bs:bs + 64, hp, :], lambda hp, bs: Vtok[bs:bs + 64, hp, :])])
                    OP("pe", mmH, reads=[BtokB, UB_, KtokB, VtokB], writes=[PB[6]])
                    last = 63 if d == 0 else 0
                    OP("dve", lambda e: e.tensor_tensor(out=Hd, in0=Hd, in1=ps3(6), op=ALU.add), reads=[hb_, PB[6]], writes=[hb_])
                    OP("dve", lambda e: e.tensor_tensor(out=Hd, in0=Hd, in1=WC[:, :, last:last + 1].to_broadcast([128, 8, 64]), op=ALU.mult),
                       reads=[hb_, WCB], writes=[hb_])

                cnum = 0
                for s in range(NS):
                    if ph == 0:
                        OP("pool", lambda e: e.memset(view(0, (128, 1024)), 0.0), writes=HsB)
                    else:
                        K.dma("sp", Hs, I["h0"], writes=HsB)
                    for c in range(nchk):
                        gens = [chunk(s, 0, c, 0), chunk(s, 1, c, 1)]
                        while gens:
                            for g_ in list(gens):
                                try:
                                    next(g_)
                                except StopIteration:
                                    gens.remove(g_)
                    if ph == 0:
                        K.dma("sp", O["hst"][:, s, :, :, :], Hs, reads=HsB)
                K.barrier()
                if os.environ.get('K_RW') == 'scan':
                    return
            else:
                seq_scan_all()
            o = 8192
            P0 = view(o, (128, 8, 256)); o += 2048
            P1 = view(o, (128, 8, 256)); o += 2048
            P2 = view(o, (128, 8, 256)); o += 2048
            P3 = view(o, (128, 8, 256)); o += 2048
            CA = view(o, (128, 256)); o += 256
            CBt = view(o, (128, 256)); o += 256
            P0B, P1B, P2B, P3B, CAB, CBB = [Buf() for _ in range(6)]
            for u in range(T // 256):
                tk0 = u * 256
                K.dma("sp", P0[:, :, :], DS["Y0"][:, :, tk0:tk0 + 256], writes=[P0B])
                K.dma("sp", P1[:, :, :], DS["Y1"][:, :, tk0:tk0 + 256], writes=[P1B])
                K.dma("sp", P2[:, :, :], DS["BON"][:, :, tk0:tk0 + 256], writes=[P2B])
                K.dma("sp", P3[:, :, :], DS["G"][:, :, tk0:tk0 + 256], writes=[P3B])
                OP("dve", lambda e: e.tensor_tensor(out=P0[:, :, :], in0=P0[:, :, :], in1=P1[:, :, :], op=ALU.add),
                   reads=[P0B, P1B], writes=[P0B])
                for m in range(8):
                    OP("pe", lambda e, m=m: e.matmul(ps[2][:, 0:256], BO, P0[:, m, :], start=True, stop=True),
                       reads=[P0B, CMB], writes=[PB[2]])
                    OP("dve", lambda e, m=m: e.scalar_tensor_tensor(out=CA[:, :], in0=ps[2][:, 0:256], scalar=-1.0 / 64,
                                                                    in1=P0[:, m, :], op0=ALU.mult, op1=ALU.add),
                       reads=[PB[2], P0B], writes=[CAB])
                    OP("act", lambda e: e.activation(out=CBt[:, :], in_=CA[:, :], func=AF.Square), reads=[CAB], writes=[CBB])
                    OP("pe", lambda e: e.matmul(ps[3][:, 0:256], BO, CBt[:, :], start=True, stop=True),
                       reads=[CBB, CMB], writes=[PB[3]])
                    rs, rsb = rstd_from(ps[3][:, 0:256], PB[3], 128, 256, 1.0 / 64, RW_GN_EPS)
                    OP("dve", lambda e, rs=rs: e.tensor_tensor(out=CA[:, :], in0=CA[:, :], in1=rs, op=ALU.mult),
                       reads=[CAB, rsb], writes=[CAB])
                    OP("dve", lambda e, m=m: e.tensor_scalar(out=CA[:, :], in0=CA[:, :], scalar1=V("rlg")[:, m:m + 1],
                                                             scalar2=V("rlb")[:, m:m + 1], op0=ALU.mult, op1=ALU.add),
                       reads=[CAB, VB], writes=[CAB])
                    OP("dve", lambda e, m=m: e.tensor_tensor(out=CA[:, :], in0=CA[:, :], in1=P2[:, m, :], op=ALU.add),
                       reads=[CAB, P2B], writes=[CAB])
                    OP("dve", lambda e, m=m: e.tensor_tensor(out=P1[:, m, :], in0=CA[:, :], in1=P3[:, m, :], op=ALU.mult),
                       reads=[CAB, P3B], writes=[P1B])
                linear(I["rw_w_o"], 8, 128, 0, 1024, 128, lambda kc: (P1[:, kc, :], [P1B]), 256, resid_epi(G, tk0, 256))

        def final(ph, T):
            Z8 = SM[:, 24:32]
            OP("dve", lambda e: e.memset(Z8, 0.0), writes=[SMB])
            dst = O["yp"] if ph == 0 else O["ys"]
            for blk in range(T // 512):
                t0 = blk * 512
                hv, hb = make_h(t0, 512, V("fg"), Z8)
                K.dma("sp", dst[:, :, t0:t0 + 512], hv[:, :, :], reads=[hb])

        compute_mods()
        K.barrier()
        mixers = [gqa, convmod, diffattn, rwkv]
        for ph in K_PHASES:
            NS, L = (4, 256) if ph == 0 else (1, 2048)
            T = NS * L
            K.dma("sp", xT[:, :, 0:T], I["xp"] if ph == 0 else I["xs"], writes=XB)
            for l in range(K_LAYERS):
                prep_mods(ph, l)
                if os.environ.get('K_NOFFN') != '1':
                    (ffn16 if K_FFN16 else ffn)(ph, l, 0, T)
                K.barrier()
                if os.environ.get('K_NOMIX') != '1':
                    mixers[l](ph, l, NS, L)
                K.barrier()
                if os.environ.get('K_NOFFN') != '1':
                    (ffn16 if K_FFN16 else ffn)(ph, l, 1, T)
                K.barrier()
            final(ph, T)
            K.barrier()
        _PROG['counts'] = {n: (e.count, len(e.prog)) for n, e in K.eng.items()}
        _PROG['dmav'] = dict(K.dma_val)
        K.emit(nc)
    return nc


def kernel(**inp):
    inp = {k: np.asarray(v) for k, v in inp.items()}
    if "nc" not in _PROG:
        _PROG["nc"] = build_program()
    nc = _PROG["nc"]
    cmat = const_mats()
    rope = rope_tables()
    shared = {k: np.ascontiguousarray(inp[k], dtype=np.float32) for k in (
        "mod_w", "ffn_w_in", "ffn_w_down", "gq_w_qkv", "gq_w_o", "cv_w_in", "cv_w_out", "df_w_qkv", "df_w_o",
        "rw_w_r", "rw_w_k", "rw_w_v", "rw_w_o", "rw_g1", "rw_g2", "rw_w1", "rw_w2", "rw_a1", "rw_a2")}
    in_maps = []
    for c in range(K_CORES):
        b = c // 4
        m = dict(shared)
        xp = inp["x_prompt"][4 * c:4 * c + 4].reshape(1024, 1024)
        m["xp"] = np.ascontiguousarray(xp.T.reshape(8, 128, 1024).transpose(1, 0, 2))
        xs = inp["x_sample"][b]
        m["xs"] = np.ascontiguousarray(xs.T.reshape(8, 128, 2048).transpose(1, 0, 2))
        m["vecs"] = pack_vecs(inp, inp["c"][b])
        m["cmat"] = cmat
        m["rope"] = rope
        m["k0c"] = np.ascontiguousarray(inp["cache_k0"][b].transpose(2, 1, 0))
        m["v0c"] = np.ascontiguousarray(inp["cache_v0"][b].reshape(4, 128, 256).transpose(1, 0, 2))
        m["k2c"] = np.ascontiguousarray(inp["cache_k2"][b].transpose(3, 1, 2, 0))
        m["v2c"] = np.ascontiguousarray(inp["cache_v2"][b].reshape(4, 128, 8, 128).transpose(1, 0, 2, 3))
        st = inp["state_wkv3"][b].reshape(2, 8, 2, 64, 64)
        m["h0"] = np.ascontiguousarray(st.transpose(2, 4, 0, 1, 3)).reshape(128, 2, 8, 64)
        in_maps.append(m)
    res = run_bass_kernel_spmd(nc, in_maps, core_ids=list(range(K_CORES)))
    R = res.results
    y_prompt = np.zeros((32, 256, 1024), np.float32)
    y_sample = np.zeros((2, 2048, 1024), np.float32)
    nk0 = np.zeros((32, 256, 4, 64), np.float32)
    nv0 = np.zeros((32, 256, 4, 64), np.float32)
    nk2 = np.zeros((32, 256, 8, 2, 64), np.float32)
    nv2 = np.zeros((32, 256, 8, 128), np.float32)
    nwkv = np.zeros((32, 2, 16, 64, 64), np.float32)
    for c in range(K_CORES):
        r = R[c]
        y_prompt[4 * c:4 * c + 4] = np.asarray(r["yp"]).transpose(2, 1, 0).reshape(4, 256, 1024)
        if c % 4 == 0:
            y_sample[c // 4] = np.asarray(r["ys"]).transpose(2, 1, 0).reshape(2048, 1024)
        nk0[4 * c:4 * c + 4] = np.asarray(r["nk0"]).transpose(2, 1, 0).reshape(4, 256, 4, 64)
        nv0[4 * c:4 * c + 4] = np.asarray(r["nv0"]).reshape(4, 256, 4, 64)
        nk2[4 * c:4 * c + 4] = np.asarray(r["nk2"]).transpose(3, 1, 2, 0).reshape(4, 256, 8, 2, 64)
        nv2[4 * c:4 * c + 4] = np.asarray(r["nv2"]).reshape(4, 256, 8, 128)
        hs = np.asarray(r["hst"]).reshape(2, 64, 4, 2, 8, 64)
        nwkv[4 * c:4 * c + 4] = hs.transpose(2, 3, 4, 0, 5, 1).reshape(4, 2, 16, 64, 64)
    return (y_prompt, y_sample, nk0, nv0, nk2, nv2, nwkv)
```
